# Optimizing a Trainium2 kernel written in Bass

```python
import jax, jax.numpy as jnp
from jax import lax
import numpy as np


D_MODEL = 2048
BATCH = 16
SEQ = 2048
DEPTH = 2
DEC_BATCH = 16
DEC_SEQ = 32
PAST_LEN = 4096

CHUNK = 64
N_A_LAYERS = (DEPTH + 1) // 2
N_C_LAYERS = DEPTH // 2
MLSTM_HEADS = 8
MLSTM_DV = D_MODEL // MLSTM_HEADS
MLSTM_DK = MLSTM_DV // 2
MLSTM_WIDTH = MLSTM_HEADS * MLSTM_DV
MLSTM_QK_WIDTH = MLSTM_HEADS * MLSTM_DK
LRU_WIDTH = D_MODEL
LRU_BLOCKS = 16
LRU_BLOCK = LRU_WIDTH // LRU_BLOCKS
CONV_W = 4
LRU_C = 8.0
IN_A_SPLITS = (MLSTM_QK_WIDTH, MLSTM_QK_WIDTH, MLSTM_WIDTH, MLSTM_WIDTH, MLSTM_HEADS, MLSTM_HEADS, LRU_WIDTH, LRU_WIDTH)
IN_A_WIDTH = 2 * MLSTM_QK_WIDTH + 2 * MLSTM_WIDTH + 2 * MLSTM_HEADS + 2 * LRU_WIDTH
OUT_A_WIDTH = MLSTM_WIDTH + LRU_WIDTH
RWKV_HEAD = 64
RWKV_HEADS = D_MODEL // RWKV_HEAD
DECAY_LORA = 96
AAA_LORA = 96
GATE_LORA = 256
RWKV_GN_EPS = 64e-5
D_FF = 4 * D_MODEL
DN_ALPHA = (2 * DEPTH) ** 0.25
DN_BETA = (8 * DEPTH) ** -0.25
LN_EPS = 1e-5

kernel_name = 'hybrid_mlstm_rglru_rwkv7_stream_step'

F32 = jnp.float32


def _split_points(sizes):
    pts, acc = [], 0
    for s in sizes[:-1]:
        acc += s
        pts.append(acc)
    return pts


def layer_norm(x, g, b, eps=LN_EPS):
    xf = x.astype(F32)
    mu = jnp.mean(xf, -1, keepdims=True)
    var = jnp.mean(jnp.square(xf - mu), -1, keepdims=True)
    return (xf - mu) * lax.rsqrt(var + eps) * g + b


def head_norm(y, eps):
    mu = jnp.mean(y, -1, keepdims=True)
    var = jnp.mean(jnp.square(y - mu), -1, keepdims=True)
    return (y - mu) * lax.rsqrt(var + eps)


def squared_relu_mlp(x, w1, w2):
    h = jax.nn.relu(x @ w1)
    return (h * h) @ w2


def mlstm_chunk(C, n, m, q, k, v, ig, lf):
    L = q.shape[2]
    b = jnp.cumsum(lf, axis=-1)
    causal = jnp.tril(jnp.ones((L, L), bool))
    D = jnp.where(causal, b[..., :, None] - b[..., None, :] + ig[..., None, :], -jnp.inf)
    inter = b + m[..., None]
    m_t = jnp.maximum(inter, jnp.max(D, -1))
    w_intra = jnp.exp(D - m_t[..., None])
    w_inter = jnp.exp(inter - m_t)
    qk = jnp.einsum('bhtk,bhsk->bhts', q, k) * w_intra
    num = jnp.einsum('bhts,bhsv->bhtv', qk, v) + w_inter[..., None] * jnp.einsum('bhtk,bhkv->bhtv', q, C)
    den = jnp.sum(qk, -1) + w_inter * jnp.einsum('bhtk,bhk->bht', q, n)
    h = num / jnp.maximum(jnp.abs(den), jnp.exp(-m_t))[..., None]
    b_L = b[..., -1]
    lw = b_L[..., None] - b + ig
    m_new = jnp.maximum(b_L + m, jnp.max(lw, -1))
    ws = jnp.exp(lw - m_new[..., None])
    wc = jnp.exp(b_L + m - m_new)
    C_new = wc[..., None, None] * C + jnp.einsum('bhs,bhsk,bhsv->bhkv', ws, k, v)
    n_new = wc[..., None] * n + jnp.einsum('bhs,bhsk->bhk', ws, k)
    return h, C_new, n_new, m_new


def mlstm_sequence(C, n, m, q, k, v, ig, lf):
    T = q.shape[2]
    if T <= CHUNK:
        return mlstm_chunk(C, n, m, q, k, v, ig, lf)
    nc = T // CHUNK

    def to_chunks(a):
        return jnp.moveaxis(a.reshape(a.shape[:2] + (nc, CHUNK) + a.shape[3:]), 2, 0)

    def step(carry, xs):
        h, C1, n1, m1 = mlstm_chunk(*carry, *xs)
        return (C1, n1, m1), h

    (C, n, m), h = lax.scan(step, (C, n, m), (to_chunks(q), to_chunks(k), to_chunks(v), to_chunks(ig), to_chunks(lf)))
    h = jnp.moveaxis(h, 0, 2).reshape(q.shape[:3] + (v.shape[-1],))
    return h, C, n, m


def causal_conv(x, buf, w, bias):
    T = x.shape[1]
    xp = jnp.concatenate([buf.astype(x.dtype), x], axis=1)
    y = sum(xp[:, j:j + T] * w[j] for j in range(CONV_W)) + bias
    return y, xp[:, -(CONV_W - 1):]


def _lin_combine(c1, c2):
    a1, b1 = c1
    a2, b2 = c2
    return a1 * a2, a2 * b1 + b2


def rglru(x, h0, w_a, b_a, w_x, b_x, lam, reset_first):
    B, T, W = x.shape
    xf = x.astype(F32)
    xb = xf.reshape(B, T, LRU_BLOCKS, LRU_BLOCK)
    gate_r = jax.nn.sigmoid(jnp.einsum('btgi,gij->btgj', xb, w_a).reshape(B, T, W) + b_a)
    gate_i = jax.nn.sigmoid(jnp.einsum('btgi,gij->btgj', xb, w_x).reshape(B, T, W) + b_x)
    log_a = -LRU_C * gate_r * jax.nn.softplus(-lam.astype(F32))
    a = jnp.exp(log_a)
    mult = jnp.sqrt(-jnp.expm1(2.0 * log_a))
    if reset_first:
        mult = jnp.where((jnp.arange(T) == 0)[None, :, None], 1.0, mult)
    b = mult * gate_i * xf
    b = b.at[:, 0].add(a[:, 0] * h0.astype(F32))
    _, h = lax.associative_scan(_lin_combine, (a, b), axis=1)
    return h, h[:, -1]


def mlstm_rglru_mixer(x, st, p, li, reset_first):
    C0, n0, m0, conv0, h0 = st
    B, T, _ = x.shape
    u = x @ p['a_w_in'][li]
    q, k, v, o, ig, fg, xr, yg = jnp.split(u, _split_points(IN_A_SPLITS), axis=-1)

    def to_heads(a):
        return a.reshape(B, T, MLSTM_HEADS, -1).transpose(0, 2, 1, 3).astype(F32)

    q = to_heads(q) * MLSTM_DK ** -0.5
    k = to_heads(k)
    v = to_heads(v)
    ig = (ig + p['a_b_ig'][li]).astype(F32).transpose(0, 2, 1)
    lf = jax.nn.log_sigmoid((fg + p['a_b_fg'][li]).astype(F32)).transpose(0, 2, 1)
    hm, C1, n1, m1 = mlstm_sequence(C0.astype(F32), n0.astype(F32), m0.astype(F32), q, k, v, ig, lf)
    hm = head_norm(hm, LN_EPS).transpose(0, 2, 1, 3).reshape(B, T, MLSTM_WIDTH)
    hm = hm * p['a_mlstm_norm'][li] * jax.nn.sigmoid(o)
    xc, conv1 = causal_conv(xr, conv0, p['a_conv_w'][li], p['a_conv_b'][li])
    hl, h1 = rglru(xc, h0, p['a_lru_wa'][li], p['a_lru_ba'][li], p['a_lru_wx'][li], p['a_lru_bx'][li],
                   p['a_lru_lambda'][li], reset_first)
    yb = hl * jax.nn.gelu(yg)
    out = jnp.concatenate([hm, yb], axis=-1) @ p['a_w_out'][li]
    return out, (C1, n1, m1, conv1, h1)


def rwkv7_mixer(x, shift, S0, p, li):
    B, T, D = x.shape
    H, N = RWKV_HEADS, RWKV_HEAD
    xx = jnp.concatenate([shift[:, None, :].astype(x.dtype), x[:, :-1]], axis=1) - x
    mu = p['c_mu'][li]
    xr, xw, xk, xv, xa, xg = (x + xx * mu[j] for j in range(6))
    r = xr @ p['c_w_r'][li]
    k = xk @ p['c_w_k'][li]
    v = xv @ p['c_w_v'][li]
    w_log = -jax.nn.softplus(-(p['c_w0'][li] + jnp.tanh(xw @ p['c_w1'][li]) @ p['c_w2'][li]).astype(F32)) - 0.5
    decay = jnp.exp(-jnp.exp(w_log))
    a = jax.nn.sigmoid((p['c_a0'][li] + (xa @ p['c_a1'][li]) @ p['c_a2'][li]).astype(F32))
    g = jax.nn.sigmoid(xg @ p['c_g1'][li]) @ p['c_g2'][li]

    def heads(t):
        return t.astype(F32).reshape(B, T, H, N)

    kk = heads(k * p['c_k_k'][li])
    kk = kk * lax.rsqrt(jnp.maximum(jnp.sum(kk * kk, -1, keepdims=True), 1e-24))
    k = k.astype(F32) * (1.0 + (a - 1.0) * p['c_k_a'][li])
    r, k, v, a, decay = heads(r), heads(k), heads(v), heads(a), heads(decay)

    def step(S, inp):
        r_t, k_t, v_t, kk_t, a_t, w_t = inp
        sa = jnp.einsum('bhvk,bhk->bhv', S, -kk_t)
        S = S * w_t[:, :, None, :] + sa[..., None] * (kk_t * a_t)[:, :, None, :] + v_t[..., None] * k_t[:, :, None, :]
        return S, jnp.einsum('bhvk,bhk->bhv', S, r_t)

    def tmaj(t):
        return jnp.moveaxis(t, 1, 0)

    S1, y = lax.scan(step, S0.astype(F32), (tmaj(r), tmaj(k), tmaj(v), tmaj(kk), tmaj(a), tmaj(decay)))
    y = head_norm(jnp.moveaxis(y, 0, 1), RWKV_GN_EPS).reshape(B, T, D) * p['c_gn_g'][li] + p['c_gn_b'][li]
    bonus = (jnp.sum(r * k * p['c_r_k'][li], -1, keepdims=True) * v).reshape(B, T, D)
    out = ((y + bonus) * g) @ p['c_w_o'][li]
    return out, (x[:, -1], S1)


def trunk(x, states, p, reset_first):
    mC, mn, mm, cv, hl, sh, S = states
    new_a, new_c = [], []
    for layer in range(DEPTH):
        li = layer // 2
        if layer % 2 == 0:
            mix, st = mlstm_rglru_mixer(x, (mC[li], mn[li], mm[li], cv[li], hl[li]), p, li, reset_first)
            new_a.append(st)
        else:
            mix, st = rwkv7_mixer(x, sh[li], S[li], p, li)
            new_c.append(st)
        x = layer_norm(DN_ALPHA * x + mix, p['ln1_g'][layer], p['ln1_b'][layer])
        x = layer_norm(DN_ALPHA * x + squared_relu_mlp(x, p['mlp_w1'][layer], p['mlp_w2'][layer]),
                       p['ln2_g'][layer], p['ln2_b'][layer])
    sa = [jnp.stack([s[j] for s in new_a]) for j in range(5)]
    sc = [jnp.stack([s[j] for s in new_c]) for j in range(2)]
    return x, sa + sc


def setup_inputs(seed: int = 0) -> dict:
    key = jax.random.key(seed)
    ks = iter(jax.random.split(key, 64))

    def nrm(shape, scale):
        return jax.random.normal(next(ks), shape, F32) * scale

    def unif(shape, lo, hi):
        return jax.random.uniform(next(ks), shape, F32, lo, hi)

    NA, NC, D = N_A_LAYERS, N_C_LAYERS, D_MODEL
    inp = {}
    inp['x_prompt'] = nrm((BATCH, SEQ, D), 1.0)
    inp['x_sample'] = nrm((DEC_BATCH, DEC_SEQ, D), 1.0)
    inp['state_mlstm_C'] = nrm((NA, DEC_BATCH, MLSTM_HEADS, MLSTM_DK, MLSTM_DV), 0.1)
    inp['state_mlstm_n'] = nrm((NA, DEC_BATCH, MLSTM_HEADS, MLSTM_DK), 0.1)
    inp['state_mlstm_m'] = nrm((NA, DEC_BATCH, MLSTM_HEADS), 0.5)
    inp['state_lru_conv'] = nrm((NA, DEC_BATCH, CONV_W - 1, LRU_WIDTH), 1.0)
    inp['state_lru_h'] = nrm((NA, DEC_BATCH, LRU_WIDTH), 0.5)
    inp['state_rwkv_shift'] = nrm((NC, DEC_BATCH, D), 1.0)
    inp['state_rwkv_S'] = nrm((NC, DEC_BATCH, RWKV_HEADS, RWKV_HEAD, RWKV_HEAD), 0.1)
    inp['a_w_in'] = nrm((NA, D, IN_A_WIDTH), D ** -0.5)
    inp['a_b_ig'] = nrm((NA, MLSTM_HEADS), 0.1)
    inp['a_b_fg'] = jnp.broadcast_to(jnp.linspace(3.0, 6.0, MLSTM_HEADS, dtype=F32), (NA, MLSTM_HEADS)) + nrm((NA, MLSTM_HEADS), 0.1)
    inp['a_mlstm_norm'] = 1.0 + nrm((NA, MLSTM_WIDTH), 0.02)
    inp['a_conv_w'] = nrm((NA, CONV_W, LRU_WIDTH), CONV_W ** -0.5)
    inp['a_conv_b'] = nrm((NA, LRU_WIDTH), 0.02)
    inp['a_lru_wa'] = nrm((NA, LRU_BLOCKS, LRU_BLOCK, LRU_BLOCK), LRU_BLOCK ** -0.5)
    inp['a_lru_ba'] = nrm((NA, LRU_WIDTH), 0.02)
    inp['a_lru_wx'] = nrm((NA, LRU_BLOCKS, LRU_BLOCK, LRU_BLOCK), LRU_BLOCK ** -0.5)
    inp['a_lru_bx'] = nrm((NA, LRU_WIDTH), 0.02)
    s = unif((NA, LRU_WIDTH), 0.9, 0.999) ** (1.0 / LRU_C)
    inp['a_lru_lambda'] = jnp.log(s) - jnp.log1p(-s)
    inp['a_w_out'] = nrm((NA, OUT_A_WIDTH, D), DN_BETA * OUT_A_WIDTH ** -0.5)
    inp['c_mu'] = unif((NC, 6, D), 0.0, 1.0)
    inp['c_w_r'] = nrm((NC, D, D), D ** -0.5)
    inp['c_w_k'] = nrm((NC, D, D), D ** -0.5)
    inp['c_w_v'] = nrm((NC, D, D), DN_BETA * D ** -0.5)
    inp['c_w0'] = unif((NC, D), -6.0, -1.0)
    inp['c_w1'] = nrm((NC, D, DECAY_LORA), D ** -0.5)
    inp['c_w2'] = nrm((NC, DECAY_LORA, D), 0.1 * DECAY_LORA ** -0.5)
    inp['c_a0'] = nrm((NC, D), 0.1)
    inp['c_a1'] = nrm((NC, D, AAA_LORA), D ** -0.5)
    inp['c_a2'] = nrm((NC, AAA_LORA, D), 0.1 * AAA_LORA ** -0.5)
    inp['c_g1'] = nrm((NC, D, GATE_LORA), D ** -0.5)
    inp['c_g2'] = nrm((NC, GATE_LORA, D), GATE_LORA ** -0.5)
    inp['c_k_k'] = 0.85 + nrm((NC, D), 0.02)
    inp['c_k_a'] = 1.0 + nrm((NC, D), 0.02)
    inp['c_r_k'] = nrm((NC, RWKV_HEADS, RWKV_HEAD), 0.1)
    inp['c_gn_g'] = 1.0 + nrm((NC, D), 0.02)
    inp['c_gn_b'] = nrm((NC, D), 0.02)
    inp['c_w_o'] = nrm((NC, D, D), DN_BETA * D ** -0.5)
    inp['ln1_g'] = 1.0 + nrm((DEPTH, D), 0.02)
    inp['ln1_b'] = nrm((DEPTH, D), 0.02)
    inp['ln2_g'] = 1.0 + nrm((DEPTH, D), 0.02)
    inp['ln2_b'] = nrm((DEPTH, D), 0.02)
    inp['mlp_w1'] = nrm((DEPTH, D, D_FF), DN_BETA * D ** -0.5)
    inp['mlp_w2'] = nrm((DEPTH, D_FF, D), DN_BETA * D_FF ** -0.5)
    return inp


def reference(x_prompt, x_sample, state_mlstm_C, state_mlstm_n, state_mlstm_m, state_lru_conv, state_lru_h,
              state_rwkv_shift, state_rwkv_S, a_w_in, a_b_ig, a_b_fg, a_mlstm_norm, a_conv_w, a_conv_b,
              a_lru_wa, a_lru_ba, a_lru_wx, a_lru_bx, a_lru_lambda, a_w_out, c_mu, c_w_r, c_w_k, c_w_v,
              c_w0, c_w1, c_w2, c_a0, c_a1, c_a2, c_g1, c_g2, c_k_k, c_k_a, c_r_k, c_gn_g, c_gn_b, c_w_o,
              ln1_g, ln1_b, ln2_g, ln2_b, mlp_w1, mlp_w2):
    p = dict(a_w_in=a_w_in, a_b_ig=a_b_ig, a_b_fg=a_b_fg, a_mlstm_norm=a_mlstm_norm, a_conv_w=a_conv_w,
             a_conv_b=a_conv_b, a_lru_wa=a_lru_wa, a_lru_ba=a_lru_ba, a_lru_wx=a_lru_wx, a_lru_bx=a_lru_bx,
             a_lru_lambda=a_lru_lambda, a_w_out=a_w_out, c_mu=c_mu, c_w_r=c_w_r, c_w_k=c_w_k, c_w_v=c_w_v,
             c_w0=c_w0, c_w1=c_w1, c_w2=c_w2, c_a0=c_a0, c_a1=c_a1, c_a2=c_a2, c_g1=c_g1, c_g2=c_g2,
             c_k_k=c_k_k, c_k_a=c_k_a, c_r_k=c_r_k, c_gn_g=c_gn_g, c_gn_b=c_gn_b, c_w_o=c_w_o,
             ln1_g=ln1_g, ln1_b=ln1_b, ln2_g=ln2_g, ln2_b=ln2_b, mlp_w1=mlp_w1, mlp_w2=mlp_w2)
    Bp = x_prompt.shape[0]
    init = (jnp.zeros((N_A_LAYERS, Bp, MLSTM_HEADS, MLSTM_DK, MLSTM_DV), F32),
            jnp.zeros((N_A_LAYERS, Bp, MLSTM_HEADS, MLSTM_DK), F32),
            jnp.zeros((N_A_LAYERS, Bp, MLSTM_HEADS), F32),
            jnp.zeros((N_A_LAYERS, Bp, CONV_W - 1, LRU_WIDTH), x_prompt.dtype),
            jnp.zeros((N_A_LAYERS, Bp, LRU_WIDTH), F32),
            jnp.zeros((N_C_LAYERS, Bp, D_MODEL), x_prompt.dtype),
            jnp.zeros((N_C_LAYERS, Bp, RWKV_HEADS, RWKV_HEAD, RWKV_HEAD), F32))
    y_prompt, ps = trunk(x_prompt, init, p, True)
    y_sample, ss = trunk(x_sample, (state_mlstm_C, state_mlstm_n, state_mlstm_m, state_lru_conv, state_lru_h,
                                    state_rwkv_shift, state_rwkv_S), p, False)
    return (y_prompt, y_sample, ps[0], ps[1], ps[2], ps[3], ps[4], ps[5], ps[6],
            ss[0], ss[1], ss[2], ss[3], ss[4], ss[5], ss[6])
```

```python
import numpy as np
import ml_dtypes
from contextlib import ExitStack
import concourse.bass as bass
import concourse.mybir as mybir
from concourse.bass_utils import run_bass_kernel_spmd

F32 = mybir.dt.float32
BF16 = mybir.dt.bfloat16
AF = mybir.ActivationFunctionType
OP = mybir.AluOpType
AX = mybir.AxisListType

D = 2048
NCH = 16
DFF = 8192
HM = 8
DK = 128
DV = 256
INW = 10256
C_Q, C_K, C_V, C_O, C_IG, C_FG, C_XR, C_YG = 0, 1024, 2048, 4096, 6144, 6152, 6160, 8208
ALPHA = 4.0 ** 0.25
LN_EPS = 1e-5
GN_EPS = 64e-5
RH = 32
RN = 64
RB = 32


class Buf:
    __slots__ = ("name", "w", "r", "excl")

    def __init__(self, name="", excl=False):
        self.name = name
        self.excl = excl
        self.w = None
        self.r = {}


class Prog:
    NDMA = {"sp": 12, "pool": 12, "act": 4}

    def __init__(self, nc):
        self.nc = nc
        self.E = {"pe": nc.tensor, "dve": nc.vector, "act": nc.scalar, "pool": nc.gpsimd, "sp": nc.sync}
        self.sems = {}
        self.cnt = {}
        for e in ("pe", "dve", "act", "pool"):
            self.sems[e] = nc.alloc_semaphore("c_" + e)
            self.cnt[e] = 0
        self.dsem = {}
        self.dcnt = {}
        self.dnext = {}
        for q, n in self.NDMA.items():
            self.dnext[q] = 0
            for i in range(n):
                self.dsem[(q, i)] = nc.alloc_semaphore("d_%s%d" % (q, i))
                self.dcnt[(q, i)] = 0
        self.seen = {e: {} for e in self.E}
        self.psb = []
        self.psi = 0
        self.pctr = {}
        self.ninst = 0
        self.trace = {e: [] for e in self.E}

    def _sem(self, key):
        return self.sems[key] if key in self.sems else self.dsem[key]

    def _wait(self, eng, key, val):
        if val <= 0:
            return
        if self.seen[eng].get(key, 0) >= val:
            return
        self.E[eng].wait_ge(self._sem(key), val)
        self.trace[eng].append(("w", key, val))
        self.seen[eng][key] = val
        self.ninst += 1

    def _deps(self, eng, reads, writes):
        deps = []
        for b in reads:
            if b.w is not None:
                deps.append(b.w)
        for b in writes:
            if b.w is not None:
                deps.append(b.w)
            for k, (v, e) in b.r.items():
                deps.append((k, v, e))
        for k, v, e in deps:
            if e == "pe" and eng == "pe":
                continue
            self._wait(eng, k, v)

    def _commit(self, ev, reads, writes):
        for b in writes:
            b.w = ev
            b.r = {}
        for b in reads:
            b.r[ev[0]] = (ev[1], ev[2])

    def op(self, eng, fn, reads=(), writes=()):
        ex = [b for b in reads if b.excl]
        if ex:
            self._deps(eng, reads, list(writes) + ex)
            writes = list(writes) + ex
            reads = [b for b in reads if not b.excl]
        else:
            self._deps(eng, reads, writes)
        ins = fn(self.E[eng])
        self.cnt[eng] += 1
        ins.then_inc(self.sems[eng], 1)
        self.trace[eng].append(("i", eng, 1))
        self.ninst += 1
        self._commit((eng, self.cnt[eng], eng), reads, writes)

    def dma(self, q, out, in_, reads=(), writes=(), slow=False):
        self._deps(q, reads, writes)
        n = self.NDMA[q]
        s = self.dnext[q]
        self.dnext[q] = (s + 1) % n
        key = (q, s)
        self._wait(q, key, self.dcnt[key])
        if slow:
            ins = self.E[q].dma_start(out=out, in_=in_, allow_slow_non_contiguous=True)
        else:
            ins = self.E[q].dma_start(out=out, in_=in_)
        self.dcnt[key] += 16
        ins.then_inc(self.dsem[key], 16)
        self.trace[q].append(("i", key, 16))
        self.ninst += 1
        self._commit((key, self.dcnt[key], "dma_" + q), reads, writes)

    def barrier(self):
        for e in self.E:
            for k in self.sems:
                if k != e:
                    self._wait(e, k, self.cnt[k])
            for k in self.dsem:
                self._wait(e, k, self.dcnt[k])

    def finish(self):
        for k in self.dsem:
            self._wait("sp", k, self.dcnt[k])
        for k in self.sems:
            self._wait("sp", k, self.cnt[k])

    def simulate(self):
        pc = {e: 0 for e in self.E}
        val = {}
        prog = True
        while prog:
            prog = False
            for e in self.E:
                tr = self.trace[e]
                while pc[e] < len(tr):
                    k, key, v = tr[pc[e]]
                    if k == "w":
                        if val.get(key, 0) >= v:
                            pc[e] += 1
                            prog = True
                        else:
                            break
                    else:
                        val[key] = val.get(key, 0) + v
                        pc[e] += 1
                        prog = True
        stuck = {e: (pc[e], len(self.trace[e]), self.trace[e][pc[e]] if pc[e] < len(self.trace[e]) else None) for e in self.E}
        return all(pc[e] == len(self.trace[e]) for e in self.E), stuck

    def _psinit(self):
        if not self.psb:
            self.pst = [self.nc.alloc_psum_tensor("psp%d" % i, [128, 1024], F32) for i in range(4)]
            for i in range(8):
                self.psb.append((self.pst[i // 2][:, (i % 2) * 512:(i % 2 + 1) * 512], Buf("ps%d" % i, excl=True)))

    def psum(self):
        self._psinit()
        r = self.psb[self.psi]
        self.psi = (self.psi + 1) % 8
        return r

    def psum2(self, pairs=None, key=None):
        self._psinit()
        if pairs is not None:
            c = self.pctr.get(key, 0)
            self.pctr[key] = c + 1
            i = 2 * pairs[c % len(pairs)]
            return self.pst[i // 2][:, :], [self.psb[i][1], self.psb[i + 1][1]]
        if self.psi % 2:
            self.psi = (self.psi + 1) % 8
        i = self.psi
        self.psi = (self.psi + 2) % 8
        return self.pst[i // 2][:, :], [self.psb[i][1], self.psb[i + 1][1]]


_UID = [0]


def sbt(nc, name, shape, dt):
    _UID[0] += 1
    return nc.sbuf_tensor("%s_u%d" % (name, _UID[0]), shape, dt)


class Ring:
    def __init__(self, nc, stack, name, n, shape, dt):
        self.t = [stack.enter_context(sbt(nc, "%s%d" % (name, i), shape, dt)) for i in range(n)]
        self.b = [Buf("%s%d" % (name, i)) for i in range(n)]
        self.i = 0

    def get(self):
        r = (self.t[self.i], self.b[self.i])
        self.i = (self.i + 1) % len(self.t)
        return r


def tiles_of(n, step):
    return [(s, min(step, n - s)) for s in range(0, n, step)]


class Ctx:
    pass


def build(cfg):
    Tp, Ts = cfg["Tp"], cfg["Ts"]
    dbg = cfg.get("debug", ())
    stop_after = cfg.get("stop_after", None)
    NT = 2 * Tp + 2 * Ts
    seqs = [(0, Tp, True), (Tp, Tp, True), (2 * Tp, Ts, False), (2 * Tp + Ts, Ts, False)]
    nc = bass.Bass("TRN2", target_bir_lowering=False)
    g = Ctx()
    g.nc, g.cfg, g.NT, g.seqs, g.Tp, g.Ts = nc, cfg, NT, seqs, Tp, Ts
    p = Prog(nc)
    g.p = p

    def din(name, shape, dt=F32):
        return nc.dram_tensor(name, list(shape), dt, kind="ExternalInput").ap()

    def dout(name, shape, dt=F32):
        return nc.dram_tensor(name, list(shape), dt, kind="ExternalOutput").ap()

    def dscr(name, shape, dt):
        kind = "ExternalOutput" if name in dbg else "Internal"
        return nc.dram_tensor(name, list(shape), dt, kind=kind).ap()

    I = Ctx()
    g.I = I
    I.x = din("x_all", [NT, D])
    I.mC = din("st_mC", [2, HM, DK, DV])
    I.mn = din("st_mn", [2, HM, DK])
    I.mm = din("st_mm", [2, HM])
    I.conv = din("st_conv", [2, 3, D])
    I.lruh = din("st_lruh", [2, D])
    I.shift = din("st_shift", [2, D])
    I.S = din("st_S", [2, RH, RN, RN])
    for nm, shp in [("a_w_in", [D, INW]), ("a_b_ig", [1, HM]), ("a_b_fg", [1, HM]), ("a_mlstm_norm", [1, D]),
                    ("a_conv_w", [4, D]), ("a_conv_b", [1, D]), ("a_lru_wa", [16, 128, 128]), ("a_lru_ba", [1, D]),
                    ("a_lru_wx", [16, 128, 128]), ("a_lru_bx", [1, D]), ("a_lru_lambda", [1, D]),
                    ("a_w_out", [2 * D, D]), ("c_mu", [6, D]), ("c_w_r", [D, D]), ("c_w_k", [D, D]),
                    ("c_w_v", [D, D]), ("c_w0", [1, D]), ("c_w1", [D, 96]), ("c_w2", [96, D]), ("c_a0", [1, D]),
                    ("c_a1", [D, 96]), ("c_a2", [96, D]), ("c_g1", [D, 256]), ("c_g2", [256, D]),
                    ("c_k_k", [1, D]), ("c_k_a", [1, D]), ("c_r_k", [1, D]), ("c_gn_g", [1, D]),
                    ("c_gn_b", [1, D]), ("c_w_o", [D, D]), ("ln1_g", [2, D]), ("ln1_b", [2, D]),
                    ("ln2_g", [2, D]), ("ln2_b", [2, D]), ("mlp_w1", [2, D, DFF]), ("mlp_w2", [2, DFF, D])]:
        setattr(I, nm, din(nm, shp))
    I.ident = din("k_ident", [128, 128])
    I.tri = din("k_tri", [128, 128])
    I.ones = din("k_ones", [128, 128])
    I.blk = din("k_blk", [128, 128])

    O = Ctx()
    g.O = O
    O.y = dout("o_y", [NT, D])
    O.mC = dout("o_mC", [4, HM, DK, DV])
    O.mn = dout("o_mn", [4, HM, DK])
    O.mm = dout("o_mm", [4, HM])
    O.conv = dout("o_conv", [4, 3, D])
    O.lruh = dout("o_lruh", [4, D])
    O.shift = dout("o_shift", [4, D])
    O.S = dout("o_S", [4, RH, RN, RN])

    Sx = Ctx()
    g.Sx = Sx
    Sx.XT32 = dscr("s_xt32", [D, NT], F32)
    Sx.QT = dscr("s_qt", [1024, NT], BF16)
    Sx.KT = dscr("s_kt", [1024, NT], BF16)
    Sx.XR = dscr("s_xr", [D, NT], F32)
    Sx.YG = dscr("s_yg", [D, NT], BF16)
    Sx.VT = dscr("s_vtok", [NT, D], BF16)
    Sx.SO = dscr("s_sotok", [NT, D], BF16)
    Sx.KK = dscr("s_ktok", [NT, 1024], BF16)
    Sx.GT = dscr("s_gtok", [NT, 16], F32)
    Sx.CAT = dscr("s_cat", [2 * D, NT], BF16)
    Sx.X1 = dscr("s_x1", [D, NT], F32)
    for nm in ("RR", "KRAW", "VV", "DEC", "AA", "NKK", "BB", "K2", "BONUS"):
        setattr(Sx, nm, dscr("s_" + nm.lower(), [D, NT], F32))
    Sx.GG = dscr("s_gg", [D, NT], BF16)
    Sx.GAM = dscr("s_gam", [D, NT], F32)
    Sx.RR2 = dscr("s_rr2", [D, NT], F32)
    Sx.YT = dscr("s_yt", [NT, D], F32)
    Sx.YP = dscr("s_yp", [NT, D], F32)
    Sx.VTOK = dscr("s_vtok2", [NT, D], BF16)
    Sx.YG2 = dscr("s_yg2", [D, NT], BF16)

    with ExitStack() as gs:
        K = Ctx()
        g.K = K
        K.ident = gs.enter_context(nc.sbuf_tensor("ident", [128, 128], F32))
        K.tri = gs.enter_context(nc.sbuf_tensor("tri", [128, 128], F32))
        K.ones = gs.enter_context(nc.sbuf_tensor("ones", [128, 128], F32))
        K.blk = gs.enter_context(nc.sbuf_tensor("blk", [128, 128], F32))
        K.buf = Buf("consts")
        K.onesD = gs.enter_context(nc.sbuf_tensor("onesD", [128, 128], F32))
        p.op("dve", lambda e: e.memset(K.onesD[:, :], 1.0 / D), writes=[K.buf])
        K.eps_ln = gs.enter_context(nc.sbuf_tensor("eps_ln", [128, 1], F32))
        K.eps_gn = gs.enter_context(nc.sbuf_tensor("eps_gn", [128, 1], F32))
        p.op("dve", lambda e: e.memset(K.eps_ln[:, :], LN_EPS), writes=[K.buf])
        p.op("dve", lambda e: e.memset(K.eps_gn[:, :], GN_EPS), writes=[K.buf])
        for t, src in ((K.ident, I.ident), (K.tri, I.tri), (K.ones, I.ones), (K.blk, I.blk)):
            p.dma("sp", t[:], src[:, :], writes=[K.buf])
        p.barrier()
        if stop_after == "consts":
            p.finish()
            g_last[0] = p
            return nc

        load_params(g, gs)
        stage_inproj(g)
        p.barrier()
        if stop_after == "inproj":
            p.finish()
            g_last[0] = p
            return nc
        stage_mlstm(g)
        p.barrier()
        if stop_after == "mlstm":
            p.finish()
            g_last[0] = p
            return nc
        stage_lru(g)
        p.barrier()
        if stop_after == "lru":
            p.finish()
            g_last[0] = p
            return nc
        stage_tail(g, 0, Sx.CAT, 32, I.a_w_out, Sx.XT32)
        p.barrier()
        if stop_after == "tail0":
            p.finish()
            g_last[0] = p
            return nc
        stage_rwkv_proj(g)
        p.barrier()
        stage_rwkv_prep(g)
        p.barrier()
        if stop_after == "rprep":
            p.finish()
            g_last[0] = p
            return nc
        stage_rwkv_rec(g)
        p.barrier()
        stage_rwkv_post(g)
        p.barrier()
        if stop_after == "rpost":
            p.finish()
            g_last[0] = p
            return nc
        stage_tail(g, 1, Sx.YG2, 16, I.c_w_o, Sx.X1)
        p.barrier()
    p.finish()
    g_last[0] = p
    return nc


g_last = [None]


def fm(ap, c=128):
    return ap.rearrange("(c p) t -> p c t", p=c)


def stage_inproj(g):
    nc, p, I, Sx, K, NT = g.nc, g.p, g.I, g.Sx, g.K, g.NT
    with ExitStack() as st:
        xin = Ring(nc, st, "xin", 2, [128, D], F32)
        xt32 = Ring(nc, st, "xt32", 2, [128, NCH, 512], F32)
        xtb = Ring(nc, st, "xtb", 2, [128, NCH, 512], BF16)
        wf = Ring(nc, st, "wf", 2, [128, NCH, 512], BF16)
        wt = Ring(nc, st, "wt", 2, [128, NCH, 256], BF16)
        ob = Ring(nc, st, "ob", 4, [128, 512], BF16)
        of = Ring(nc, st, "of", 3, [128, 512], F32)
        og = Ring(nc, st, "og", 2, [128, 16], F32)
        win = I.a_w_in.rearrange("(kc p) n -> p kc n", p=128)
        XT32v = fm(Sx.XT32)
        for (t0, n) in tiles_of(NT, 512):
            x32, x32b = xt32.get()
            xb, xbb = xtb.get()
            for (s0, m) in tiles_of(n, 128):
                xi, xib = xin.get()
                p.dma("sp", xi[:m, :], I.x[t0 + s0:t0 + s0 + m, :], writes=[xib])
                for c4 in range(4):
                    ps, psb = p.psum()
                    for j in range(4):
                        c = c4 * 4 + j
                        p.op("pe", lambda e, c=c, j=j: e.transpose(ps[:, j * 128:j * 128 + m], xi[:m, c * 128:(c + 1) * 128],
                                                                K.ident[:m, :m]),
                             reads=[xib, K.buf], writes=[psb])
                    src = ps[:, :].rearrange("p (j t) -> p j t", j=4)[:, :, :m]
                    if "noact" not in g.cfg.get("flags", ""):
                        p.op("act", lambda e, c4=c4, src=src: e.copy(out=x32[:, c4 * 4:c4 * 4 + 4, s0:s0 + m], in_=src),
                             reads=[psb], writes=[x32b])
                    if "nodve" not in g.cfg.get("flags", ""):
                        p.op("dve", lambda e, c4=c4, src=src: e.tensor_copy(out=xb[:, c4 * 4:c4 * 4 + 4, s0:s0 + m], in_=src),
                             reads=[psb], writes=[xbb])
            if "nostore" in g.cfg.get("flags", ""):
                pass
            elif g.cfg.get("split_store", True):
                for c in range(NCH):
                    p.dma("sp", Sx.XT32[c * 128:(c + 1) * 128, t0:t0 + n], x32[:, c, :n], reads=[x32b])
            else:
                p.dma("sp", XT32v[:, :, t0:t0 + n], x32[:, :, :n], reads=[x32b])
            parts = g.cfg.get("parts", "tr,fm,tm")
            fm_jobs = ([("q", C_Q + 128 * i, i) for i in range(8)] + [("k", C_K + 128 * i, i) for i in range(8)] +
                       [("xr", C_XR + 128 * i, i) for i in range(16)] + [("yg", C_YG + 128 * i, i) for i in range(16)])
            for ji, (kind, col, ci) in enumerate(fm_jobs if "fm" in parts else []):
                if ji % 4 == 0:
                    w, wb = wf.get()
                    p.dma("pool", w[:], win[:, :, col:col + 512], writes=[wb])
                wo_ = (ji % 4) * 128
                ps, psb = p.psum()
                for kc in range(NCH):
                    p.op("pe", lambda e, kc=kc, wo_=wo_: e.matmul(ps[:, :n], lhsT=w[:, kc, wo_:wo_ + 128], rhs=xb[:, kc, :n],
                                                                  start=(kc == 0), stop=(kc == NCH - 1)),
                         reads=[wb, xbb], writes=[psb])
                if kind == "q":
                    o, obb = ob.get()
                    p.op("act", lambda e: e.activation(out=o[:, :n], in_=ps[:, :n], func=AF.Copy, scale=float(DK) ** -0.5),
                         reads=[psb], writes=[obb])
                    p.dma("sp", Sx.QT[ci * 128:(ci + 1) * 128, t0:t0 + n], o[:, :n], reads=[obb])
                elif kind == "k":
                    o, obb = ob.get()
                    p.op("dve", lambda e: e.tensor_copy(out=o[:, :n], in_=ps[:, :n]), reads=[psb], writes=[obb])
                    p.dma("sp", Sx.KT[ci * 128:(ci + 1) * 128, t0:t0 + n], o[:, :n], reads=[obb])
                elif kind == "xr":
                    o, obb = of.get()
                    p.op("dve", lambda e: e.tensor_copy(out=o[:, :n], in_=ps[:, :n]), reads=[psb], writes=[obb])
                    p.dma("sp", Sx.XR[ci * 128:(ci + 1) * 128, t0:t0 + n], o[:, :n], reads=[obb])
                else:
                    o, obb = ob.get()
                    p.op("act", lambda e: e.activation(out=o[:, :n], in_=ps[:, :n], func=AF.Gelu_apprx_tanh),
                         reads=[psb], writes=[obb])
                    p.dma("sp", Sx.YG[ci * 128:(ci + 1) * 128, t0:t0 + n], o[:, :n], reads=[obb])
            tm_jobs = ([("v", C_V + 256 * i, 256 * i, 256) for i in range(8)] +
                       [("o", C_O + 256 * i, 256 * i, 256) for i in range(8)] +
                       [("k", C_K + 256 * i, 256 * i, 256) for i in range(4)] + [("g", C_IG, 0, 16)])
            for kind, col, oc, nw in (tm_jobs if "tm" in parts else []):
                w, wb = wt.get()
                p.dma("pool", w[:, :, :nw], win[:, :, col:col + nw], writes=[wb])
                for (s0, m) in tiles_of(n, 128):
                    ps, psb = p.psum()
                    for kc in range(NCH):
                        p.op("pe", lambda e, kc=kc: e.matmul(ps[:m, :nw], lhsT=xb[:, kc, s0:s0 + m], rhs=w[:, kc, :nw],
                                                             start=(kc == 0), stop=(kc == NCH - 1)),
                             reads=[wb, xbb], writes=[psb])
                    r0 = t0 + s0
                    if kind == "g":
                        o, obb = og.get()
                        p.op("dve", lambda e: e.tensor_copy(out=o[:m, :], in_=ps[:m, :16]), reads=[psb], writes=[obb])
                        p.dma("sp", Sx.GT[r0:r0 + m, :], o[:m, :], reads=[obb])
                    elif kind == "o":
                        o, obb = ob.get()
                        p.op("act", lambda e: e.activation(out=o[:m, :nw], in_=ps[:m, :nw], func=AF.Sigmoid),
                             reads=[psb], writes=[obb])
                        p.dma("sp", Sx.SO[r0:r0 + m, oc:oc + nw], o[:m, :nw], reads=[obb])
                    else:
                        o, obb = ob.get()
                        p.op("dve", lambda e: e.tensor_copy(out=o[:m, :nw], in_=ps[:m, :nw]), reads=[psb], writes=[obb])
                        dst = Sx.VT if kind == "v" else Sx.KK
                        p.dma("sp", dst[r0:r0 + m, oc:oc + nw], o[:m, :nw], reads=[obb])


def stage_mlstm(g):
    nc, p, I, O, Sx, K = g.nc, g.p, g.I, g.O, g.Sx, g.K
    with ExitStack() as st:
        sb = lambda name, shape, dt=F32: st.enter_context(sbt(nc, name, shape, dt))
        Cn = sb("Cn", [128, HM, 257])
        Cnb = sb("Cnb", [128, HM, 257], BF16)
        CnB = [Buf("Cn%d" % h) for h in range(HM)]
        CnbB = [Buf("Cnb%d" % h) for h in range(HM)]
        mrun = sb("mrun", [8, 1])
        mrunB = Buf("mrun")
        mnbc = sb("mnbc", [128, D])
        bigbc = sb("bigbc", [128, 16])
        cB = Buf("mconst")
        p.dma("sp", mnbc[:], I.a_mlstm_norm[0:1, :].partition_broadcast(128), writes=[cB])
        p.dma("sp", bigbc[:, 0:8], I.a_b_ig[0:1, :].partition_broadcast(128), writes=[cB])
        p.dma("sp", bigbc[:, 8:16], I.a_b_fg[0:1, :].partition_broadcast(128), writes=[cB])
        em0 = sb("em0", [128, 8])
        em0B = Buf("em0")
        gtr = Ring(nc, st, "gt", 2, [128, 16], F32)
        vxr = Ring(nc, st, "vx", 2, [128, HM, 257], BF16)
        sor = Ring(nc, st, "so", 2, [128, D], BF16)
        ktr = Ring(nc, st, "ktk", 2, [128, 1024], BF16)
        qTr = Ring(nc, st, "qT", 2, [128, HM, 128], BF16)
        kTr = Ring(nc, st, "kT", 2, [128, HM, 128], BF16)
        catr = Ring(nc, st, "catT", 2, [128, 16, 128], BF16)
        for t_, b_ in zip(vxr.t, vxr.b):
            p.op("dve", lambda e, t_=t_: e.memset(t_[:, :, 256:257], 1.0), writes=[b_])
        gw = Ring(nc, st, "gw", 2, [128, 64], F32)
        g8r = Ring(nc, st, "g8", 2, [8, 8], F32)
        PTr = Ring(nc, st, "PT", 3, [128, 128], BF16)
        ksr = Ring(nc, st, "ks", 3, [128, 128], BF16)
        smr = Ring(nc, st, "sm", 4, [128, 16], F32)
        hhr = Ring(nc, st, "hh", 3, [128, 256], F32)
        hmr = Ring(nc, st, "hm", 3, [128, 256], F32)
        QTv = Sx.QT.rearrange("(h p) t -> p h t", p=128)
        KTv = Sx.KT.rearrange("(h p) t -> p h t", p=128)
        for si, (tok0, T, isp) in enumerate(g.seqs):
            L = 128 if T % 128 == 0 else T
            assert T % L == 0 and L <= 128
            if isp:
                for h in range(HM):
                    p.op("dve", lambda e, h=h: e.memset(Cn[:, h, :], 0.0), writes=[CnB[h]])
                    p.op("act", lambda e, h=h: e.copy(out=Cnb[:, h, :], in_=Cn[:, h, :]), reads=[CnB[h]], writes=[CnbB[h]])
                p.op("dve", lambda e: e.memset(mrun[:, :], 0.0), writes=[mrunB])
            else:
                b = si - 2
                p.dma("sp", em0[:, :], I.mm[b:b + 1, :].partition_broadcast(128), writes=[em0B])
                p.op("act", lambda e: e.activation(out=em0[:, :], in_=em0[:, :], func=AF.Exp), reads=[em0B], writes=[em0B])
                p.dma("sp", mrun[:, :], I.mm[b:b + 1, :].rearrange("o h -> h o"), writes=[mrunB])
                for h in range(HM):
                    p.dma("sp", Cn[:, h, 0:256], I.mC[b, h, :, :], writes=[CnB[h]])
                    p.dma("sp", Cn[:, h, 256:257], I.mn[b, h:h + 1, :].rearrange("o k -> k o"), writes=[CnB[h]])
                    p.op("dve", lambda e, h=h: e.tensor_scalar(out=Cn[:, h, :], in0=Cn[:, h, :], scalar1=em0[:, h:h + 1],
                                                                scalar2=None, op0=OP.mult), reads=[CnB[h], em0B], writes=[CnB[h]])
                    p.op("act", lambda e, h=h: e.copy(out=Cnb[:, h, :], in_=Cn[:, h, :]), reads=[CnB[h]], writes=[CnbB[h]])
            for c in range(T // L):
                r0 = tok0 + c * L
                gt, gtB = gtr.get()
                vx, vxB = vxr.get()
                so, soB = sor.get()
                kt, ktB = ktr.get()
                qT, qTB = qTr.get()
                kT, kTB = kTr.get()
                cat, catB = catr.get()
                p.dma("sp", gt[:L, :], Sx.GT[r0:r0 + L, :], writes=[gtB])
                p.dma("sp", vx[:L, :, 0:256], Sx.VT[r0:r0 + L, :].rearrange("t (h v) -> t h v", h=HM), writes=[vxB])
                p.dma("sp", so[:L, :], Sx.SO[r0:r0 + L, :], writes=[soB])
                p.dma("sp", kt[:L, :], Sx.KK[r0:r0 + L, :], writes=[ktB])
                p.dma("sp", qT[:, :, :L], QTv[:, :, r0:r0 + L], writes=[qTB])
                p.dma("sp", kT[:, :, :L], KTv[:, :, r0:r0 + L], writes=[kTB])
                w, wB = gw.get()
                p.op("dve", lambda e: e.tensor_tensor(out=w[:L, 0:16], in0=gt[:L, :], in1=bigbc[:L, :], op=OP.add),
                     reads=[gtB, cB], writes=[wB])
                p.op("act", lambda e: e.activation(out=w[:L, 8:16], in_=w[:L, 8:16], func=AF.Exp, scale=-1.0), reads=[wB], writes=[wB])
                p.op("act", lambda e: e.activation(out=w[:L, 8:16], in_=w[:L, 8:16], func=AF.Ln, bias=1.0), reads=[wB], writes=[wB])
                p.op("dve", lambda e: e.tensor_scalar(out=w[:L, 8:16], in0=w[:L, 8:16], scalar1=-1.0, scalar2=None, op0=OP.mult),
                     reads=[wB], writes=[wB])
                ps, psB = p.psum()
                p.op("pe", lambda e: e.matmul(ps[:L, 0:8], lhsT=K.tri[:L, :L], rhs=w[:L, 8:16], start=True, stop=True),
                     reads=[wB, K.buf], writes=[psB])
                p.op("pe", lambda e: e.matmul(ps[:, 8:16], lhsT=K.ones[:L, :], rhs=w[:L, 8:16], start=True, stop=True),
                     reads=[wB, K.buf], writes=[psB])
                p.op("pe", lambda e: e.matmul(ps[:8, 16:17], lhsT=w[:L, 8:16], rhs=K.ones[:L, 0:1], start=True, stop=True),
                     reads=[wB, K.buf], writes=[psB])
                p.op("dve", lambda e: e.tensor_tensor(out=w[:L, 24:32], in0=w[:L, 0:8], in1=ps[:L, 0:8], op=OP.subtract),
                     reads=[wB, psB], writes=[wB])
                p.op("act", lambda e: e.activation(out=w[:L, 32:40], in_=w[:L, 24:32], func=AF.Exp), reads=[wB], writes=[wB])
                p.op("act", lambda e: e.activation(out=w[:L, 40:48], in_=ps[:L, 0:8], func=AF.Exp, scale=-1.0), reads=[psB], writes=[wB])
                p.op("act", lambda e: e.activation(out=w[:, 48:56], in_=ps[:, 8:16], func=AF.Exp), reads=[psB], writes=[wB])
                g8, g8B = g8r.get()
                p.op("dve", lambda e: e.tensor_copy(out=g8[:, 0:1], in_=ps[:8, 16:17]), reads=[psB], writes=[g8B])
                ps2, ps2B = p.psum()
                p.op("pe", lambda e: e.transpose(ps2[:8, :L], w[:L, 24:32], K.ident[:L, :L]), reads=[wB, K.buf], writes=[ps2B])
                p.op("dve", lambda e: e.reduce_max(out=g8[:, 1:2], in_=ps2[:8, :L], axis=AX.X), reads=[ps2B], writes=[g8B])
                p.op("dve", lambda e: e.tensor_tensor(out=g8[:, 2:3], in0=g8[:, 1:2], in1=mrun[:, :], op=OP.max),
                     reads=[g8B, mrunB], writes=[g8B])
                p.op("dve", lambda e: e.tensor_tensor(out=mrun[:, :], in0=g8[:, 2:3], in1=g8[:, 0:1], op=OP.add),
                     reads=[g8B], writes=[mrunB])
                for h in range(HM):
                    pS, pSB = p.psum()
                    p.op("pe", lambda e: e.matmul(pS[:L, :L], lhsT=kT[:, h, :L], rhs=qT[:, h, :L], start=True, stop=True),
                         reads=[kTB, qTB], writes=[pSB])
                    PT, PTB = PTr.get()
                    p.op("dve", lambda e: e.scalar_tensor_tensor(out=PT[:L, :L], in0=pS[:L, :L], scalar=w[:L, 32 + h:33 + h],
                                                                 in1=K.tri[:L, :L], op0=OP.mult, op1=OP.mult),
                         reads=[pSB, wB, K.buf], writes=[PTB])
                    pN, pNB = p.psum()
                    p.op("pe", lambda e: e.matmul(pN[:L, 0:257], lhsT=PT[:L, :L], rhs=vx[:L, h, :], start=True, stop=False),
                         reads=[PTB, vxB], writes=[pNB])
                    p.op("pe", lambda e: e.matmul(pN[:L, 0:257], lhsT=qT[:, h, :L], rhs=Cnb[:, h, :], start=False, stop=True),
                         reads=[qTB, CnbB[h]], writes=[pNB])
                    ks, ksB = ksr.get()
                    p.op("dve", lambda e: e.tensor_scalar(out=ks[:L, :], in0=kt[:L, h * 128:(h + 1) * 128],
                                                          scalar1=w[:L, 32 + h:33 + h], scalar2=None, op0=OP.mult),
                         reads=[ktB, wB], writes=[ksB])
                    pC, pCB = p.psum()
                    p.op("pe", lambda e: e.matmul(pC[:, 0:257], lhsT=ks[:L, :], rhs=vx[:L, h, :], start=True, stop=True),
                         reads=[ksB, vxB], writes=[pCB])
                    p.op("dve", lambda e: e.tensor_scalar(out=Cn[:, h, :], in0=Cn[:, h, :], scalar1=w[:, 48 + h:49 + h],
                                                          scalar2=None, op0=OP.mult), reads=[CnB[h], wB], writes=[CnB[h]])
                    p.op("dve", lambda e: e.scalar_tensor_tensor(out=Cn[:, h, :], in0=pC[:, 0:257], scalar=w[:, 48 + h:49 + h],
                                                                 in1=Cn[:, h, :], op0=OP.mult, op1=OP.add),
                         reads=[pCB, wB, CnB[h]], writes=[CnB[h]])
                    p.op("act", lambda e: e.copy(out=Cnb[:, h, :], in_=Cn[:, h, :]), reads=[CnB[h]], writes=[CnbB[h]])
                    sm, smB = smr.get()
                    p.op("act", lambda e: e.activation(out=sm[:L, 13:14], in_=pN[:L, 256:257], func=AF.Abs),
                         reads=[pNB], writes=[smB])
                    p.op("dve", lambda e: e.tensor_tensor(out=sm[:L, 0:1], in0=sm[:L, 13:14], in1=w[:L, 40 + h:41 + h], op=OP.max),
                         reads=[smB, wB], writes=[smB])
                    p.op("dve", lambda e: e.reciprocal(out=sm[:L, 1:2], in_=sm[:L, 0:1]), reads=[smB], writes=[smB])
                    hh, hhB = hhr.get()
                    p.op("act", lambda e: e.activation(out=hh[:L, :], in_=pN[:L, 0:256], func=AF.Copy, scale=sm[:L, 1:2]),
                         reads=[pNB, smB], writes=[hhB])
                    p.op("dve", lambda e: e.bn_stats(out=sm[:L, 2:8], in_=hh[:L, :]), reads=[hhB], writes=[smB])
                    p.op("dve", lambda e: e.bn_aggr(out=sm[:L, 8:10], in_=sm[:L, 2:8]), reads=[smB], writes=[smB])
                    p.op("act", lambda e: e.activation(out=sm[:L, 10:11], in_=sm[:L, 9:10], func=AF.Sqrt, bias=g.K.eps_ln[:L, :]),
                         reads=[smB, K.buf], writes=[smB])
                    p.op("dve", lambda e: e.reciprocal(out=sm[:L, 11:12], in_=sm[:L, 10:11]), reads=[smB], writes=[smB])
                    p.op("dve", lambda e: e.scalar_tensor_tensor(out=sm[:L, 12:13], in0=sm[:L, 8:9], scalar=-1.0,
                                                                 in1=sm[:L, 11:12], op0=OP.mult, op1=OP.mult),
                         reads=[smB], writes=[smB])
                    hm, hmB = hmr.get()
                    p.op("act", lambda e: e.activation(out=hm[:L, :], in_=hh[:L, :], func=AF.Identity,
                                                       scale=sm[:L, 11:12], bias=sm[:L, 12:13]),
                         reads=[hhB, smB], writes=[hmB])
                    p.op("dve", lambda e: e.tensor_tensor(out=hm[:L, :], in0=hm[:L, :], in1=mnbc[:L, h * 256:(h + 1) * 256], op=OP.mult),
                         reads=[hmB, cB], writes=[hmB])
                    p.op("dve", lambda e: e.tensor_tensor(out=hm[:L, :], in0=hm[:L, :], in1=so[:L, h * 256:(h + 1) * 256], op=OP.mult),
                         reads=[hmB, soB], writes=[hmB])
                    pT, pTB = p.psum()
                    for j in range(2):
                        p.op("pe", lambda e, j=j: e.transpose(pT[:, j * 128:j * 128 + L], hm[:L, j * 128:(j + 1) * 128], K.ident[:L, :L]),
                             reads=[hmB, K.buf], writes=[pTB])
                    p.op("act", lambda e: e.copy(out=cat[:, 2 * h:2 * h + 2, :L],
                                                 in_=pT[:, 0:256].rearrange("p (j t) -> p j t", j=2)[:, :, :L]),
                         reads=[pTB], writes=[catB])
                p.dma("sp", fm(Sx.CAT)[:, 0:16, r0:r0 + L], cat[:, :, :L], reads=[catB])
            oi = si
            g8, g8B = g8r.get()
            p.op("dve", lambda e: e.tensor_scalar(out=g8[:, 0:8], in0=K.ident[:8, :8], scalar1=mrun[:, 0:1], scalar2=None, op0=OP.mult),
                 reads=[mrunB, K.buf], writes=[g8B])
            ps, psB = p.psum()
            p.op("pe", lambda e: e.matmul(ps[:, 0:8], lhsT=K.ones[:8, :], rhs=g8[:, 0:8], start=True, stop=True),
                 reads=[g8B, K.buf], writes=[psB])
            p.op("act", lambda e: e.activation(out=em0[:, :], in_=ps[:, 0:8], func=AF.Exp, scale=-1.0), reads=[psB], writes=[em0B])
            p.dma("sp", O.mm[oi:oi + 1, :].rearrange("o h -> h o"), mrun[:, :], reads=[mrunB])
            for h in range(HM):
                p.op("dve", lambda e, h=h: e.tensor_scalar(out=Cn[:, h, :], in0=Cn[:, h, :], scalar1=em0[:, h:h + 1], scalar2=None,
                                                            op0=OP.mult), reads=[CnB[h], em0B], writes=[CnB[h]])
                p.dma("sp", O.mC[oi, h, :, :], Cn[:, h, 0:256], reads=[CnB[h]])
                p.dma("sp", O.mn[oi, h:h + 1, :].rearrange("o k -> k o"), Cn[:, h, 256:257], reads=[CnB[h]])


PARAMS = ["conv_w0", "conv_w1", "conv_w2", "conv_w3", "conv_b", "lru_ba", "lru_bx", "lru_lam",
          "ln1_g0", "ln1_b0", "ln2_g0", "ln2_b0",
          "mu0", "mu1", "mu2", "mu3", "mu4", "mu5", "w0", "a0", "k_k", "k_a", "r_k", "gn_g", "gn_b",
          "ln1_g1", "ln1_b1", "ln2_g1", "ln2_b1", "shift0", "shift1"]


def load_params(g, stack):
    nc, p, I, K = g.nc, g.p, g.I, g.K
    src = {"conv_w0": I.a_conv_w[0:1, :], "conv_w1": I.a_conv_w[1:2, :], "conv_w2": I.a_conv_w[2:3, :],
           "conv_w3": I.a_conv_w[3:4, :], "conv_b": I.a_conv_b, "lru_ba": I.a_lru_ba, "lru_bx": I.a_lru_bx,
           "lru_lam": I.a_lru_lambda, "ln1_g0": I.ln1_g[0:1, :], "ln1_b0": I.ln1_b[0:1, :], "ln2_g0": I.ln2_g[0:1, :],
           "ln2_b0": I.ln2_b[0:1, :], "w0": I.c_w0, "a0": I.c_a0, "k_k": I.c_k_k, "k_a": I.c_k_a, "r_k": I.c_r_k,
           "gn_g": I.c_gn_g, "gn_b": I.c_gn_b, "ln1_g1": I.ln1_g[1:2, :], "ln1_b1": I.ln1_b[1:2, :],
           "ln2_g1": I.ln2_g[1:2, :], "ln2_b1": I.ln2_b[1:2, :]}
    for j in range(6):
        src["mu%d" % j] = I.c_mu[j:j + 1, :]
    src["shift0"] = I.shift[0:1, :]
    src["shift1"] = I.shift[1:2, :]
    n = len(PARAMS)
    PC = stack.enter_context(nc.sbuf_tensor("PC", [128, n * 16], F32))
    PCB = Buf("PC")
    with ExitStack() as st:
        rows = Ring(nc, st, "prow", 2, [128, 128], F32)
        for g0 in range(0, n, 8):
            names = PARAMS[g0:g0 + 8]
            rt, rb = rows.get()
            for i, nm in enumerate(names):
                p.dma("sp", rt[i * 16:(i + 1) * 16, :], src[nm].rearrange("o (c q) -> (o c) q", q=128), writes=[rb])
            R = len(names) * 16
            ps, psB = p.psum()
            p.op("pe", lambda e: e.transpose(ps[:, :R], rt[:R, :], K.ident[:R, :R]), reads=[rb, K.buf], writes=[psB])
            p.op("dve", lambda e: e.tensor_copy(out=PC[:, g0 * 16:g0 * 16 + R], in_=ps[:, :R]), reads=[psB], writes=[PCB])
        p.barrier()
    g.PC, g.PCB = PC, PCB
    g.pcol = lambda name, c: PC[:, PARAMS.index(name) * 16 + c:PARAMS.index(name) * 16 + c + 1]


def stage_lru(g):
    nc, p, I, O, Sx, K = g.nc, g.p, g.I, g.O, g.Sx, g.K
    pcol, PCB = g.pcol, g.PCB
    Tmax = max(T for _, T, _ in g.seqs)
    with ExitStack() as st:
        sb = lambda name, shape, dt=F32: st.enter_context(sbt(nc, name, shape, dt))
        wa = sb("wa", [128, 16, 128], BF16)
        wx = sb("wx", [128, 16, 128], BF16)
        wB = Buf("lruw")
        p.dma("pool", wa[:], I.a_lru_wa.rearrange("g i j -> i g j"), writes=[wB])
        p.dma("pool", wx[:], I.a_lru_wx.rearrange("g i j -> i g j"), writes=[wB])
        cl = sb("cl", [128, 32])
        clB = Buf("cl")
        lam = g.PC[:, PARAMS.index("lru_lam") * 16:PARAMS.index("lru_lam") * 16 + 16]
        p.op("act", lambda e: e.activation(out=cl[:, 0:16], in_=lam, func=AF.Exp, scale=-1.0), reads=[PCB], writes=[clB])
        p.op("act", lambda e: e.activation(out=cl[:, 0:16], in_=cl[:, 0:16], func=AF.Ln, bias=1.0), reads=[clB], writes=[clB])
        p.op("dve", lambda e: e.tensor_scalar(out=cl[:, 16:32], in0=cl[:, 0:16], scalar1=-16.0, scalar2=None, op0=OP.mult),
             reads=[clB], writes=[clB])
        p.op("dve", lambda e: e.tensor_scalar(out=cl[:, 0:16], in0=cl[:, 0:16], scalar1=-8.0, scalar2=None, op0=OP.mult),
             reads=[clB], writes=[clB])
        xpr = Ring(nc, st, "xp", 2, [128, Tmax + 3], F32)
        ygr = Ring(nc, st, "ygl", 2, [128, Tmax], BF16)
        xcr = Ring(nc, st, "xc", 2, [128, Tmax], F32)
        xcbr = Ring(nc, st, "xcb", 2, [128, Tmax], BF16)
        rr = Ring(nc, st, "rr", 2, [128, Tmax], F32)
        gir = Ring(nc, st, "gi", 2, [128, Tmax], F32)
        mr = Ring(nc, st, "ml", 2, [128, Tmax], F32)
        hsr = Ring(nc, st, "hs", 2, [128, Tmax], F32)
        ybr = Ring(nc, st, "yb", 2, [128, Tmax], BF16)
        h0r = Ring(nc, st, "h0", 2, [128, 1], F32)
        for si, (tok0, T, isp) in enumerate(g.seqs):
            for cc in range(NCH):
                rows = slice(cc * 128, (cc + 1) * 128)
                xp, xpB = xpr.get()
                yg, ygB = ygr.get()
                if isp:
                    p.op("dve", lambda e: e.memset(xp[:, 0:3], 0.0), writes=[xpB])
                else:
                    p.dma("sp", xp[:, 0:3], I.conv[si - 2, :, rows].rearrange("j c -> c j"), writes=[xpB], slow=True)
                p.dma("sp", xp[:, 3:3 + T], Sx.XR[rows, tok0:tok0 + T], writes=[xpB])
                p.dma("sp", yg[:, :T], Sx.YG[rows, tok0:tok0 + T], writes=[ygB])
                xc, xcB = xcr.get()
                p.op("dve", lambda e: e.tensor_scalar(out=xc[:, :T], in0=xp[:, 3:3 + T], scalar1=pcol("conv_w3", cc),
                                                      scalar2=pcol("conv_b", cc), op0=OP.mult, op1=OP.add),
                     reads=[xpB, PCB], writes=[xcB])
                for j in range(3):
                    p.op("dve", lambda e, j=j: e.scalar_tensor_tensor(out=xc[:, :T], in0=xp[:, j:j + T], scalar=pcol("conv_w%d" % j, cc),
                                                                      in1=xc[:, :T], op0=OP.mult, op1=OP.add),
                         reads=[xpB, PCB, xcB], writes=[xcB])
                xcb, xcbB = xcbr.get()
                p.op("act", lambda e: e.copy(out=xcb[:, :T], in_=xc[:, :T]), reads=[xcB], writes=[xcbB])
                r, rB = rr.get()
                gi, giB = gir.get()
                for (c0, n) in tiles_of(T, 512):
                    ps, psB = p.psum()
                    p.op("pe", lambda e: e.matmul(ps[:, :n], lhsT=wa[:, cc, :], rhs=xcb[:, c0:c0 + n], start=True, stop=True),
                         reads=[wB, xcbB], writes=[psB])
                    p.op("act", lambda e: e.activation(out=r[:, c0:c0 + n], in_=ps[:, :n], func=AF.Sigmoid, bias=pcol("lru_ba", cc)),
                         reads=[psB, PCB], writes=[rB])
                    ps2, ps2B = p.psum()
                    p.op("pe", lambda e: e.matmul(ps2[:, :n], lhsT=wx[:, cc, :], rhs=xcb[:, c0:c0 + n], start=True, stop=True),
                         reads=[wB, xcbB], writes=[ps2B])
                    p.op("act", lambda e: e.activation(out=gi[:, c0:c0 + n], in_=ps2[:, :n], func=AF.Sigmoid, bias=pcol("lru_bx", cc)),
                         reads=[ps2B, PCB], writes=[giB])
                ml, mlB = mr.get()
                p.op("act", lambda e: e.activation(out=ml[:, :T], in_=r[:, :T], func=AF.Exp, scale=cl[:, 16 + cc:17 + cc]),
                     reads=[rB, clB], writes=[mlB])
                p.op("act", lambda e: e.activation(out=r[:, :T], in_=r[:, :T], func=AF.Exp, scale=cl[:, cc:cc + 1]),
                     reads=[rB, clB], writes=[rB])
                p.op("dve", lambda e: e.tensor_scalar(out=ml[:, :T], in0=ml[:, :T], scalar1=-1.0, scalar2=1.0, op0=OP.mult, op1=OP.add),
                     reads=[mlB], writes=[mlB])
                p.op("act", lambda e: e.activation(out=ml[:, :T], in_=ml[:, :T], func=AF.Sqrt), reads=[mlB], writes=[mlB])
                if isp:
                    p.op("dve", lambda e: e.memset(ml[:, 0:1], 1.0), reads=[mlB], writes=[mlB])
                p.op("dve", lambda e: e.tensor_tensor(out=gi[:, :T], in0=gi[:, :T], in1=xc[:, :T], op=OP.mult),
                     reads=[giB, xcB], writes=[giB])
                p.op("dve", lambda e: e.tensor_tensor(out=gi[:, :T], in0=gi[:, :T], in1=ml[:, :T], op=OP.mult),
                     reads=[giB, mlB], writes=[giB])
                hs, hsB = hsr.get()
                if isp:
                    p.op("dve", lambda e: e.tensor_tensor_scan(out=hs[:, :T], data0=r[:, :T], data1=gi[:, :T], initial=0.0,
                                                               op0=OP.mult, op1=OP.add), reads=[rB, giB], writes=[hsB])
                else:
                    h0, h0B = h0r.get()
                    p.dma("sp", h0[:, :], I.lruh[si - 2:si - 1, rows].rearrange("o c -> c o"), writes=[h0B], slow=True)
                    p.op("dve", lambda e: e.tensor_tensor_scan(out=hs[:, :T], data0=r[:, :T], data1=gi[:, :T], initial=h0[:, 0:1],
                                                               op0=OP.mult, op1=OP.add), reads=[rB, giB, h0B], writes=[hsB])
                yb, ybB = ybr.get()
                p.op("dve", lambda e: e.tensor_tensor(out=yb[:, :T], in0=hs[:, :T], in1=yg[:, :T], op=OP.mult),
                     reads=[hsB, ygB], writes=[ybB])
                p.dma("sp", Sx.CAT[D + cc * 128:D + (cc + 1) * 128, tok0:tok0 + T], yb[:, :T], reads=[ybB])
                p.dma("sp", O.lruh[si:si + 1, rows].rearrange("o c -> c o"), hs[:, T - 1:T], reads=[hsB], slow=True)
                p.dma("sp", O.conv[si, :, rows].rearrange("j c -> c j"), xp[:, T:T + 3], reads=[xpB], slow=True)


def emit_ln(g, R, z, zB, n, gname, bname, xb=None, xbB=None):
    nc, p, K = g.nc, g.p, g.K
    pcol, PCB = g.pcol, g.PCB
    psM, psMB = p.psum()
    psQ, psQB = p.psum()
    for c in range(NCH):
        sq, sqB = R["sq"].get()
        p.op("act", lambda e, c=c: e.activation(out=sq[:, :n], in_=z[:, c, :n], func=AF.Square), reads=[zB], writes=[sqB])
        p.op("pe", lambda e, c=c: e.matmul(psM[:, :n], lhsT=K.onesD[:, :], rhs=z[:, c, :n], start=(c == 0), stop=(c == NCH - 1)),
             reads=[zB, K.buf], writes=[psMB])
        p.op("pe", lambda e, c=c: e.matmul(psQ[:, :n], lhsT=K.onesD[:, :], rhs=sq[:, :n], start=(c == 0), stop=(c == NCH - 1)),
             reads=[sqB, K.buf], writes=[psQB])
    mean, meanB = R["st"].get()
    rstd, rstdB = R["st"].get()
    p.op("act", lambda e: e.copy(out=mean[:, :n], in_=psM[:, :n]), reads=[psMB], writes=[meanB])
    p.op("dve", lambda e: e.tensor_tensor(out=rstd[:, :n], in0=mean[:, :n], in1=mean[:, :n], op=OP.mult), reads=[meanB], writes=[rstdB])
    p.op("dve", lambda e: e.tensor_tensor(out=rstd[:, :n], in0=psQ[:, :n], in1=rstd[:, :n], op=OP.subtract), reads=[psQB, rstdB], writes=[rstdB])
    p.op("act", lambda e: e.activation(out=rstd[:, :n], in_=rstd[:, :n], func=AF.Sqrt, bias=K.eps_ln[:, :]), reads=[rstdB, K.buf], writes=[rstdB])
    p.op("dve", lambda e: e.reciprocal(out=rstd[:, :n], in_=rstd[:, :n]), reads=[rstdB], writes=[rstdB])
    for c in range(NCH):
        p.op("dve", lambda e, c=c: e.tensor_tensor(out=z[:, c, :n], in0=z[:, c, :n], in1=mean[:, :n], op=OP.subtract),
             reads=[zB, meanB], writes=[zB])
        p.op("dve", lambda e, c=c: e.tensor_tensor(out=z[:, c, :n], in0=z[:, c, :n], in1=rstd[:, :n], op=OP.mult),
             reads=[zB, rstdB], writes=[zB])
        p.op("act", lambda e, c=c: e.activation(out=z[:, c, :n], in_=z[:, c, :n], func=AF.Identity, scale=pcol(gname, c), bias=pcol(bname, c)),
             reads=[zB, PCB], writes=[zB])
        if xb is not None:
            p.op("dve", lambda e, c=c: e.tensor_copy(out=xb[:, c, :n], in_=z[:, c, :n]), reads=[zB], writes=[xbB])


def stage_tail(g, layer, XinT, Kc, W, res_in):
    nc, p, I, O, Sx, K = g.nc, g.p, g.I, g.O, g.Sx, g.K
    NT = g.NT
    with ExitStack() as st:
        sb = lambda name, shape, dt=F32: st.enter_context(sbt(nc, name, shape, dt))
        big = sb("big", [128, 32, 512], BF16)
        bigB = Buf("big")
        z = sb("z", [128, NCH, 512])
        zB = Buf("z")
        xb = sb("x1b", [128, NCH, 512], BF16)
        xbB = Buf("x1b")
        wo = Ring(nc, st, "wo", 2, [128, Kc, 128], BF16)
        w1r = Ring(nc, st, "w1", 2, [128, NCH, 512], BF16)
        w2r = Ring(nc, st, "w2", 2, [128, 32, 256], BF16)
        R = {"sq": Ring(nc, st, "sq", 2, [128, 512], F32), "st": Ring(nc, st, "lnst", 4, [128, 512], F32)}
        rsr = Ring(nc, st, "rs", 2, [128, 512], F32)
        rlr = Ring(nc, st, "rl", 2, [128, 512], F32)
        ytr = Ring(nc, st, "ytk", 1, [128, D], F32)
        Wv = W.rearrange("(kc p) n -> p kc n", p=128)
        W1v = I.mlp_w1[layer].rearrange("(kc p) n -> p kc n", p=128)
        W2v = I.mlp_w2[layer].rearrange("(kc p) n -> p kc n", p=128)
        Xv = fm(XinT)
        sfx = str(layer)
        for (t0, n) in tiles_of(NT, 512):
            p.dma("sp", big[:, :Kc, :n], Xv[:, :, t0:t0 + n], writes=[bigB])
            for c in range(NCH):
                w, wB = wo.get()
                p.dma("pool", w[:], Wv[:, :, c * 128:(c + 1) * 128], writes=[wB])
                rs, rsB = rsr.get()
                p.dma("sp", rs[:, :n], res_in[c * 128:(c + 1) * 128, t0:t0 + n], writes=[rsB])
                ps, psB = p.psum()
                for kc in range(Kc):
                    p.op("pe", lambda e, kc=kc: e.matmul(ps[:, :n], lhsT=w[:, kc, :], rhs=big[:, kc, :n], start=(kc == 0), stop=(kc == Kc - 1)),
                         reads=[wB, bigB], writes=[psB])
                p.op("dve", lambda e, c=c: e.scalar_tensor_tensor(out=z[:, c, :n], in0=rs[:, :n], scalar=ALPHA, in1=ps[:, :n],
                                                                  op0=OP.mult, op1=OP.add), reads=[rsB, psB], writes=[zB])
            emit_ln(g, R, z, zB, n, "ln1_g" + sfx, "ln1_b" + sfx, xb, xbB)
            for half in range(2):
                for hc in range(32):
                    col = (half * 32 + hc) * 128
                    if hc % 4 == 0:
                        w, wB = w1r.get()
                        p.dma("pool", w[:], W1v[:, :, col:col + 512], writes=[wB])
                    wo_ = (hc % 4) * 128
                    ps, psB = p.psum()
                    for kc in range(NCH):
                        p.op("pe", lambda e, kc=kc, wo_=wo_: e.matmul(ps[:, :n], lhsT=w[:, kc, wo_:wo_ + 128], rhs=xb[:, kc, :n], start=(kc == 0), stop=(kc == NCH - 1)),
                             reads=[wB, xbB], writes=[psB])
                    rl, rlB = rlr.get()
                    p.op("act", lambda e: e.activation(out=rl[:, :n], in_=ps[:, :n], func=AF.Relu), reads=[psB], writes=[rlB])
                    p.op("dve", lambda e, hc=hc: e.tensor_tensor(out=big[:, hc, :n], in0=rl[:, :n], in1=rl[:, :n], op=OP.mult),
                         reads=[rlB], writes=[bigB])
                for blk in range(8):
                    w, wB = w2r.get()
                    p.dma("pool", w[:], W2v[:, half * 32:(half + 1) * 32, blk * 256:(blk + 1) * 256], writes=[wB])
                    for j in range(2):
                        c = blk * 2 + j
                        ps, psB = p.psum()
                        for kc in range(32):
                            p.op("pe", lambda e, kc=kc, j=j: e.matmul(ps[:, :n], lhsT=w[:, kc, j * 128:(j + 1) * 128], rhs=big[:, kc, :n],
                                                                      start=(kc == 0), stop=(kc == 31)), reads=[wB, bigB], writes=[psB])
                        if half == 0:
                            p.op("dve", lambda e, c=c: e.scalar_tensor_tensor(out=z[:, c, :n], in0=z[:, c, :n], scalar=ALPHA, in1=ps[:, :n],
                                                                              op0=OP.mult, op1=OP.add), reads=[psB, zB], writes=[zB])
                        else:
                            p.op("dve", lambda e, c=c: e.tensor_tensor(out=z[:, c, :n], in0=z[:, c, :n], in1=ps[:, :n], op=OP.add),
                                 reads=[psB, zB], writes=[zB])
            emit_ln(g, R, z, zB, n, "ln2_g" + sfx, "ln2_b" + sfx)
            if layer == 0:
                for c in range(NCH):
                    p.dma("sp", Sx.X1[c * 128:(c + 1) * 128, t0:t0 + n], z[:, c, :n], reads=[zB])
            else:
                for (s0, m) in tiles_of(n, 128):
                    yt, ytB = ytr.get()
                    for c4 in range(4):
                        ps, psB = p.psum()
                        for j in range(4):
                            c = c4 * 4 + j
                            p.op("pe", lambda e, c=c, j=j: e.transpose(ps[:m, j * 128:(j + 1) * 128], z[:, c, s0:s0 + m], K.ident[:, :]),
                                 reads=[zB, K.buf], writes=[psB])
                        p.op("act", lambda e, c4=c4: e.copy(out=yt[:m, c4 * 512:(c4 + 1) * 512], in_=ps[:m, :]), reads=[psB], writes=[ytB])
                    p.dma("sp", O.y[t0 + s0:t0 + s0 + m, :], yt[:m, :], reads=[ytB])


def stage_rwkv_proj(g):
    nc, p, I, O, Sx, K = g.nc, g.p, g.I, g.O, g.Sx, g.K
    pcol, PCB, PC = g.pcol, g.PCB, g.PC
    NT = g.NT
    with ExitStack() as st:
        sb = lambda name, shape, dt=F32: st.enter_context(sbt(nc, name, shape, dt))
        X = sb("X", [128, NCH, 512])
        XB = Buf("X")
        XX = sb("XX", [128, NCH, 512])
        XXB = Buf("XX")
        xmr = Ring(nc, st, "xm", 2, [128, NCH, 512], BF16)
        wr = Ring(nc, st, "wq", 2, [128, NCH, 512], BF16)
        ofr = Ring(nc, st, "of", 3, [128, 512], F32)
        obr = Ring(nc, st, "ob", 2, [128, 512], BF16)
        lor = Ring(nc, st, "lo", 2, [128, 2, 512], BF16)
        w1 = sb("lw1", [128, NCH, 96], BF16)
        w2 = sb("lw2", [96, D], BF16)
        a1 = sb("la1", [128, NCH, 96], BF16)
        a2 = sb("la2", [96, D], BF16)
        g1 = sb("lg1", [128, NCH, 256], BF16)
        g2 = sb("lg2", [128, 2, D], BF16)
        lB = Buf("lora")
        kc = lambda ap: ap.rearrange("(kc p) n -> p kc n", p=128)
        p.dma("pool", w1[:], kc(I.c_w1), writes=[lB])
        p.dma("pool", w2[:], I.c_w2[:, :], writes=[lB])
        p.dma("pool", a1[:], kc(I.c_a1), writes=[lB])
        p.dma("pool", a2[:], I.c_a2[:, :], writes=[lB])
        p.dma("pool", g1[:], kc(I.c_g1), writes=[lB])
        p.dma("pool", g2[:], kc(I.c_g2), writes=[lB])
        X1v = fm(Sx.X1)
        starts = {tok0: (si, isp) for si, (tok0, T, isp) in enumerate(g.seqs)}
        ends = {tok0 + T - 1: si for si, (tok0, T, isp) in enumerate(g.seqs)}
        zc = sb("zc", [128, NCH])
        zB = Buf("zc")
        p.op("dve", lambda e: e.memset(zc[:, :], 0.0), writes=[zB])
        for (t0, n) in tiles_of(NT, 512):
            p.dma("sp", X[:, :, :n], X1v[:, :, t0:t0 + n], writes=[XB])
            if t0 > 0:
                p.dma("sp", XX[:, :, :n], X1v[:, :, t0 - 1:t0 + n - 1], writes=[XXB])
            else:
                p.dma("sp", XX[:, :, 1:n], X1v[:, :, 0:n - 1], writes=[XXB])
            for ts, (si, isp) in starts.items():
                if t0 <= ts < t0 + n:
                    if isp:
                        src = zc[:, :]
                        rd = [zB]
                    else:
                        i0 = PARAMS.index("shift%d" % (si - 2)) * 16
                        src = PC[:, i0:i0 + 16]
                        rd = [PCB]
                    p.op("dve", lambda e, ts=ts, src=src: e.tensor_copy(out=XX[:, :, ts - t0], in_=src), reads=rd, writes=[XXB])
            for te, si in ends.items():
                if t0 <= te < t0 + n:
                    p.dma("sp", O.shift[si:si + 1, :].rearrange("o (c q) -> q (o c)", q=128), X[:, :, te - t0], reads=[XB], slow=True)
            p.op("dve", lambda e: e.tensor_tensor(out=XX[:, :, :n], in0=XX[:, :, :n], in1=X[:, :, :n], op=OP.subtract),
                 reads=[XXB, XB], writes=[XXB])

            def mix(j):
                xm, xmB = xmr.get()
                for c in range(NCH):
                    p.op("dve", lambda e, c=c: e.scalar_tensor_tensor(out=xm[:, c, :n], in0=XX[:, c, :n], scalar=pcol("mu%d" % j, c),
                                                                      in1=X[:, c, :n], op0=OP.mult, op1=OP.add),
                         reads=[XXB, XB, PCB], writes=[xmB])
                return xm, xmB

            def big_gemm(W, xm, xmB, dst):
                Wv = kc(W)
                for c in range(NCH):
                    if c % 4 == 0:
                        w, wB = wr.get()
                        p.dma("pool", w[:], Wv[:, :, c * 128:c * 128 + 512], writes=[wB])
                    wo_ = (c % 4) * 128
                    ps, psB = p.psum()
                    for k_ in range(NCH):
                        p.op("pe", lambda e, k_=k_, wo_=wo_: e.matmul(ps[:, :n], lhsT=w[:, k_, wo_:wo_ + 128], rhs=xm[:, k_, :n], start=(k_ == 0), stop=(k_ == NCH - 1)),
                             reads=[wB, xmB], writes=[psB])
                    o, oB = ofr.get()
                    p.op("act", lambda e: e.copy(out=o[:, :n], in_=ps[:, :n]), reads=[psB], writes=[oB])
                    p.dma("sp", dst[c * 128:(c + 1) * 128, t0:t0 + n], o[:, :n], reads=[oB])

            def lora_in(wt, width, xm, xmB, func):
                lo, loB = lor.get()
                for (m0, m) in tiles_of(width, 128):
                    ps, psB = p.psum()
                    for k_ in range(NCH):
                        p.op("pe", lambda e, k_=k_: e.matmul(ps[:m, :n], lhsT=wt[:, k_, m0:m0 + m], rhs=xm[:, k_, :n], start=(k_ == 0), stop=(k_ == NCH - 1)),
                             reads=[lB, xmB], writes=[psB])
                    if func is None:
                        p.op("act", lambda e: e.copy(out=lo[:m, m0 // 128, :n], in_=ps[:m, :n]), reads=[psB], writes=[loB])
                    else:
                        p.op("act", lambda e: e.activation(out=lo[:m, m0 // 128, :n], in_=ps[:m, :n], func=func), reads=[psB], writes=[loB])
                return lo, loB

            xm, xmB = mix(0)
            big_gemm(I.c_w_r, xm, xmB, Sx.RR)
            xm, xmB = mix(1)
            lo, loB = lora_in(w1, 96, xm, xmB, AF.Tanh)
            for c in range(NCH):
                ps, psB = p.psum()
                p.op("pe", lambda e, c=c: e.matmul(ps[:, :n], lhsT=w2[:, c * 128:(c + 1) * 128], rhs=lo[:96, 0, :n], start=True, stop=True),
                     reads=[lB, loB], writes=[psB])
                o, oB = ofr.get()
                p.op("act", lambda e, c=c: e.activation(out=o[:, :n], in_=ps[:, :n], func=AF.Sigmoid, bias=pcol("w0", c)),
                     reads=[psB, PCB], writes=[oB])
                p.op("act", lambda e: e.activation(out=o[:, :n], in_=o[:, :n], func=AF.Exp, scale=-float(np.exp(-0.5))), reads=[oB], writes=[oB])
                p.dma("sp", Sx.DEC[c * 128:(c + 1) * 128, t0:t0 + n], o[:, :n], reads=[oB])
            xm, xmB = mix(2)
            big_gemm(I.c_w_k, xm, xmB, Sx.KRAW)
            xm, xmB = mix(3)
            big_gemm(I.c_w_v, xm, xmB, Sx.VV)
            xm, xmB = mix(4)
            lo, loB = lora_in(a1, 96, xm, xmB, None)
            for c in range(NCH):
                ps, psB = p.psum()
                p.op("pe", lambda e, c=c: e.matmul(ps[:, :n], lhsT=a2[:, c * 128:(c + 1) * 128], rhs=lo[:96, 0, :n], start=True, stop=True),
                     reads=[lB, loB], writes=[psB])
                o, oB = ofr.get()
                p.op("act", lambda e, c=c: e.activation(out=o[:, :n], in_=ps[:, :n], func=AF.Sigmoid, bias=pcol("a0", c)),
                     reads=[psB, PCB], writes=[oB])
                p.dma("sp", Sx.AA[c * 128:(c + 1) * 128, t0:t0 + n], o[:, :n], reads=[oB])
            xm, xmB = mix(5)
            lo, loB = lora_in(g1, 256, xm, xmB, AF.Sigmoid)
            for c in range(NCH):
                ps, psB = p.psum()
                for k_ in range(2):
                    p.op("pe", lambda e, c=c, k_=k_: e.matmul(ps[:, :n], lhsT=g2[:, k_, c * 128:(c + 1) * 128], rhs=lo[:, k_, :n],
                                                              start=(k_ == 0), stop=(k_ == 1)), reads=[lB, loB], writes=[psB])
                o, oB = obr.get()
                p.op("act", lambda e: e.copy(out=o[:, :n], in_=ps[:, :n]), reads=[psB], writes=[oB])
                p.dma("sp", Sx.GG[c * 128:(c + 1) * 128, t0:t0 + n], o[:, :n], reads=[oB])


def stage_rwkv_prep(g):
    nc, p, I, O, Sx, K = g.nc, g.p, g.I, g.O, g.Sx, g.K
    pcol, PCB, PC = g.pcol, g.PCB, g.PC
    NT = g.NT
    with ExitStack() as st:
        sb = lambda name, shape, dt=F32: st.enter_context(sbt(nc, name, shape, dt))
        omka = sb("omka", [128, NCH])
        omB = Buf("omka")
        i0 = PARAMS.index("k_a") * 16
        p.op("dve", lambda e: e.tensor_scalar(out=omka[:, :], in0=PC[:, i0:i0 + 16], scalar1=-1.0, scalar2=1.0, op0=OP.mult, op1=OP.add),
             reads=[PCB], writes=[omB])
        ring = lambda nm, k=2: Ring(nc, st, nm, k, [128, 512], F32)
        kr, ar, rr_, vr = ring("pk"), ring("pa"), ring("pr"), ring("pv")
        kkr, sqr, k2r, bbr, nkr, bor = ring("pkk"), ring("psq"), ring("pk2"), ring("pbb"), ring("pnk"), ring("pbo")
        vtk = sb("vtk", [128, 4, D], BF16)
        vtkB = Buf("vtk")
        dcr, gmr, gpr = ring("pdc"), ring("pgm"), ring("pgp")
        zer = sb("zer", [128, RB])
        zerB = Buf("zer")
        p.op("dve", lambda e: e.memset(zer[:, :], 0.0), writes=[zerB])
        for (t0, n) in tiles_of(NT, 512):
            for c in range(NCH):
                rows = slice(c * 128, (c + 1) * 128)
                k, kB = kr.get()
                a, aB = ar.get()
                r, rB = rr_.get()
                v, vB = vr.get()
                p.dma("sp", k[:, :n], Sx.KRAW[rows, t0:t0 + n], writes=[kB])
                p.dma("sp", a[:, :n], Sx.AA[rows, t0:t0 + n], writes=[aB])
                p.dma("sp", r[:, :n], Sx.RR[rows, t0:t0 + n], writes=[rB])
                p.dma("sp", v[:, :n], Sx.VV[rows, t0:t0 + n], writes=[vB])
                kk, kkB = kkr.get()
                sq, sqB = sqr.get()
                p.op("dve", lambda e: e.tensor_scalar(out=kk[:, :n], in0=k[:, :n], scalar1=pcol("k_k", c), scalar2=None, op0=OP.mult),
                     reads=[kB, PCB], writes=[kkB])
                p.op("act", lambda e: e.activation(out=sq[:, :n], in_=kk[:, :n], func=AF.Square), reads=[kkB], writes=[sqB])
                ps, psB = p.psum()
                p.op("pe", lambda e: e.matmul(ps[:, :n], lhsT=K.blk[:, :], rhs=sq[:, :n], start=True, stop=True), reads=[sqB, K.buf], writes=[psB])
                p.op("dve", lambda e: e.tensor_scalar(out=sq[:, :n], in0=ps[:, :n], scalar1=1e-24, scalar2=None, op0=OP.max),
                     reads=[psB], writes=[sqB])
                p.op("act", lambda e: e.activation(out=sq[:, :n], in_=sq[:, :n], func=AF.Sqrt), reads=[sqB], writes=[sqB])
                p.op("dve", lambda e: e.reciprocal(out=sq[:, :n], in_=sq[:, :n]), reads=[sqB], writes=[sqB])
                p.op("dve", lambda e: e.tensor_tensor(out=kk[:, :n], in0=kk[:, :n], in1=sq[:, :n], op=OP.mult), reads=[kkB, sqB], writes=[kkB])
                bb, bbB = bbr.get()
                nk, nkB = nkr.get()
                p.op("dve", lambda e: e.tensor_tensor(out=bb[:, :n], in0=kk[:, :n], in1=a[:, :n], op=OP.mult), reads=[kkB, aB], writes=[bbB])
                p.op("act", lambda e: e.activation(out=nk[:, :n], in_=kk[:, :n], func=AF.Copy, scale=-1.0), reads=[kkB], writes=[nkB])
                k2, k2B = k2r.get()
                p.op("dve", lambda e: e.tensor_scalar(out=k2[:, :n], in0=a[:, :n], scalar1=pcol("k_a", c), scalar2=omka[:, c:c + 1],
                                                      op0=OP.mult, op1=OP.add), reads=[aB, PCB, omB], writes=[k2B])
                p.op("dve", lambda e: e.tensor_tensor(out=k2[:, :n], in0=k2[:, :n], in1=k[:, :n], op=OP.mult), reads=[k2B, kB], writes=[k2B])
                bo, boB = bor.get()
                p.op("dve", lambda e: e.scalar_tensor_tensor(out=bo[:, :n], in0=r[:, :n], scalar=pcol("r_k", c), in1=k2[:, :n],
                                                             op0=OP.mult, op1=OP.mult), reads=[rB, k2B, PCB], writes=[boB])
                ps2, ps2B = p.psum()
                p.op("pe", lambda e: e.matmul(ps2[:, :n], lhsT=K.blk[:, :], rhs=bo[:, :n], start=True, stop=True), reads=[boB, K.buf], writes=[ps2B])
                p.op("dve", lambda e: e.tensor_tensor(out=bo[:, :n], in0=ps2[:, :n], in1=v[:, :n], op=OP.mult), reads=[ps2B, vB], writes=[boB])
                p.dma("sp", Sx.BONUS[rows, t0:t0 + n], bo[:, :n], reads=[boB])
                dc, dcB = dcr.get()
                gm, gmB = gmr.get()
                gp, gpB = gpr.get()
                p.dma("sp", dc[:, :n], Sx.DEC[rows, t0:t0 + n], writes=[dcB])
                for c0 in range(0, n, RB):
                    p.op("dve", lambda e, c0=c0: e.tensor_tensor_scan(out=gm[:, c0:c0 + RB], data0=dc[:, c0:c0 + RB], data1=zer[:, :RB], initial=1.0,
                                                                      op0=OP.mult, op1=OP.add), reads=[dcB, zerB], writes=[gmB])
                b3 = lambda a: a.rearrange("p (b t) -> p b t", t=RB)
                p.op("act", lambda e: e.copy(out=b3(gp[:, :n])[:, :, 1:RB], in_=b3(gm[:, :n])[:, :, 0:RB - 1]), reads=[gmB], writes=[gpB])
                p.op("dve", lambda e: e.memset(b3(gp[:, :n])[:, :, 0:1], 1.0), writes=[gpB])
                p.op("dve", lambda e: e.tensor_tensor(out=nk[:, :n], in0=nk[:, :n], in1=gp[:, :n], op=OP.mult), reads=[nkB, gpB], writes=[nkB])
                p.op("dve", lambda e: e.tensor_tensor(out=r[:, :n], in0=r[:, :n], in1=gm[:, :n], op=OP.mult), reads=[rB, gmB], writes=[rB])
                p.dma("sp", Sx.GAM[rows, t0:t0 + n], gm[:, :n], reads=[gmB])
                p.op("dve", lambda e: e.reciprocal(out=gp[:, :n], in_=gm[:, :n]), reads=[gmB], writes=[gpB])
                p.op("dve", lambda e: e.tensor_tensor(out=bb[:, :n], in0=bb[:, :n], in1=gp[:, :n], op=OP.mult), reads=[bbB, gpB], writes=[bbB])
                p.op("dve", lambda e: e.tensor_tensor(out=k2[:, :n], in0=k2[:, :n], in1=gp[:, :n], op=OP.mult), reads=[k2B, gpB], writes=[k2B])
                p.dma("sp", Sx.NKK[rows, t0:t0 + n], nk[:, :n], reads=[nkB])
                p.dma("sp", Sx.BB[rows, t0:t0 + n], bb[:, :n], reads=[bbB])
                p.dma("sp", Sx.K2[rows, t0:t0 + n], k2[:, :n], reads=[k2B])
                p.dma("sp", Sx.RR2[rows, t0:t0 + n], r[:, :n], reads=[rB])
                psT, psTB = p.psum()
                subs = tiles_of(n, 128)
                for si_, (s0, m) in enumerate(subs):
                    p.op("pe", lambda e, s0=s0, m=m, si_=si_: e.transpose(psT[:m, si_ * 128:(si_ + 1) * 128], v[:, s0:s0 + m], K.ident[:, :]),
                         reads=[vB, K.buf], writes=[psTB])
                for si_, (s0, m) in enumerate(subs):
                    p.op("act", lambda e, m=m, si_=si_, c=c: e.copy(out=vtk[:m, si_, c * 128:(c + 1) * 128], in_=psT[:m, si_ * 128:(si_ + 1) * 128]),
                         reads=[psTB], writes=[vtkB])
            for si_, (s0, m) in enumerate(tiles_of(n, 128)):
                p.dma("sp", Sx.VTOK[t0 + s0:t0 + s0 + m, :], vtk[:m, si_, :], reads=[vtkB])


def stage_rwkv_rec(g):
    nc, p, I, O, Sx, K = g.nc, g.p, g.I, g.O, g.Sx, g.K
    with ExitStack() as st:
        sb = lambda name, shape, dt=F32: st.enter_context(sbt(nc, name, shape, dt))
        ST = sb("ST", [128, 32, 64])
        STB = [Buf("ST0"), Buf("ST1")]
        blkb = sb("blkb", [128, 128], BF16)
        sel = sb("sel", [128, 2], BF16)
        idb = sb("idb", [128, 128], BF16)
        MK = sb("MK", [32, 64])
        cB = Buf("rc")
        p.op("dve", lambda e: e.tensor_copy(out=blkb[:, :], in_=K.blk[:, :]), reads=[K.buf], writes=[cB])
        p.op("dve", lambda e: e.tensor_copy(out=sel[:, 0:1], in_=K.blk[:, 0:1]), reads=[K.buf], writes=[cB])
        p.op("dve", lambda e: e.tensor_copy(out=sel[:, 1:2], in_=K.blk[:, 64:65]), reads=[K.buf], writes=[cB])
        p.op("dve", lambda e: e.tensor_copy(out=idb[:, :], in_=K.ident[:, :]), reads=[K.buf], writes=[cB])
        p.op("dve", lambda e: e.tensor_tensor(out=MK[:, 0:32], in0=K.tri[:32, :32], in1=K.ident[:32, :32], op=OP.subtract), reads=[K.buf], writes=[cB])
        p.op("dve", lambda e: e.tensor_copy(out=MK[:, 32:64], in_=K.tri[:32, :32]), reads=[K.buf], writes=[cB])
        p.op("dve", lambda e: e.memset(MK[:, 63:64], 0.0), reads=[cB], writes=[cB])
        OH = sb("OH", [32, 2, 32, 128], BF16)
        p.op("dve", lambda e: e.memset(OH[:, :, :, :], 0.0), writes=[cB])
        for h2 in range(2):
            p.op("dve", lambda e, h2=h2: e.tensor_copy(out=OH[:, h2, :, h2 * 64:(h2 + 1) * 64],
                                                        in_=K.ident[:32, :32].unsqueeze(2).to_broadcast([32, 32, 64])),
                 reads=[K.buf, cB], writes=[cB])
        k2m_r = Ring(nc, st, "k2m", 1, [128, 2, 32, RB], BF16)
        nrb_r = Ring(nc, st, "nrb", 1, [128, 2, 32, RB], BF16)
        TB = RB
        names = ("GAM", "BB", "K2", "RR2")
        blk_r = {nm: Ring(nc, st, "bt" + nm, 2, [128, 32, TB], F32) for nm in names}
        cmb_r = Ring(nc, st, "btCMB", 2, [128, 2, 32, TB], F32)
        for t_, b_ in zip(cmb_r.t, cmb_r.b):
            p.op("dve", lambda e, t_=t_: e.memset(t_[:, :, :, :], 0.0), writes=[b_])
        vblk_r = Ring(nc, st, "vblk", 2, [32, 2, D], BF16)
        gm_r = Ring(nc, st, "Gm", 2, [32, 32, 64], BF16)
        sap_r = Ring(nc, st, "saP", 2, [32, 2, D], BF16)
        yp_r = Ring(nc, st, "yPs", 1, [32, D], F32)
        ktok_r = Ring(nc, st, "ktok", 2, [32, D], BF16)
        pl_r = Ring(nc, st, "PLs", 2, [128, 2, 1024], F32)
        ayr = Ring(nc, st, "ay", 2, [128, 2, 1024], BF16)
        tBr = Ring(nc, st, "tB", 2, [128, 1024], F32)
        TS = 2
        ysr = Ring(nc, st, "ys", 2, [2, 2, TS, 1024], F32)
        sior = Ring(nc, st, "sio", 2, [64, 2, 64], F32)
        v3 = lambda t2: t2.rearrange("p (j v) -> p j v", v=64)
        RRv = fm(Sx.RR)
        RR2v = fm(Sx.RR2)
        NKv = fm(Sx.NKK)
        for grp, (sis, T) in enumerate((([0, 1], g.Tp), ([2, 3], g.Ts))):
            toks = [g.seqs[si][0] for si in sis]
            assert T % TB == 0
            if grp == 0:
                for hf in range(2):
                    p.op("dve", lambda e, hf=hf: e.memset(ST[:, hf * 16:(hf + 1) * 16, :], 0.0), writes=[STB[hf]])
            else:
                for hf in range(2):
                    for j in range(16):
                        sio, sioB = sior.get()
                        p.dma("sp", sio[:, :, :], I.S[hf, 2 * j:2 * j + 2, :, :].rearrange("h v k -> v h k"), writes=[sioB])
                        ps, psB = p.psum()
                        p.op("pe", lambda e: e.transpose(ps[:, 0:64], sio[:, :, :].rearrange("v h k -> v (h k)"), K.ident[:64, :64]),
                             reads=[sioB, K.buf], writes=[psB])
                        p.op("act", lambda e, hf=hf, j=j: e.copy(out=ST[:, hf * 16 + j, :], in_=ps[:, 0:64]), reads=[psB], writes=[STB[hf]])
            ys_slots = {}
            sth = lambda hf: ST[:, hf * 16:(hf + 1) * 16, :]

            def y_flush(ty):
                if ty % TS == TS - 1 or ty == T - 1:
                    nts = ty % TS + 1
                    tf = ty - nts + 1
                    ys, ysB = ys_slots[tf // TS]
                    for hf in range(2):
                        dst = Sx.YT[toks[hf] + tf:toks[hf] + tf + nts, :].rearrange("t (j h v) -> h t j v", h=2, v=64)
                        p.dma("sp", dst, ys[:, hf, :nts, :].rearrange("h t (j v) -> h t j v", v=64), reads=[ysB])
                    del ys_slots[tf // TS]

            def emit_y(ay, ayB, hf, ty):
                if ty // TS not in ys_slots:
                    ys_slots[ty // TS] = ysr.get()
                ys, ysB = ys_slots[ty // TS]
                ps, psBs = p.psum2([2], "recY")
                for k_ in range(2):
                    p.op("pe", lambda e, k_=k_: e.matmul(ps[:2, k_ * 512:(k_ + 1) * 512], lhsT=sel[:, :], rhs=ay[:, 1, k_ * 512:(k_ + 1) * 512],
                                                         start=True, stop=True), reads=[cB, ayB], writes=[psBs[k_]])
                p.op("act", lambda e: e.copy(out=ys[:, hf, ty % TS, :], in_=ps[:2, :]), reads=psBs, writes=[ysB])

            def prologue(tb0):
                B = {}
                for nm in names:
                    t_, b_ = blk_r[nm].get()
                    src = fm(getattr(Sx, nm))
                    for hf in range(2):
                        p.dma("sp", t_[:, hf * 16:(hf + 1) * 16, :TB], src[:, :, toks[hf] + tb0:toks[hf] + tb0 + TB], writes=[b_])
                    B[nm] = (t_, b_)
                cm, cmB = cmb_r.get()
                for hf in range(2):
                    p.dma("sp", cm[:, 0, hf * 16:(hf + 1) * 16, :TB], NKv[:, :, toks[hf] + tb0:toks[hf] + tb0 + TB], writes=[cmB])
                    p.dma("sp", cm[:, 1, hf * 16:(hf + 1) * 16, 1:TB], RR2v[:, :, toks[hf] + tb0:toks[hf] + tb0 + TB - 1], writes=[cmB])
                    if tb0 > 0:
                        p.dma("sp", cm[:, 1, hf * 16:(hf + 1) * 16, 0:1], RRv[:, :, toks[hf] + tb0 - 1:toks[hf] + tb0], writes=[cmB], slow=True)
                B["cm"] = (cm, cmB)
                vb, vbB = vblk_r.get()
                for hf in range(2):
                    p.dma("sp", vb[:, hf, :], Sx.VTOK[toks[hf] + tb0:toks[hf] + tb0 + TB, :], writes=[vbB])
                sap, sapB = sap_r.get()
                pl, plB = pl_r.get()
                B["sap"] = (sap, sapB)
                B["pl"] = (pl, plB)
                yield
                k2t, k2B = B["K2"]
                r2t, r2B = B["RR2"]
                k2m, k2mB = k2m_r.get()
                nrb, nrbB = nrb_r.get()
                for h2 in range(2):
                    p.op("act", lambda e, h2=h2: e.activation(out=k2m[:, h2, :, :], in_=k2t[:, :, :], func=AF.Copy, scale=K.blk[:, h2 * 64:h2 * 64 + 1]),
                         reads=[k2B, K.buf], writes=[k2mB])
                p.op("act", lambda e: e.copy(out=nrb[:, 0, :, :], in_=cm[:, 0, :, :]), reads=[cmB], writes=[nrbB])
                p.op("act", lambda e: e.copy(out=nrb[:, 1, :, :], in_=r2t[:, :, :]), reads=[r2B], writes=[nrbB])
                yield
                for hf in range(2):
                    gm, gmB = gm_r.get()
                    for q in range(2):
                        ps, psBs = p.psum2([3], "recP")
                        for hh in range(16):
                            h = q * 16 + hh
                            j, h2 = h // 2, h % 2
                            rows = slice(h2 * 64, (h2 + 1) * 64)
                            bj = hf * 16 + j
                            p.op("pe", lambda e, hh=hh, h2=h2, bj=bj: e.matmul(ps[:32, hh * 64:hh * 64 + 32], lhsT=k2m[:, h2, bj, :], rhs=nrb[:, 0, bj, :],
                                                                              start=True, stop=True), reads=[k2mB, nrbB], writes=[psBs[hh // 8]])
                            p.op("pe", lambda e, hh=hh, h2=h2, bj=bj: e.matmul(ps[:32, hh * 64 + 32:hh * 64 + 64], lhsT=k2m[:, h2, bj, :], rhs=nrb[:, 1, bj, :],
                                                                              start=True, stop=True), reads=[k2mB, nrbB], writes=[psBs[hh // 8]])
                            if hh % 4 == 3:
                                yield
                        p.op("dve", lambda e, q=q, ps=ps: e.tensor_tensor(out=gm[:, q * 16:(q + 1) * 16, :], in0=ps[:32, :].rearrange("s (h t) -> s h t", t=64),
                                                                         in1=MK[:, :].unsqueeze(1).to_broadcast([32, 16, 64]), op=OP.mult),
                             reads=psBs + [cB], writes=[gmB])
                        yield
                    for which in range(2):
                        if which == 1:
                            yps, ypsB = yp_r.get()
                        for q in range(2):
                            ps, psBs = p.psum2([3], "recP")
                            for hh in range(16):
                                h = q * 16 + hh
                                p.op("pe", lambda e, hh=hh, h=h: e.matmul(ps[:32, hh * 64:(hh + 1) * 64], lhsT=gm[:, h, which * 32:(which + 1) * 32],
                                                                          rhs=vb[:, hf, h * 64:(h + 1) * 64], start=True, stop=True),
                                     reads=[gmB, vbB], writes=[psBs[hh // 8]])
                                if hh % 8 == 7:
                                    yield
                            if which == 0:
                                p.op("act", lambda e, q=q, ps=ps: e.copy(out=sap[:, hf, q * 1024:(q + 1) * 1024], in_=ps[:32, :]), reads=psBs, writes=[sapB])
                            else:
                                p.op("act", lambda e, q=q, ps=ps: e.copy(out=yps[:, q * 1024:(q + 1) * 1024], in_=ps[:32, :]), reads=psBs, writes=[ypsB])
                            yield
                        if which == 1:
                            p.dma("sp", Sx.YP[toks[hf] + tb0:toks[hf] + tb0 + TB, :], yps[:, :], reads=[ypsB])
                    kt, ktB = ktok_r.get()
                    for q in range(2):
                        ps, psBs = p.psum2([3], "recP")
                        for jj in range(8):
                            j = q * 8 + jj
                            p.op("pe", lambda e, jj=jj, j=j: e.transpose(ps[:32, jj * 128:(jj + 1) * 128], k2t[:, hf * 16 + j, :], K.ident[:, :]),
                                 reads=[k2B, K.buf], writes=[psBs[jj // 4]])
                        p.op("act", lambda e, q=q, ps=ps: e.copy(out=kt[:, q * 1024:(q + 1) * 1024], in_=ps[:32, :]), reads=psBs, writes=[ktB])
                        yield
                    ps, psBs = p.psum2([3], "recP")
                    for j in range(16):
                        for h2 in range(2):
                            p.op("pe", lambda e, j=j, h2=h2: e.matmul(ps[h2 * 64:(h2 + 1) * 64, j * 64:(j + 1) * 64],
                                                                      lhsT=kt[:, j * 128 + h2 * 64:j * 128 + (h2 + 1) * 64],
                                                                      rhs=vb[:, hf, (2 * j + h2) * 64:(2 * j + h2 + 1) * 64], start=True, stop=True),
                                 reads=[ktB, vbB], writes=[psBs[j // 8]])
                        if j % 4 == 3:
                            yield
                    p.op("act", lambda e, ps=ps: e.copy(out=pl[:, hf, :], in_=ps), reads=psBs, writes=[plB])
                    yield
                self_out.append(B)

            nblk = T // TB
            self_out = []
            gen = prologue(0)
            for _ in gen:
                pass
            for bi in range(nblk):
                tb0 = bi * TB
                B = self_out.pop(0)
                gen = prologue(tb0 + TB) if bi + 1 < nblk else iter(())
                bt = B
                cm, cmB = B["cm"]
                sap, sapB = B["sap"]
                pl, plB = B["pl"]
                for tt in range(TB):
                    t = tb0 + tt

                    def col(nm, hf, j0=0, nj=16):
                        t_ = bt[nm][0]
                        return t_[:, hf * 16 + j0:hf * 16 + j0 + nj, tt:tt + 1].to_broadcast([128, nj, 64])

                    pend_sa = {}
                    for hf in range(2):
                        ay, ayB = ayr.get()
                        p.op("dve", lambda e, hf=hf: e.tensor_tensor(
                            out=ay[:, :, :].rearrange("p x (j v) -> p x j v", v=64),
                            in0=sth(hf).unsqueeze(1).to_broadcast([128, 2, 16, 64]),
                            in1=cm[:, :, hf * 16:(hf + 1) * 16, tt:tt + 1].to_broadcast([128, 2, 16, 64]), op=OP.mult),
                             reads=[STB[hf], cmB], writes=[ayB])
                        ps, psBs = p.psum2([0, 1], "recS")
                        for k_ in range(2):
                            for h2 in range(2):
                                rhs = sap[:, hf, :].rearrange("t (j h v) -> t j h v", h=2, v=64)[:, k_ * 8:(k_ + 1) * 8, h2, :]
                                p.op("pe", lambda e, h2=h2, rhs=rhs, k_=k_, ps=ps: e.matmul(ps[:, k_ * 512:(k_ + 1) * 512], lhsT=OH[:, h2, tt, :],
                                                                                         rhs=rhs, start=(h2 == 0), stop=False),
                                     reads=[cB, sapB], writes=[psBs[k_]])
                            p.op("pe", lambda e, k_=k_, ps=ps: e.matmul(ps[:, k_ * 512:(k_ + 1) * 512], lhsT=blkb[:, :], rhs=ay[:, 0, k_ * 512:(k_ + 1) * 512],
                                                                       start=False, stop=True), reads=[cB, ayB], writes=[psBs[k_]])
                        pend_sa[hf] = (ps, psBs, ay, ayB)
                    if t > 0:
                        for hf in range(2):
                            emit_y(pend_sa[hf][2], pend_sa[hf][3], hf, t - 1)
                    for hf in range(2):
                        ps, psBs = pend_sa[hf][0], pend_sa[hf][1]
                        tB, tBB = tBr.get()
                        p.op("dve", lambda e, hf=hf, ps=ps, tB=tB: e.tensor_tensor(out=v3(tB[:, :]), in0=v3(ps), in1=col("BB", hf), op=OP.mult),
                             reads=psBs + [bt["BB"][1]], writes=[tBB])
                        p.op("dve", lambda e, hf=hf, tB=tB: e.tensor_tensor(out=sth(hf), in0=sth(hf), in1=v3(tB[:, :]), op=OP.add),
                             reads=[STB[hf], tBB], writes=[STB[hf]])
                        if tt == TB - 1:
                            p.op("dve", lambda e, hf=hf: e.tensor_tensor(out=sth(hf), in0=sth(hf), in1=v3(pl[:, hf, :]), op=OP.add),
                                 reads=[STB[hf], plB], writes=[STB[hf]])
                            p.op("dve", lambda e, hf=hf: e.tensor_tensor(out=sth(hf), in0=sth(hf), in1=col("GAM", hf), op=OP.mult),
                                 reads=[STB[hf], bt["GAM"][1]], writes=[STB[hf]])
                    if t > 0:
                        y_flush(t - 1)
                    for _ in range(g.cfg.get("pro_rate", 3)):
                        next(gen, None)
                for _ in gen:
                    pass
            for hf in range(2):
                ay, ayB = ayr.get()
                rt, rtB = blk_r["GAM"].get()
                p.dma("sp", rt[:, hf * 16:(hf + 1) * 16, 0:1], RRv[:, :, toks[hf] + T - 1:toks[hf] + T], writes=[rtB], slow=True)
                p.op("dve", lambda e, hf=hf: e.tensor_tensor(out=v3(ay[:, 1, :]), in0=sth(hf),
                                                              in1=rt[:, hf * 16:(hf + 1) * 16, 0:1].to_broadcast([128, 16, 64]), op=OP.mult),
                     reads=[STB[hf], rtB], writes=[ayB])
                emit_y(ay, ayB, hf, T - 1)
            y_flush(T - 1)
            for hf in range(2):
                oi = sis[hf]
                for j in range(16):
                    ps, psB = p.psum()
                    p.op("pe", lambda e, hf=hf, j=j: e.transpose(ps[:64, 0:128], ST[:, hf * 16 + j, :], K.ident[:, :]),
                         reads=[STB[hf], K.buf], writes=[psB])
                    sio, sioB = sior.get()
                    p.op("act", lambda e: e.copy(out=sio[:, :, :].rearrange("v h k -> v (h k)"), in_=ps[:64, 0:128]), reads=[psB], writes=[sioB])
                    p.dma("sp", O.S[oi, 2 * j:2 * j + 2, :, :].rearrange("h v k -> v h k"), sio[:, :, :], reads=[sioB])


def stage_rwkv_post(g):
    nc, p, I, O, Sx, K = g.nc, g.p, g.I, g.O, g.Sx, g.K
    pcol, PCB = g.pcol, g.PCB
    with ExitStack() as st:
        ytr = Ring(nc, st, "pyt", 2, [128, D], F32)
        sqr = Ring(nc, st, "psq", 2, [128, D], F32)
        str_ = Ring(nc, st, "pst", 2, [128, 96], F32)
        fmr = Ring(nc, st, "pfm", 2, [128, NCH, 128], F32)
        bor = Ring(nc, st, "pbo", 2, [128, NCH, 128], F32)
        ggr = Ring(nc, st, "pgg", 2, [128, NCH, 128], BF16)
        outr = Ring(nc, st, "pout", 2, [128, NCH, 128], BF16)
        h3 = lambda a: a.rearrange("t (h v) -> t h v", v=64)
        for (r0, m) in tiles_of(g.NT, 128):
            yt, ytB = ytr.get()
            sq, sqB = sqr.get()
            sx, sxB = str_.get()
            bo, boB = bor.get()
            gg, ggB = ggr.get()
            p.dma("sp", yt[:m, :], Sx.YT[r0:r0 + m, :], writes=[ytB])
            p.dma("sp", sq[:m, :], Sx.YP[r0:r0 + m, :], writes=[sqB])
            p.op("dve", lambda e: e.tensor_tensor(out=yt[:m, :], in0=yt[:m, :], in1=sq[:m, :], op=OP.add), reads=[ytB, sqB], writes=[ytB])
            p.dma("sp", bo[:, :, :m], fm(Sx.BONUS)[:, :, r0:r0 + m], writes=[boB])
            p.dma("sp", gg[:, :, :m], fm(Sx.GG)[:, :, r0:r0 + m], writes=[ggB])
            p.op("dve", lambda e: e.tensor_reduce(out=sx[:m, 0:32], in_=h3(yt[:m, :]), axis=AX.X, op=OP.add), reads=[ytB], writes=[sxB])
            p.op("dve", lambda e: e.tensor_scalar(out=sx[:m, 0:32], in0=sx[:m, 0:32], scalar1=1.0 / 64, scalar2=None, op0=OP.mult),
                 reads=[sxB], writes=[sxB])
            p.op("dve", lambda e: e.tensor_tensor(out=h3(yt[:m, :]), in0=h3(yt[:m, :]), in1=sx[:m, 0:32].unsqueeze(2).to_broadcast([m, 32, 64]),
                                                  op=OP.subtract), reads=[ytB, sxB], writes=[ytB])
            p.op("act", lambda e: e.activation(out=sq[:m, :], in_=yt[:m, :], func=AF.Square), reads=[ytB], writes=[sqB])
            p.op("dve", lambda e: e.tensor_reduce(out=sx[:m, 32:64], in_=h3(sq[:m, :]), axis=AX.X, op=OP.add), reads=[sqB], writes=[sxB])
            p.op("act", lambda e: e.activation(out=sx[:m, 64:96], in_=sx[:m, 32:64], func=AF.Sqrt, scale=1.0 / 64, bias=K.eps_gn[:m, :]),
                 reads=[sxB, K.buf], writes=[sxB])
            p.op("dve", lambda e: e.reciprocal(out=sx[:m, 64:96], in_=sx[:m, 64:96]), reads=[sxB], writes=[sxB])
            p.op("dve", lambda e: e.tensor_tensor(out=h3(yt[:m, :]), in0=h3(yt[:m, :]), in1=sx[:m, 64:96].unsqueeze(2).to_broadcast([m, 32, 64]),
                                                  op=OP.mult), reads=[ytB, sxB], writes=[ytB])
            fmt, fmB = fmr.get()
            for c4 in range(4):
                ps, psB = p.psum()
                for j in range(4):
                    c = c4 * 4 + j
                    p.op("pe", lambda e, c=c, j=j: e.transpose(ps[:, j * 128:j * 128 + m], yt[:m, c * 128:(c + 1) * 128], K.ident[:m, :m]),
                         reads=[ytB, K.buf], writes=[psB])
                for j in range(4):
                    c = c4 * 4 + j
                    p.op("act", lambda e, c=c, j=j: e.activation(out=fmt[:, c, :m], in_=ps[:, j * 128:j * 128 + m], func=AF.Identity,
                                                                  scale=pcol("gn_g", c), bias=pcol("gn_b", c)),
                         reads=[psB, PCB], writes=[fmB])
            p.op("dve", lambda e: e.tensor_tensor(out=fmt[:, :, :m], in0=fmt[:, :, :m], in1=bo[:, :, :m], op=OP.add), reads=[fmB, boB], writes=[fmB])
            ot, otB = outr.get()
            p.op("dve", lambda e: e.tensor_tensor(out=ot[:, :, :m], in0=fmt[:, :, :m], in1=gg[:, :, :m], op=OP.mult), reads=[fmB, ggB], writes=[otB])
            p.dma("sp", fm(Sx.YG2)[:, :, r0:r0 + m], ot[:, :, :m], reads=[otB])

def host_consts():
    ident = np.eye(128, dtype=np.float32)
    tri = np.triu(np.ones((128, 128), np.float32))
    ones = np.ones((128, 128), np.float32)
    blk = np.zeros((128, 128), np.float32)
    blk[:64, :64] = 1
    blk[64:, 64:] = 1
    return {"k_ident": ident, "k_tri": tri, "k_ones": ones, "k_blk": blk}


W2D = {"a_b_ig": (1, HM), "a_b_fg": (1, HM), "a_mlstm_norm": (1, D), "a_conv_w": (4, D), "a_conv_b": (1, D),
       "a_lru_wa": (16, 128, 128), "a_lru_ba": (1, D), "a_lru_wx": (16, 128, 128), "a_lru_bx": (1, D),
       "a_lru_lambda": (1, D), "a_w_in": (D, INW), "a_w_out": (2 * D, D), "c_mu": (6, D), "c_w_r": (D, D),
       "c_w_k": (D, D), "c_w_v": (D, D), "c_w0": (1, D), "c_w1": (D, 96), "c_w2": (96, D), "c_a0": (1, D),
       "c_a1": (D, 96), "c_a2": (96, D), "c_g1": (D, 256), "c_g2": (256, D), "c_k_k": (1, D), "c_k_a": (1, D),
       "c_r_k": (1, D), "c_gn_g": (1, D), "c_gn_b": (1, D), "c_w_o": (D, D)}
WKEEP = ("ln1_g", "ln1_b", "ln2_g", "ln2_b", "mlp_w1", "mlp_w2")


def make_in_maps(inputs, n_cores, Tp, Ts):
    f = lambda a: np.ascontiguousarray(np.asarray(a, dtype=np.float32))
    shared = dict(host_consts())
    for k, shp in W2D.items():
        shared[k] = f(inputs[k]).reshape(shp)
    for k in WKEEP:
        shared[k] = f(inputs[k])
    xp, xs = f(inputs["x_prompt"]), f(inputs["x_sample"])
    maps = []
    for c in range(n_cores):
        b = slice(2 * c, 2 * c + 2)
        m = dict(shared)
        m["x_all"] = np.concatenate([xp[b].reshape(2 * Tp, D), xs[b].reshape(2 * Ts, D)], 0)
        m["st_mC"] = f(inputs["state_mlstm_C"])[0, b]
        m["st_mn"] = f(inputs["state_mlstm_n"])[0, b]
        m["st_mm"] = f(inputs["state_mlstm_m"])[0, b]
        m["st_conv"] = f(inputs["state_lru_conv"])[0, b]
        m["st_lruh"] = f(inputs["state_lru_h"])[0, b]
        m["st_shift"] = f(inputs["state_rwkv_shift"])[0, b]
        m["st_S"] = f(inputs["state_rwkv_S"])[0, b]
        m = {k: np.ascontiguousarray(v) for k, v in m.items()}
        maps.append(m)
    return maps


_CACHE = {}


def run(inputs, n_cores=8, cfg=None):
    Tp = inputs["x_prompt"].shape[1]
    Ts = inputs["x_sample"].shape[1]
    cfg = dict(cfg or {})
    cfg.update(Tp=Tp, Ts=Ts)
    nc = build(cfg)
    maps = make_in_maps(inputs, n_cores, Tp, Ts)
    res = run_bass_kernel_spmd(nc, maps, core_ids=list(range(n_cores)))
    return res.results


def kernel(**inputs):
    r = run(inputs, 8)
    Tp = inputs["x_prompt"].shape[1]
    Ts = inputs["x_sample"].shape[1]
    n = len(r)
    yp = np.stack([r[c]["o_y"][:2 * Tp].reshape(2, Tp, D) for c in range(n)]).reshape(2 * n, Tp, D)
    ys = np.stack([r[c]["o_y"][2 * Tp:].reshape(2, Ts, D) for c in range(n)]).reshape(2 * n, Ts, D)

    def gather(name, lo):
        a = np.concatenate([r[c][name][lo:lo + 2] for c in range(n)], 0)
        return np.ascontiguousarray(a[None].astype(np.float32))

    outs = [yp.astype(np.float32), ys.astype(np.float32)]
    for lo in (0, 2):
        for nm in ("o_mC", "o_mn", "o_mm", "o_conv", "o_lruh", "o_shift", "o_S"):
            outs.append(gather(nm, lo))
    return tuple(outs)
```

```python
import numpy as np
import ml_dtypes
from contextlib import ExitStack
import concourse.bass as bass
import concourse.mybir as mybir
from concourse.bass_utils import run_bass_kernel_spmd

F32 = mybir.dt.float32
BF16 = mybir.dt.bfloat16
AF = mybir.ActivationFunctionType
OP = mybir.AluOpType
AX = mybir.AxisListType

D = 2048
NCH = 16
DFF = 8192
HM = 8
DK = 128
DV = 256
INW = 10256
C_Q, C_K, C_V, C_O, C_IG, C_FG, C_XR, C_YG = 0, 1024, 2048, 4096, 6144, 6152, 6160, 8208
ALPHA = 4.0 ** 0.25
LN_EPS = 1e-5
GN_EPS = 64e-5
RH = 32
RN = 64
RB = 32


class Buf:
    __slots__ = ("name", "w", "r", "excl")

    def __init__(self, name="", excl=False):
        self.name = name
        self.excl = excl
        self.w = None
        self.r = {}


class Prog:
    NDMA = {"sp": 12, "pool": 12, "act": 4}

    def __init__(self, nc):
        self.nc = nc
        self.E = {"pe": nc.tensor, "dve": nc.vector, "act": nc.scalar, "pool": nc.gpsimd, "sp": nc.sync}
        self.sems = {}
        self.cnt = {}
        for e in ("pe", "dve", "act", "pool"):
            self.sems[e] = nc.alloc_semaphore("c_" + e)
            self.cnt[e] = 0
        self.dsem = {}
        self.dcnt = {}
        self.dnext = {}
        for q, n in self.NDMA.items():
            self.dnext[q] = 0
            for i in range(n):
                self.dsem[(q, i)] = nc.alloc_semaphore("d_%s%d" % (q, i))
                self.dcnt[(q, i)] = 0
        self.seen = {e: {} for e in self.E}
        self.psb = []
        self.psi = 0
        self.pctr = {}
        self.ninst = 0
        self.trace = {e: [] for e in self.E}

    def _sem(self, key):
        return self.sems[key] if key in self.sems else self.dsem[key]

    def _wait(self, eng, key, val):
        if val <= 0:
            return
        if self.seen[eng].get(key, 0) >= val:
            return
        self.E[eng].wait_ge(self._sem(key), val)
        self.trace[eng].append(("w", key, val))
        self.seen[eng][key] = val
        self.ninst += 1

    def _deps(self, eng, reads, writes):
        deps = []
        for b in reads:
            if b.w is not None:
                deps.append(b.w)
        for b in writes:
            if b.w is not None:
                deps.append(b.w)
            for k, (v, e) in b.r.items():
                deps.append((k, v, e))
        for k, v, e in deps:
            if e == "pe" and eng == "pe":
                continue
            self._wait(eng, k, v)

    def _commit(self, ev, reads, writes):
        for b in writes:
            b.w = ev
            b.r = {}
        for b in reads:
            b.r[ev[0]] = (ev[1], ev[2])

    def op(self, eng, fn, reads=(), writes=()):
        ex = [b for b in reads if b.excl]
        if ex:
            self._deps(eng, reads, list(writes) + ex)
            writes = list(writes) + ex
            reads = [b for b in reads if not b.excl]
        else:
            self._deps(eng, reads, writes)
        ins = fn(self.E[eng])
        self.cnt[eng] += 1
        ins.then_inc(self.sems[eng], 1)
        self.trace[eng].append(("i", eng, 1))
        self.ninst += 1
        self._commit((eng, self.cnt[eng], eng), reads, writes)

    def dma(self, q, out, in_, reads=(), writes=(), slow=False):
        self._deps(q, reads, writes)
        n = self.NDMA[q]
        s = self.dnext[q]
        self.dnext[q] = (s + 1) % n
        key = (q, s)
        self._wait(q, key, self.dcnt[key])
        if slow:
            ins = self.E[q].dma_start(out=out, in_=in_, allow_slow_non_contiguous=True)
        else:
            ins = self.E[q].dma_start(out=out, in_=in_)
        self.dcnt[key] += 16
        ins.then_inc(self.dsem[key], 16)
        self.trace[q].append(("i", key, 16))
        self.ninst += 1
        self._commit((key, self.dcnt[key], "dma_" + q), reads, writes)

    def barrier(self):
        for e in self.E:
            for k in self.sems:
                if k != e:
                    self._wait(e, k, self.cnt[k])
            for k in self.dsem:
                self._wait(e, k, self.dcnt[k])

    def finish(self):
        for k in self.dsem:
            self._wait("sp", k, self.dcnt[k])
        for k in self.sems:
            self._wait("sp", k, self.cnt[k])

    def simulate(self):
        pc = {e: 0 for e in self.E}
        val = {}
        prog = True
        while prog:
            prog = False
            for e in self.E:
                tr = self.trace[e]
                while pc[e] < len(tr):
                    k, key, v = tr[pc[e]]
                    if k == "w":
                        if val.get(key, 0) >= v:
                            pc[e] += 1
                            prog = True
                        else:
                            break
                    else:
                        val[key] = val.get(key, 0) + v
                        pc[e] += 1
                        prog = True
        stuck = {e: (pc[e], len(self.trace[e]), self.trace[e][pc[e]] if pc[e] < len(self.trace[e]) else None) for e in self.E}
        return all(pc[e] == len(self.trace[e]) for e in self.E), stuck

    def _psinit(self):
        if not self.psb:
            self.pst = [self.nc.alloc_psum_tensor("psp%d" % i, [128, 1024], F32) for i in range(4)]
            for i in range(8):
                self.psb.append((self.pst[i // 2][:, (i % 2) * 512:(i % 2 + 1) * 512], Buf("ps%d" % i, excl=True)))

    def psum(self):
        self._psinit()
        r = self.psb[self.psi]
        self.psi = (self.psi + 1) % 8
        return r

    def psum2(self, pairs=None, key=None):
        self._psinit()
        if pairs is not None:
            c = self.pctr.get(key, 0)
            self.pctr[key] = c + 1
            i = 2 * pairs[c % len(pairs)]
            return self.pst[i // 2][:, :], [self.psb[i][1], self.psb[i + 1][1]]
        if self.psi % 2:
            self.psi = (self.psi + 1) % 8
        i = self.psi
        self.psi = (self.psi + 2) % 8
        return self.pst[i // 2][:, :], [self.psb[i][1], self.psb[i + 1][1]]


_UID = [0]


def sbt(nc, name, shape, dt):
    _UID[0] += 1
    return nc.sbuf_tensor("%s_u%d" % (name, _UID[0]), shape, dt)


class Ring:
    def __init__(self, nc, stack, name, n, shape, dt):
        self.t = [stack.enter_context(sbt(nc, "%s%d" % (name, i), shape, dt)) for i in range(n)]
        self.b = [Buf("%s%d" % (name, i)) for i in range(n)]
        self.i = 0

    def get(self):
        r = (self.t[self.i], self.b[self.i])
        self.i = (self.i + 1) % len(self.t)
        return r


def tiles_of(n, step):
    return [(s, min(step, n - s)) for s in range(0, n, step)]


class Ctx:
    pass


def build(cfg):
    Tp, Ts = cfg["Tp"], cfg["Ts"]
    dbg = cfg.get("debug", ())
    stop_after = cfg.get("stop_after", None)
    NT = 2 * Tp + 2 * Ts
    seqs = [(0, Tp, True), (Tp, Tp, True), (2 * Tp, Ts, False), (2 * Tp + Ts, Ts, False)]
    nc = bass.Bass("TRN2", target_bir_lowering=False)
    g = Ctx()
    g.nc, g.cfg, g.NT, g.seqs, g.Tp, g.Ts = nc, cfg, NT, seqs, Tp, Ts
    p = Prog(nc)
    g.p = p

    def din(name, shape, dt=F32):
        return nc.dram_tensor(name, list(shape), dt, kind="ExternalInput").ap()

    def dout(name, shape, dt=F32):
        return nc.dram_tensor(name, list(shape), dt, kind="ExternalOutput").ap()

    def dscr(name, shape, dt):
        kind = "ExternalOutput" if name in dbg else "Internal"
        return nc.dram_tensor(name, list(shape), dt, kind=kind).ap()

    I = Ctx()
    g.I = I
    I.x = din("x_all", [NT, D])
    I.mC = din("st_mC", [2, HM, DK, DV])
    I.mn = din("st_mn", [2, HM, DK])
    I.mm = din("st_mm", [2, HM])
    I.conv = din("st_conv", [2, 3, D])
    I.lruh = din("st_lruh", [2, D])
    I.shift = din("st_shift", [2, D])
    I.S = din("st_S", [2, RH, RN, RN])
    for nm, shp in [("a_w_in", [D, INW]), ("a_b_ig", [1, HM]), ("a_b_fg", [1, HM]), ("a_mlstm_norm", [1, D]),
                    ("a_conv_w", [4, D]), ("a_conv_b", [1, D]), ("a_lru_wa", [16, 128, 128]), ("a_lru_ba", [1, D]),
                    ("a_lru_wx", [16, 128, 128]), ("a_lru_bx", [1, D]), ("a_lru_lambda", [1, D]),
                    ("a_w_out", [2 * D, D]), ("c_mu", [6, D]), ("c_w_r", [D, D]), ("c_w_k", [D, D]),
                    ("c_w_v", [D, D]), ("c_w0", [1, D]), ("c_w1", [D, 96]), ("c_w2", [96, D]), ("c_a0", [1, D]),
                    ("c_a1", [D, 96]), ("c_a2", [96, D]), ("c_g1", [D, 256]), ("c_g2", [256, D]),
                    ("c_k_k", [1, D]), ("c_k_a", [1, D]), ("c_r_k", [1, D]), ("c_gn_g", [1, D]),
                    ("c_gn_b", [1, D]), ("c_w_o", [D, D]), ("ln1_g", [2, D]), ("ln1_b", [2, D]),
                    ("ln2_g", [2, D]), ("ln2_b", [2, D]), ("mlp_w1", [2, D, DFF]), ("mlp_w2", [2, DFF, D])]:
        setattr(I, nm, din(nm, shp))
    I.ident = din("k_ident", [128, 128])
    I.tri = din("k_tri", [128, 128])
    I.ones = din("k_ones", [128, 128])
    I.blk = din("k_blk", [128, 128])

    O = Ctx()
    g.O = O
    O.y = dout("o_y", [NT, D])
    O.mC = dout("o_mC", [4, HM, DK, DV])
    O.mn = dout("o_mn", [4, HM, DK])
    O.mm = dout("o_mm", [4, HM])
    O.conv = dout("o_conv", [4, 3, D])
    O.lruh = dout("o_lruh", [4, D])
    O.shift = dout("o_shift", [4, D])
    O.S = dout("o_S", [4, RH, RN, RN])

    Sx = Ctx()
    g.Sx = Sx
    Sx.XT32 = dscr("s_xt32", [D, NT], F32)
    Sx.QT = dscr("s_qt", [1024, NT], BF16)
    Sx.KT = dscr("s_kt", [1024, NT], BF16)
    Sx.XR = dscr("s_xr", [D, NT], F32)
    Sx.YG = dscr("s_yg", [D, NT], BF16)
    Sx.VT = dscr("s_vtok", [NT, D], BF16)
    Sx.SO = dscr("s_sotok", [NT, D], BF16)
    Sx.KK = dscr("s_ktok", [NT, 1024], BF16)
    Sx.GT = dscr("s_gtok", [NT, 16], F32)
    Sx.CAT = dscr("s_cat", [2 * D, NT], BF16)
    Sx.X1 = dscr("s_x1", [D, NT], F32)
    for nm in ("RR", "KRAW", "VV", "DEC", "AA", "NKK", "BB", "K2", "BONUS"):
        setattr(Sx, nm, dscr("s_" + nm.lower(), [D, NT], F32))
    Sx.GG = dscr("s_gg", [D, NT], BF16)
    Sx.GAM = dscr("s_gam", [D, NT], F32)
    Sx.RR2 = dscr("s_rr2", [D, NT], F32)
    Sx.YT = dscr("s_yt", [NT, D], F32)
    Sx.YP = dscr("s_yp", [NT, D], F32)
    Sx.VTOK = dscr("s_vtok2", [NT, D], BF16)
    Sx.YG2 = dscr("s_yg2", [D, NT], BF16)

    with ExitStack() as gs:
        K = Ctx()
        g.K = K
        K.ident = gs.enter_context(nc.sbuf_tensor("ident", [128, 128], F32))
        K.tri = gs.enter_context(nc.sbuf_tensor("tri", [128, 128], F32))
        K.ones = gs.enter_context(nc.sbuf_tensor("ones", [128, 128], F32))
        K.blk = gs.enter_context(nc.sbuf_tensor("blk", [128, 128], F32))
        K.buf = Buf("consts")
        K.onesD = gs.enter_context(nc.sbuf_tensor("onesD", [128, 128], F32))
        p.op("dve", lambda e: e.memset(K.onesD[:, :], 1.0 / D), writes=[K.buf])
        K.eps_ln = gs.enter_context(nc.sbuf_tensor("eps_ln", [128, 1], F32))
        K.eps_gn = gs.enter_context(nc.sbuf_tensor("eps_gn", [128, 1], F32))
        p.op("dve", lambda e: e.memset(K.eps_ln[:, :], LN_EPS), writes=[K.buf])
        p.op("dve", lambda e: e.memset(K.eps_gn[:, :], GN_EPS), writes=[K.buf])
        for t, src in ((K.ident, I.ident), (K.tri, I.tri), (K.ones, I.ones), (K.blk, I.blk)):
            p.dma("sp", t[:], src[:, :], writes=[K.buf])
        p.barrier()
        if stop_after == "consts":
            p.finish()
            g_last[0] = p
            return nc

        load_params(g, gs)
        stage_inproj(g)
        p.barrier()
        if stop_after == "inproj":
            p.finish()
            g_last[0] = p
            return nc
        stage_mlstm(g)
        p.barrier()
        if stop_after == "mlstm":
            p.finish()
            g_last[0] = p
            return nc
        stage_lru(g)
        p.barrier()
        if stop_after == "lru":
            p.finish()
            g_last[0] = p
            return nc
        stage_tail(g, 0, Sx.CAT, 32, I.a_w_out, Sx.XT32)
        p.barrier()
        if stop_after == "tail0":
            p.finish()
            g_last[0] = p
            return nc
        stage_rwkv_proj(g)
        p.barrier()
        stage_rwkv_prep(g)
        p.barrier()
        if stop_after == "rprep":
            p.finish()
            g_last[0] = p
            return nc
        stage_rwkv_rec(g)
        p.barrier()
        stage_rwkv_post(g)
        p.barrier()
        if stop_after == "rpost":
            p.finish()
            g_last[0] = p
            return nc
        stage_tail(g, 1, Sx.YG2, 16, I.c_w_o, Sx.X1)
        p.barrier()
    p.finish()
    g_last[0] = p
    return nc


g_last = [None]


def fm(ap, c=128):
    return ap.rearrange("(c p) t -> p c t", p=c)


def stage_inproj(g):
    nc, p, I, Sx, K, NT = g.nc, g.p, g.I, g.Sx, g.K, g.NT
    with ExitStack() as st:
        xin = Ring(nc, st, "xin", 2, [128, D], F32)
        xt32 = Ring(nc, st, "xt32", 2, [128, NCH, 512], F32)
        xtb = Ring(nc, st, "xtb", 2, [128, NCH, 512], BF16)
        wf = Ring(nc, st, "wf", 2, [128, NCH, 512], BF16)
        wt = Ring(nc, st, "wt", 2, [128, NCH, 256], BF16)
        ob = Ring(nc, st, "ob", 4, [128, 512], BF16)
        of = Ring(nc, st, "of", 3, [128, 512], F32)
        og = Ring(nc, st, "og", 2, [128, 16], F32)
        win = I.a_w_in.rearrange("(kc p) n -> p kc n", p=128)
        XT32v = fm(Sx.XT32)
        for (t0, n) in tiles_of(NT, 512):
            x32, x32b = xt32.get()
            xb, xbb = xtb.get()
            for (s0, m) in tiles_of(n, 128):
                xi, xib = xin.get()
                p.dma("sp", xi[:m, :], I.x[t0 + s0:t0 + s0 + m, :], writes=[xib])
                for c4 in range(4):
                    ps, psb = p.psum()
                    for j in range(4):
                        c = c4 * 4 + j
                        p.op("pe", lambda e, c=c, j=j: e.transpose(ps[:, j * 128:j * 128 + m], xi[:m, c * 128:(c + 1) * 128],
                                                                K.ident[:m, :m]),
                             reads=[xib, K.buf], writes=[psb])
                    src = ps[:, :].rearrange("p (j t) -> p j t", j=4)[:, :, :m]
                    if "noact" not in g.cfg.get("flags", ""):
                        p.op("act", lambda e, c4=c4, src=src: e.copy(out=x32[:, c4 * 4:c4 * 4 + 4, s0:s0 + m], in_=src),
                             reads=[psb], writes=[x32b])
                    if "nodve" not in g.cfg.get("flags", ""):
                        p.op("dve", lambda e, c4=c4, src=src: e.tensor_copy(out=xb[:, c4 * 4:c4 * 4 + 4, s0:s0 + m], in_=src),
                             reads=[psb], writes=[xbb])
            if "nostore" in g.cfg.get("flags", ""):
                pass
            elif g.cfg.get("split_store", True):
                for c in range(NCH):
                    p.dma("sp", Sx.XT32[c * 128:(c + 1) * 128, t0:t0 + n], x32[:, c, :n], reads=[x32b])
            else:
                p.dma("sp", XT32v[:, :, t0:t0 + n], x32[:, :, :n], reads=[x32b])
            parts = g.cfg.get("parts", "tr,fm,tm")
            fm_jobs = ([("q", C_Q + 128 * i, i) for i in range(8)] + [("k", C_K + 128 * i, i) for i in range(8)] +
                       [("xr", C_XR + 128 * i, i) for i in range(16)] + [("yg", C_YG + 128 * i, i) for i in range(16)])
            for ji, (kind, col, ci) in enumerate(fm_jobs if "fm" in parts else []):
                if ji % 4 == 0:
                    w, wb = wf.get()
                    p.dma("pool", w[:], win[:, :, col:col + 512], writes=[wb])
                wo_ = (ji % 4) * 128
                ps, psb = p.psum()
                for kc in range(NCH):
                    p.op("pe", lambda e, kc=kc, wo_=wo_: e.matmul(ps[:, :n], lhsT=w[:, kc, wo_:wo_ + 128], rhs=xb[:, kc, :n],
                                                                  start=(kc == 0), stop=(kc == NCH - 1)),
                         reads=[wb, xbb], writes=[psb])
                if kind == "q":
                    o, obb = ob.get()
                    p.op("act", lambda e: e.activation(out=o[:, :n], in_=ps[:, :n], func=AF.Copy, scale=float(DK) ** -0.5),
                         reads=[psb], writes=[obb])
                    p.dma("sp", Sx.QT[ci * 128:(ci + 1) * 128, t0:t0 + n], o[:, :n], reads=[obb])
                elif kind == "k":
                    o, obb = ob.get()
                    p.op("dve", lambda e: e.tensor_copy(out=o[:, :n], in_=ps[:, :n]), reads=[psb], writes=[obb])
                    p.dma("sp", Sx.KT[ci * 128:(ci + 1) * 128, t0:t0 + n], o[:, :n], reads=[obb])
                elif kind == "xr":
                    o, obb = of.get()
                    p.op("dve", lambda e: e.tensor_copy(out=o[:, :n], in_=ps[:, :n]), reads=[psb], writes=[obb])
                    p.dma("sp", Sx.XR[ci * 128:(ci + 1) * 128, t0:t0 + n], o[:, :n], reads=[obb])
                else:
                    o, obb = ob.get()
                    p.op("act", lambda e: e.activation(out=o[:, :n], in_=ps[:, :n], func=AF.Gelu_apprx_tanh),
                         reads=[psb], writes=[obb])
                    p.dma("sp", Sx.YG[ci * 128:(ci + 1) * 128, t0:t0 + n], o[:, :n], reads=[obb])
            tm_jobs = ([("v", C_V + 256 * i, 256 * i, 256) for i in range(8)] +
                       [("o", C_O + 256 * i, 256 * i, 256) for i in range(8)] +
                       [("k", C_K + 256 * i, 256 * i, 256) for i in range(4)] + [("g", C_IG, 0, 16)])
            for kind, col, oc, nw in (tm_jobs if "tm" in parts else []):
                w, wb = wt.get()
                p.dma("pool", w[:, :, :nw], win[:, :, col:col + nw], writes=[wb])
                for (s0, m) in tiles_of(n, 128):
                    ps, psb = p.psum()
                    for kc in range(NCH):
                        p.op("pe", lambda e, kc=kc: e.matmul(ps[:m, :nw], lhsT=xb[:, kc, s0:s0 + m], rhs=w[:, kc, :nw],
                                                             start=(kc == 0), stop=(kc == NCH - 1)),
                             reads=[wb, xbb], writes=[psb])
                    r0 = t0 + s0
                    if kind == "g":
                        o, obb = og.get()
                        p.op("dve", lambda e: e.tensor_copy(out=o[:m, :], in_=ps[:m, :16]), reads=[psb], writes=[obb])
                        p.dma("sp", Sx.GT[r0:r0 + m, :], o[:m, :], reads=[obb])
                    elif kind == "o":
                        o, obb = ob.get()
                        p.op("act", lambda e: e.activation(out=o[:m, :nw], in_=ps[:m, :nw], func=AF.Sigmoid),
                             reads=[psb], writes=[obb])
                        p.dma("sp", Sx.SO[r0:r0 + m, oc:oc + nw], o[:m, :nw], reads=[obb])
                    else:
                        o, obb = ob.get()
                        p.op("dve", lambda e: e.tensor_copy(out=o[:m, :nw], in_=ps[:m, :nw]), reads=[psb], writes=[obb])
                        dst = Sx.VT if kind == "v" else Sx.KK
                        p.dma("sp", dst[r0:r0 + m, oc:oc + nw], o[:m, :nw], reads=[obb])


def stage_mlstm(g):
    nc, p, I, O, Sx, K = g.nc, g.p, g.I, g.O, g.Sx, g.K
    with ExitStack() as st:
        sb = lambda name, shape, dt=F32: st.enter_context(sbt(nc, name, shape, dt))
        Cn = sb("Cn", [128, HM, 257])
        Cnb = sb("Cnb", [128, HM, 257], BF16)
        CnB = [Buf("Cn%d" % h) for h in range(HM)]
        CnbB = [Buf("Cnb%d" % h) for h in range(HM)]
        mrun = sb("mrun", [8, 1])
        mrunB = Buf("mrun")
        mnbc = sb("mnbc", [128, D])
        bigbc = sb("bigbc", [128, 16])
        cB = Buf("mconst")
        p.dma("sp", mnbc[:], I.a_mlstm_norm[0:1, :].partition_broadcast(128), writes=[cB])
        p.dma("sp", bigbc[:, 0:8], I.a_b_ig[0:1, :].partition_broadcast(128), writes=[cB])
        p.dma("sp", bigbc[:, 8:16], I.a_b_fg[0:1, :].partition_broadcast(128), writes=[cB])
        em0 = sb("em0", [128, 8])
        em0B = Buf("em0")
        gtr = Ring(nc, st, "gt", 2, [128, 16], F32)
        vxr = Ring(nc, st, "vx", 2, [128, HM, 257], BF16)
        sor = Ring(nc, st, "so", 2, [128, D], BF16)
        ktr = Ring(nc, st, "ktk", 2, [128, 1024], BF16)
        qTr = Ring(nc, st, "qT", 2, [128, HM, 128], BF16)
        kTr = Ring(nc, st, "kT", 2, [128, HM, 128], BF16)
        catr = Ring(nc, st, "catT", 2, [128, 16, 128], BF16)
        for t_, b_ in zip(vxr.t, vxr.b):
            p.op("dve", lambda e, t_=t_: e.memset(t_[:, :, 256:257], 1.0), writes=[b_])
        gw = Ring(nc, st, "gw", 2, [128, 64], F32)
        g8r = Ring(nc, st, "g8", 2, [8, 8], F32)
        PTr = Ring(nc, st, "PT", 3, [128, 128], BF16)
        ksr = Ring(nc, st, "ks", 3, [128, 128], BF16)
        smr = Ring(nc, st, "sm", 4, [128, 16], F32)
        hhr = Ring(nc, st, "hh", 3, [128, 256], F32)
        hmr = Ring(nc, st, "hm", 3, [128, 256], F32)
        QTv = Sx.QT.rearrange("(h p) t -> p h t", p=128)
        KTv = Sx.KT.rearrange("(h p) t -> p h t", p=128)
        for si, (tok0, T, isp) in enumerate(g.seqs):
            L = 128 if T % 128 == 0 else T
            assert T % L == 0 and L <= 128
            if isp:
                for h in range(HM):
                    p.op("dve", lambda e, h=h: e.memset(Cn[:, h, :], 0.0), writes=[CnB[h]])
                    p.op("act", lambda e, h=h: e.copy(out=Cnb[:, h, :], in_=Cn[:, h, :]), reads=[CnB[h]], writes=[CnbB[h]])
                p.op("dve", lambda e: e.memset(mrun[:, :], 0.0), writes=[mrunB])
            else:
                b = si - 2
                p.dma("sp", em0[:, :], I.mm[b:b + 1, :].partition_broadcast(128), writes=[em0B])
                p.op("act", lambda e: e.activation(out=em0[:, :], in_=em0[:, :], func=AF.Exp), reads=[em0B], writes=[em0B])
                p.dma("sp", mrun[:, :], I.mm[b:b + 1, :].rearrange("o h -> h o"), writes=[mrunB])
                for h in range(HM):
                    p.dma("sp", Cn[:, h, 0:256], I.mC[b, h, :, :], writes=[CnB[h]])
                    p.dma("sp", Cn[:, h, 256:257], I.mn[b, h:h + 1, :].rearrange("o k -> k o"), writes=[CnB[h]])
                    p.op("dve", lambda e, h=h: e.tensor_scalar(out=Cn[:, h, :], in0=Cn[:, h, :], scalar1=em0[:, h:h + 1],
                                                                scalar2=None, op0=OP.mult), reads=[CnB[h], em0B], writes=[CnB[h]])
                    p.op("act", lambda e, h=h: e.copy(out=Cnb[:, h, :], in_=Cn[:, h, :]), reads=[CnB[h]], writes=[CnbB[h]])
            for c in range(T // L):
                r0 = tok0 + c * L
                gt, gtB = gtr.get()
                vx, vxB = vxr.get()
                so, soB = sor.get()
                kt, ktB = ktr.get()
                qT, qTB = qTr.get()
                kT, kTB = kTr.get()
                cat, catB = catr.get()
                p.dma("sp", gt[:L, :], Sx.GT[r0:r0 + L, :], writes=[gtB])
                p.dma("sp", vx[:L, :, 0:256], Sx.VT[r0:r0 + L, :].rearrange("t (h v) -> t h v", h=HM), writes=[vxB])
                p.dma("sp", so[:L, :], Sx.SO[r0:r0 + L, :], writes=[soB])
                p.dma("sp", kt[:L, :], Sx.KK[r0:r0 + L, :], writes=[ktB])
                p.dma("sp", qT[:, :, :L], QTv[:, :, r0:r0 + L], writes=[qTB])
                p.dma("sp", kT[:, :, :L], KTv[:, :, r0:r0 + L], writes=[kTB])
                w, wB = gw.get()
                p.op("dve", lambda e: e.tensor_tensor(out=w[:L, 0:16], in0=gt[:L, :], in1=bigbc[:L, :], op=OP.add),
                     reads=[gtB, cB], writes=[wB])
                p.op("act", lambda e: e.activation(out=w[:L, 8:16], in_=w[:L, 8:16], func=AF.Exp, scale=-1.0), reads=[wB], writes=[wB])
                p.op("act", lambda e: e.activation(out=w[:L, 8:16], in_=w[:L, 8:16], func=AF.Ln, bias=1.0), reads=[wB], writes=[wB])
                p.op("dve", lambda e: e.tensor_scalar(out=w[:L, 8:16], in0=w[:L, 8:16], scalar1=-1.0, scalar2=None, op0=OP.mult),
                     reads=[wB], writes=[wB])
                ps, psB = p.psum()
                p.op("pe", lambda e: e.matmul(ps[:L, 0:8], lhsT=K.tri[:L, :L], rhs=w[:L, 8:16], start=True, stop=True),
                     reads=[wB, K.buf], writes=[psB])
                p.op("pe", lambda e: e.matmul(ps[:, 8:16], lhsT=K.ones[:L, :], rhs=w[:L, 8:16], start=True, stop=True),
                     reads=[wB, K.buf], writes=[psB])
                p.op("pe", lambda e: e.matmul(ps[:8, 16:17], lhsT=w[:L, 8:16], rhs=K.ones[:L, 0:1], start=True, stop=True),
                     reads=[wB, K.buf], writes=[psB])
                p.op("dve", lambda e: e.tensor_tensor(out=w[:L, 24:32], in0=w[:L, 0:8], in1=ps[:L, 0:8], op=OP.subtract),
                     reads=[wB, psB], writes=[wB])
                p.op("act", lambda e: e.activation(out=w[:L, 32:40], in_=w[:L, 24:32], func=AF.Exp), reads=[wB], writes=[wB])
                p.op("act", lambda e: e.activation(out=w[:L, 40:48], in_=ps[:L, 0:8], func=AF.Exp, scale=-1.0), reads=[psB], writes=[wB])
                p.op("act", lambda e: e.activation(out=w[:, 48:56], in_=ps[:, 8:16], func=AF.Exp), reads=[psB], writes=[wB])
                g8, g8B = g8r.get()
                p.op("dve", lambda e: e.tensor_copy(out=g8[:, 0:1], in_=ps[:8, 16:17]), reads=[psB], writes=[g8B])
                ps2, ps2B = p.psum()
                p.op("pe", lambda e: e.transpose(ps2[:8, :L], w[:L, 24:32], K.ident[:L, :L]), reads=[wB, K.buf], writes=[ps2B])
                p.op("dve", lambda e: e.reduce_max(out=g8[:, 1:2], in_=ps2[:8, :L], axis=AX.X), reads=[ps2B], writes=[g8B])
                p.op("dve", lambda e: e.tensor_tensor(out=g8[:, 2:3], in0=g8[:, 1:2], in1=mrun[:, :], op=OP.max),
                     reads=[g8B, mrunB], writes=[g8B])
                p.op("dve", lambda e: e.tensor_tensor(out=mrun[:, :], in0=g8[:, 2:3], in1=g8[:, 0:1], op=OP.add),
                     reads=[g8B], writes=[mrunB])
                for h in range(HM):
                    pS, pSB = p.psum()
                    p.op("pe", lambda e: e.matmul(pS[:L, :L], lhsT=kT[:, h, :L], rhs=qT[:, h, :L], start=True, stop=True),
                         reads=[kTB, qTB], writes=[pSB])
                    PT, PTB = PTr.get()
                    p.op("dve", lambda e: e.scalar_tensor_tensor(out=PT[:L, :L], in0=pS[:L, :L], scalar=w[:L, 32 + h:33 + h],
                                                                 in1=K.tri[:L, :L], op0=OP.mult, op1=OP.mult),
                         reads=[pSB, wB, K.buf], writes=[PTB])
                    pN, pNB = p.psum()
                    p.op("pe", lambda e: e.matmul(pN[:L, 0:257], lhsT=PT[:L, :L], rhs=vx[:L, h, :], start=True, stop=False),
                         reads=[PTB, vxB], writes=[pNB])
                    p.op("pe", lambda e: e.matmul(pN[:L, 0:257], lhsT=qT[:, h, :L], rhs=Cnb[:, h, :], start=False, stop=True),
                         reads=[qTB, CnbB[h]], writes=[pNB])
                    ks, ksB = ksr.get()
                    p.op("dve", lambda e: e.tensor_scalar(out=ks[:L, :], in0=kt[:L, h * 128:(h + 1) * 128],
                                                          scalar1=w[:L, 32 + h:33 + h], scalar2=None, op0=OP.mult),
                         reads=[ktB, wB], writes=[ksB])
                    pC, pCB = p.psum()
                    p.op("pe", lambda e: e.matmul(pC[:, 0:257], lhsT=ks[:L, :], rhs=vx[:L, h, :], start=True, stop=True),
                         reads=[ksB, vxB], writes=[pCB])
                    p.op("dve", lambda e: e.tensor_scalar(out=Cn[:, h, :], in0=Cn[:, h, :], scalar1=w[:, 48 + h:49 + h],
                                                          scalar2=None, op0=OP.mult), reads=[CnB[h], wB], writes=[CnB[h]])
                    p.op("dve", lambda e: e.scalar_tensor_tensor(out=Cn[:, h, :], in0=pC[:, 0:257], scalar=w[:, 48 + h:49 + h],
                                                                 in1=Cn[:, h, :], op0=OP.mult, op1=OP.add),
                         reads=[pCB, wB, CnB[h]], writes=[CnB[h]])
                    p.op("act", lambda e: e.copy(out=Cnb[:, h, :], in_=Cn[:, h, :]), reads=[CnB[h]], writes=[CnbB[h]])
                    sm, smB = smr.get()
                    p.op("act", lambda e: e.activation(out=sm[:L, 13:14], in_=pN[:L, 256:257], func=AF.Abs),
                         reads=[pNB], writes=[smB])
                    p.op("dve", lambda e: e.tensor_tensor(out=sm[:L, 0:1], in0=sm[:L, 13:14], in1=w[:L, 40 + h:41 + h], op=OP.max),
                         reads=[smB, wB], writes=[smB])
                    p.op("dve", lambda e: e.reciprocal(out=sm[:L, 1:2], in_=sm[:L, 0:1]), reads=[smB], writes=[smB])
                    hh, hhB = hhr.get()
                    p.op("act", lambda e: e.activation(out=hh[:L, :], in_=pN[:L, 0:256], func=AF.Copy, scale=sm[:L, 1:2]),
                         reads=[pNB, smB], writes=[hhB])
                    p.op("dve", lambda e: e.bn_stats(out=sm[:L, 2:8], in_=hh[:L, :]), reads=[hhB], writes=[smB])
                    p.op("dve", lambda e: e.bn_aggr(out=sm[:L, 8:10], in_=sm[:L, 2:8]), reads=[smB], writes=[smB])
                    p.op("act", lambda e: e.activation(out=sm[:L, 10:11], in_=sm[:L, 9:10], func=AF.Sqrt, bias=g.K.eps_ln[:L, :]),
                         reads=[smB, K.buf], writes=[smB])
                    p.op("dve", lambda e: e.reciprocal(out=sm[:L, 11:12], in_=sm[:L, 10:11]), reads=[smB], writes=[smB])
                    p.op("dve", lambda e: e.scalar_tensor_tensor(out=sm[:L, 12:13], in0=sm[:L, 8:9], scalar=-1.0,
                                                                 in1=sm[:L, 11:12], op0=OP.mult, op1=OP.mult),
                         reads=[smB], writes=[smB])
                    hm, hmB = hmr.get()
                    p.op("act", lambda e: e.activation(out=hm[:L, :], in_=hh[:L, :], func=AF.Identity,
                                                       scale=sm[:L, 11:12], bias=sm[:L, 12:13]),
                         reads=[hhB, smB], writes=[hmB])
                    p.op("dve", lambda e: e.tensor_tensor(out=hm[:L, :], in0=hm[:L, :], in1=mnbc[:L, h * 256:(h + 1) * 256], op=OP.mult),
                         reads=[hmB, cB], writes=[hmB])
                    p.op("dve", lambda e: e.tensor_tensor(out=hm[:L, :], in0=hm[:L, :], in1=so[:L, h * 256:(h + 1) * 256], op=OP.mult),
                         reads=[hmB, soB], writes=[hmB])
                    pT, pTB = p.psum()
                    for j in range(2):
                        p.op("pe", lambda e, j=j: e.transpose(pT[:, j * 128:j * 128 + L], hm[:L, j * 128:(j + 1) * 128], K.ident[:L, :L]),
                             reads=[hmB, K.buf], writes=[pTB])
                    p.op("act", lambda e: e.copy(out=cat[:, 2 * h:2 * h + 2, :L],
                                                 in_=pT[:, 0:256].rearrange("p (j t) -> p j t", j=2)[:, :, :L]),
                         reads=[pTB], writes=[catB])
                p.dma("sp", fm(Sx.CAT)[:, 0:16, r0:r0 + L], cat[:, :, :L], reads=[catB])
            oi = si
            g8, g8B = g8r.get()
            p.op("dve", lambda e: e.tensor_scalar(out=g8[:, 0:8], in0=K.ident[:8, :8], scalar1=mrun[:, 0:1], scalar2=None, op0=OP.mult),
                 reads=[mrunB, K.buf], writes=[g8B])
            ps, psB = p.psum()
            p.op("pe", lambda e: e.matmul(ps[:, 0:8], lhsT=K.ones[:8, :], rhs=g8[:, 0:8], start=True, stop=True),
                 reads=[g8B, K.buf], writes=[psB])
            p.op("act", lambda e: e.activation(out=em0[:, :], in_=ps[:, 0:8], func=AF.Exp, scale=-1.0), reads=[psB], writes=[em0B])
            p.dma("sp", O.mm[oi:oi + 1, :].rearrange("o h -> h o"), mrun[:, :], reads=[mrunB])
            for h in range(HM):
                p.op("dve", lambda e, h=h: e.tensor_scalar(out=Cn[:, h, :], in0=Cn[:, h, :], scalar1=em0[:, h:h + 1], scalar2=None,
                                                            op0=OP.mult), reads=[CnB[h], em0B], writes=[CnB[h]])
                p.dma("sp", O.mC[oi, h, :, :], Cn[:, h, 0:256], reads=[CnB[h]])
                p.dma("sp", O.mn[oi, h:h + 1, :].rearrange("o k -> k o"), Cn[:, h, 256:257], reads=[CnB[h]])


PARAMS = ["conv_w0", "conv_w1", "conv_w2", "conv_w3", "conv_b", "lru_ba", "lru_bx", "lru_lam",
          "ln1_g0", "ln1_b0", "ln2_g0", "ln2_b0",
          "mu0", "mu1", "mu2", "mu3", "mu4", "mu5", "w0", "a0", "k_k", "k_a", "r_k", "gn_g", "gn_b",
          "ln1_g1", "ln1_b1", "ln2_g1", "ln2_b1", "shift0", "shift1"]


def load_params(g, stack):
    nc, p, I, K = g.nc, g.p, g.I, g.K
    src = {"conv_w0": I.a_conv_w[0:1, :], "conv_w1": I.a_conv_w[1:2, :], "conv_w2": I.a_conv_w[2:3, :],
           "conv_w3": I.a_conv_w[3:4, :], "conv_b": I.a_conv_b, "lru_ba": I.a_lru_ba, "lru_bx": I.a_lru_bx,
           "lru_lam": I.a_lru_lambda, "ln1_g0": I.ln1_g[0:1, :], "ln1_b0": I.ln1_b[0:1, :], "ln2_g0": I.ln2_g[0:1, :],
           "ln2_b0": I.ln2_b[0:1, :], "w0": I.c_w0, "a0": I.c_a0, "k_k": I.c_k_k, "k_a": I.c_k_a, "r_k": I.c_r_k,
           "gn_g": I.c_gn_g, "gn_b": I.c_gn_b, "ln1_g1": I.ln1_g[1:2, :], "ln1_b1": I.ln1_b[1:2, :],
           "ln2_g1": I.ln2_g[1:2, :], "ln2_b1": I.ln2_b[1:2, :]}
    for j in range(6):
        src["mu%d" % j] = I.c_mu[j:j + 1, :]
    src["shift0"] = I.shift[0:1, :]
    src["shift1"] = I.shift[1:2, :]
    n = len(PARAMS)
    PC = stack.enter_context(nc.sbuf_tensor("PC", [128, n * 16], F32))
    PCB = Buf("PC")
    with ExitStack() as st:
        rows = Ring(nc, st, "prow", 2, [128, 128], F32)
        for g0 in range(0, n, 8):
            names = PARAMS[g0:g0 + 8]
            rt, rb = rows.get()
            for i, nm in enumerate(names):
                p.dma("sp", rt[i * 16:(i + 1) * 16, :], src[nm].rearrange("o (c q) -> (o c) q", q=128), writes=[rb])
            R = len(names) * 16
            ps, psB = p.psum()
            p.op("pe", lambda e: e.transpose(ps[:, :R], rt[:R, :], K.ident[:R, :R]), reads=[rb, K.buf], writes=[psB])
            p.op("dve", lambda e: e.tensor_copy(out=PC[:, g0 * 16:g0 * 16 + R], in_=ps[:, :R]), reads=[psB], writes=[PCB])
        p.barrier()
    g.PC, g.PCB = PC, PCB
    g.pcol = lambda name, c: PC[:, PARAMS.index(name) * 16 + c:PARAMS.index(name) * 16 + c + 1]


def stage_lru(g):
    nc, p, I, O, Sx, K = g.nc, g.p, g.I, g.O, g.Sx, g.K
    pcol, PCB = g.pcol, g.PCB
    Tmax = max(T for _, T, _ in g.seqs)
    with ExitStack() as st:
        sb = lambda name, shape, dt=F32: st.enter_context(sbt(nc, name, shape, dt))
        wa = sb("wa", [128, 16, 128], BF16)
        wx = sb("wx", [128, 16, 128], BF16)
        wB = Buf("lruw")
        p.dma("pool", wa[:], I.a_lru_wa.rearrange("g i j -> i g j"), writes=[wB])
        p.dma("pool", wx[:], I.a_lru_wx.rearrange("g i j -> i g j"), writes=[wB])
        cl = sb("cl", [128, 32])
        clB = Buf("cl")
        lam = g.PC[:, PARAMS.index("lru_lam") * 16:PARAMS.index("lru_lam") * 16 + 16]
        p.op("act", lambda e: e.activation(out=cl[:, 0:16], in_=lam, func=AF.Exp, scale=-1.0), reads=[PCB], writes=[clB])
        p.op("act", lambda e: e.activation(out=cl[:, 0:16], in_=cl[:, 0:16], func=AF.Ln, bias=1.0), reads=[clB], writes=[clB])
        p.op("dve", lambda e: e.tensor_scalar(out=cl[:, 16:32], in0=cl[:, 0:16], scalar1=-16.0, scalar2=None, op0=OP.mult),
             reads=[clB], writes=[clB])
        p.op("dve", lambda e: e.tensor_scalar(out=cl[:, 0:16], in0=cl[:, 0:16], scalar1=-8.0, scalar2=None, op0=OP.mult),
             reads=[clB], writes=[clB])
        xpr = Ring(nc, st, "xp", 2, [128, Tmax + 3], F32)
        ygr = Ring(nc, st, "ygl", 2, [128, Tmax], BF16)
        xcr = Ring(nc, st, "xc", 2, [128, Tmax], F32)
        xcbr = Ring(nc, st, "xcb", 2, [128, Tmax], BF16)
        rr = Ring(nc, st, "rr", 2, [128, Tmax], F32)
        gir = Ring(nc, st, "gi", 2, [128, Tmax], F32)
        mr = Ring(nc, st, "ml", 2, [128, Tmax], F32)
        hsr = Ring(nc, st, "hs", 2, [128, Tmax], F32)
        ybr = Ring(nc, st, "yb", 2, [128, Tmax], BF16)
        h0r = Ring(nc, st, "h0", 2, [128, 1], F32)
        for si, (tok0, T, isp) in enumerate(g.seqs):
            for cc in range(NCH):
                rows = slice(cc * 128, (cc + 1) * 128)
                xp, xpB = xpr.get()
                yg, ygB = ygr.get()
                if isp:
                    p.op("dve", lambda e: e.memset(xp[:, 0:3], 0.0), writes=[xpB])
                else:
                    p.dma("sp", xp[:, 0:3], I.conv[si - 2, :, rows].rearrange("j c -> c j"), writes=[xpB], slow=True)
                p.dma("sp", xp[:, 3:3 + T], Sx.XR[rows, tok0:tok0 + T], writes=[xpB])
                p.dma("sp", yg[:, :T], Sx.YG[rows, tok0:tok0 + T], writes=[ygB])
                xc, xcB = xcr.get()
                p.op("dve", lambda e: e.tensor_scalar(out=xc[:, :T], in0=xp[:, 3:3 + T], scalar1=pcol("conv_w3", cc),
                                                      scalar2=pcol("conv_b", cc), op0=OP.mult, op1=OP.add),
                     reads=[xpB, PCB], writes=[xcB])
                for j in range(3):
                    p.op("dve", lambda e, j=j: e.scalar_tensor_tensor(out=xc[:, :T], in0=xp[:, j:j + T], scalar=pcol("conv_w%d" % j, cc),
                                                                      in1=xc[:, :T], op0=OP.mult, op1=OP.add),
                         reads=[xpB, PCB, xcB], writes=[xcB])
                xcb, xcbB = xcbr.get()
                p.op("act", lambda e: e.copy(out=xcb[:, :T], in_=xc[:, :T]), reads=[xcB], writes=[xcbB])
                r, rB = rr.get()
                gi, giB = gir.get()
                for (c0, n) in tiles_of(T, 512):
                    ps, psB = p.psum()
                    p.op("pe", lambda e: e.matmul(ps[:, :n], lhsT=wa[:, cc, :], rhs=xcb[:, c0:c0 + n], start=True, stop=True),
                         reads=[wB, xcbB], writes=[psB])
                    p.op("act", lambda e: e.activation(out=r[:, c0:c0 + n], in_=ps[:, :n], func=AF.Sigmoid, bias=pcol("lru_ba", cc)),
                         reads=[psB, PCB], writes=[rB])
                    ps2, ps2B = p.psum()
                    p.op("pe", lambda e: e.matmul(ps2[:, :n], lhsT=wx[:, cc, :], rhs=xcb[:, c0:c0 + n], start=True, stop=True),
                         reads=[wB, xcbB], writes=[ps2B])
                    p.op("act", lambda e: e.activation(out=gi[:, c0:c0 + n], in_=ps2[:, :n], func=AF.Sigmoid, bias=pcol("lru_bx", cc)),
                         reads=[ps2B, PCB], writes=[giB])
                ml, mlB = mr.get()
                p.op("act", lambda e: e.activation(out=ml[:, :T], in_=r[:, :T], func=AF.Exp, scale=cl[:, 16 + cc:17 + cc]),
                     reads=[rB, clB], writes=[mlB])
                p.op("act", lambda e: e.activation(out=r[:, :T], in_=r[:, :T], func=AF.Exp, scale=cl[:, cc:cc + 1]),
                     reads=[rB, clB], writes=[rB])
                p.op("dve", lambda e: e.tensor_scalar(out=ml[:, :T], in0=ml[:, :T], scalar1=-1.0, scalar2=1.0, op0=OP.mult, op1=OP.add),
                     reads=[mlB], writes=[mlB])
                p.op("act", lambda e: e.activation(out=ml[:, :T], in_=ml[:, :T], func=AF.Sqrt), reads=[mlB], writes=[mlB])
                if isp:
                    p.op("dve", lambda e: e.memset(ml[:, 0:1], 1.0), reads=[mlB], writes=[mlB])
                p.op("dve", lambda e: e.tensor_tensor(out=gi[:, :T], in0=gi[:, :T], in1=xc[:, :T], op=OP.mult),
                     reads=[giB, xcB], writes=[giB])
                p.op("dve", lambda e: e.tensor_tensor(out=gi[:, :T], in0=gi[:, :T], in1=ml[:, :T], op=OP.mult),
                     reads=[giB, mlB], writes=[giB])
                hs, hsB = hsr.get()
                if isp:
                    p.op("dve", lambda e: e.tensor_tensor_scan(out=hs[:, :T], data0=r[:, :T], data1=gi[:, :T], initial=0.0,
                                                               op0=OP.mult, op1=OP.add), reads=[rB, giB], writes=[hsB])
                else:
                    h0, h0B = h0r.get()
                    p.dma("sp", h0[:, :], I.lruh[si - 2:si - 1, rows].rearrange("o c -> c o"), writes=[h0B], slow=True)
                    p.op("dve", lambda e: e.tensor_tensor_scan(out=hs[:, :T], data0=r[:, :T], data1=gi[:, :T], initial=h0[:, 0:1],
                                                               op0=OP.mult, op1=OP.add), reads=[rB, giB, h0B], writes=[hsB])
                yb, ybB = ybr.get()
                p.op("dve", lambda e: e.tensor_tensor(out=yb[:, :T], in0=hs[:, :T], in1=yg[:, :T], op=OP.mult),
                     reads=[hsB, ygB], writes=[ybB])
                p.dma("sp", Sx.CAT[D + cc * 128:D + (cc + 1) * 128, tok0:tok0 + T], yb[:, :T], reads=[ybB])
                p.dma("sp", O.lruh[si:si + 1, rows].rearrange("o c -> c o"), hs[:, T - 1:T], reads=[hsB], slow=True)
                p.dma("sp", O.conv[si, :, rows].rearrange("j c -> c j"), xp[:, T:T + 3], reads=[xpB], slow=True)


def emit_ln(g, R, z, zB, n, gname, bname, xb=None, xbB=None):
    nc, p, K = g.nc, g.p, g.K
    pcol, PCB = g.pcol, g.PCB
    psM, psMB = p.psum()
    psQ, psQB = p.psum()
    for c in range(NCH):
        sq, sqB = R["sq"].get()
        p.op("act", lambda e, c=c: e.activation(out=sq[:, :n], in_=z[:, c, :n], func=AF.Square), reads=[zB], writes=[sqB])
        p.op("pe", lambda e, c=c: e.matmul(psM[:, :n], lhsT=K.onesD[:, :], rhs=z[:, c, :n], start=(c == 0), stop=(c == NCH - 1)),
             reads=[zB, K.buf], writes=[psMB])
        p.op("pe", lambda e, c=c: e.matmul(psQ[:, :n], lhsT=K.onesD[:, :], rhs=sq[:, :n], start=(c == 0), stop=(c == NCH - 1)),
             reads=[sqB, K.buf], writes=[psQB])
    mean, meanB = R["st"].get()
    rstd, rstdB = R["st"].get()
    p.op("act", lambda e: e.copy(out=mean[:, :n], in_=psM[:, :n]), reads=[psMB], writes=[meanB])
    p.op("dve", lambda e: e.tensor_tensor(out=rstd[:, :n], in0=mean[:, :n], in1=mean[:, :n], op=OP.mult), reads=[meanB], writes=[rstdB])
    p.op("dve", lambda e: e.tensor_tensor(out=rstd[:, :n], in0=psQ[:, :n], in1=rstd[:, :n], op=OP.subtract), reads=[psQB, rstdB], writes=[rstdB])
    p.op("act", lambda e: e.activation(out=rstd[:, :n], in_=rstd[:, :n], func=AF.Sqrt, bias=K.eps_ln[:, :]), reads=[rstdB, K.buf], writes=[rstdB])
    p.op("dve", lambda e: e.reciprocal(out=rstd[:, :n], in_=rstd[:, :n]), reads=[rstdB], writes=[rstdB])
    for c in range(NCH):
        p.op("dve", lambda e, c=c: e.tensor_tensor(out=z[:, c, :n], in0=z[:, c, :n], in1=mean[:, :n], op=OP.subtract),
             reads=[zB, meanB], writes=[zB])
        p.op("dve", lambda e, c=c: e.tensor_tensor(out=z[:, c, :n], in0=z[:, c, :n], in1=rstd[:, :n], op=OP.mult),
             reads=[zB, rstdB], writes=[zB])
        p.op("act", lambda e, c=c: e.activation(out=z[:, c, :n], in_=z[:, c, :n], func=AF.Identity, scale=pcol(gname, c), bias=pcol(bname, c)),
             reads=[zB, PCB], writes=[zB])
        if xb is not None:
            p.op("dve", lambda e, c=c: e.tensor_copy(out=xb[:, c, :n], in_=z[:, c, :n]), reads=[zB], writes=[xbB])


def stage_tail(g, layer, XinT, Kc, W, res_in):
    nc, p, I, O, Sx, K = g.nc, g.p, g.I, g.O, g.Sx, g.K
    NT = g.NT
    with ExitStack() as st:
        sb = lambda name, shape, dt=F32: st.enter_context(sbt(nc, name, shape, dt))
        big = sb("big", [128, 32, 512], BF16)
        bigB = Buf("big")
        z = sb("z", [128, NCH, 512])
        zB = Buf("z")
        xb = sb("x1b", [128, NCH, 512], BF16)
        xbB = Buf("x1b")
        wo = Ring(nc, st, "wo", 2, [128, Kc, 128], BF16)
        w1r = Ring(nc, st, "w1", 2, [128, NCH, 512], BF16)
        w2r = Ring(nc, st, "w2", 2, [128, 32, 256], BF16)
        R = {"sq": Ring(nc, st, "sq", 2, [128, 512], F32), "st": Ring(nc, st, "lnst", 4, [128, 512], F32)}
        rsr = Ring(nc, st, "rs", 2, [128, 512], F32)
        rlr = Ring(nc, st, "rl", 2, [128, 512], F32)
        ytr = Ring(nc, st, "ytk", 1, [128, D], F32)
        Wv = W.rearrange("(kc p) n -> p kc n", p=128)
        W1v = I.mlp_w1[layer].rearrange("(kc p) n -> p kc n", p=128)
        W2v = I.mlp_w2[layer].rearrange("(kc p) n -> p kc n", p=128)
        Xv = fm(XinT)
        sfx = str(layer)
        for (t0, n) in tiles_of(NT, 512):
            p.dma("sp", big[:, :Kc, :n], Xv[:, :, t0:t0 + n], writes=[bigB])
            for c in range(NCH):
                w, wB = wo.get()
                p.dma("pool", w[:], Wv[:, :, c * 128:(c + 1) * 128], writes=[wB])
                rs, rsB = rsr.get()
                p.dma("sp", rs[:, :n], res_in[c * 128:(c + 1) * 128, t0:t0 + n], writes=[rsB])
                ps, psB = p.psum()
                for kc in range(Kc):
                    p.op("pe", lambda e, kc=kc: e.matmul(ps[:, :n], lhsT=w[:, kc, :], rhs=big[:, kc, :n], start=(kc == 0), stop=(kc == Kc - 1)),
                         reads=[wB, bigB], writes=[psB])
                p.op("dve", lambda e, c=c: e.scalar_tensor_tensor(out=z[:, c, :n], in0=rs[:, :n], scalar=ALPHA, in1=ps[:, :n],
                                                                  op0=OP.mult, op1=OP.add), reads=[rsB, psB], writes=[zB])
            emit_ln(g, R, z, zB, n, "ln1_g" + sfx, "ln1_b" + sfx, xb, xbB)
            for half in range(2):
                for hc in range(32):
                    col = (half * 32 + hc) * 128
                    if hc % 4 == 0:
                        w, wB = w1r.get()
                        p.dma("pool", w[:], W1v[:, :, col:col + 512], writes=[wB])
                    wo_ = (hc % 4) * 128
                    ps, psB = p.psum()
                    for kc in range(NCH):
                        p.op("pe", lambda e, kc=kc, wo_=wo_: e.matmul(ps[:, :n], lhsT=w[:, kc, wo_:wo_ + 128], rhs=xb[:, kc, :n], start=(kc == 0), stop=(kc == NCH - 1)),
                             reads=[wB, xbB], writes=[psB])
                    rl, rlB = rlr.get()
                    p.op("act", lambda e: e.activation(out=rl[:, :n], in_=ps[:, :n], func=AF.Relu), reads=[psB], writes=[rlB])
                    p.op("dve", lambda e, hc=hc: e.tensor_tensor(out=big[:, hc, :n], in0=rl[:, :n], in1=rl[:, :n], op=OP.mult),
                         reads=[rlB], writes=[bigB])
                for blk in range(8):
                    w, wB = w2r.get()
                    p.dma("pool", w[:], W2v[:, half * 32:(half + 1) * 32, blk * 256:(blk + 1) * 256], writes=[wB])
                    for j in range(2):
                        c = blk * 2 + j
                        ps, psB = p.psum()
                        for kc in range(32):
                            p.op("pe", lambda e, kc=kc, j=j: e.matmul(ps[:, :n], lhsT=w[:, kc, j * 128:(j + 1) * 128], rhs=big[:, kc, :n],
                                                                      start=(kc == 0), stop=(kc == 31)), reads=[wB, bigB], writes=[psB])
                        if half == 0:
                            p.op("dve", lambda e, c=c: e.scalar_tensor_tensor(out=z[:, c, :n], in0=z[:, c, :n], scalar=ALPHA, in1=ps[:, :n],
                                                                              op0=OP.mult, op1=OP.add), reads=[psB, zB], writes=[zB])
                        else:
                            p.op("dve", lambda e, c=c: e.tensor_tensor(out=z[:, c, :n], in0=z[:, c, :n], in1=ps[:, :n], op=OP.add),
                                 reads=[psB, zB], writes=[zB])
            emit_ln(g, R, z, zB, n, "ln2_g" + sfx, "ln2_b" + sfx)
            if layer == 0:
                for c in range(NCH):
                    p.dma("sp", Sx.X1[c * 128:(c + 1) * 128, t0:t0 + n], z[:, c, :n], reads=[zB])
            else:
                for (s0, m) in tiles_of(n, 128):
                    yt, ytB = ytr.get()
                    for c4 in range(4):
                        ps, psB = p.psum()
                        for j in range(4):
                            c = c4 * 4 + j
                            p.op("pe", lambda e, c=c, j=j: e.transpose(ps[:m, j * 128:(j + 1) * 128], z[:, c, s0:s0 + m], K.ident[:, :]),
                                 reads=[zB, K.buf], writes=[psB])
                        p.op("act", lambda e, c4=c4: e.copy(out=yt[:m, c4 * 512:(c4 + 1) * 512], in_=ps[:m, :]), reads=[psB], writes=[ytB])
                    p.dma("sp", O.y[t0 + s0:t0 + s0 + m, :], yt[:m, :], reads=[ytB])


def stage_rwkv_proj(g):
    nc, p, I, O, Sx, K = g.nc, g.p, g.I, g.O, g.Sx, g.K
    pcol, PCB, PC = g.pcol, g.PCB, g.PC
    NT = g.NT
    with ExitStack() as st:
        sb = lambda name, shape, dt=F32: st.enter_context(sbt(nc, name, shape, dt))
        X = sb("X", [128, NCH, 512])
        XB = Buf("X")
        XX = sb("XX", [128, NCH, 512])
        XXB = Buf("XX")
        xmr = Ring(nc, st, "xm", 2, [128, NCH, 512], BF16)
        wr = Ring(nc, st, "wq", 2, [128, NCH, 512], BF16)
        ofr = Ring(nc, st, "of", 3, [128, 512], F32)
        obr = Ring(nc, st, "ob", 2, [128, 512], BF16)
        lor = Ring(nc, st, "lo", 2, [128, 2, 512], BF16)
        w1 = sb("lw1", [128, NCH, 96], BF16)
        w2 = sb("lw2", [96, D], BF16)
        a1 = sb("la1", [128, NCH, 96], BF16)
        a2 = sb("la2", [96, D], BF16)
        g1 = sb("lg1", [128, NCH, 256], BF16)
        g2 = sb("lg2", [128, 2, D], BF16)
        lB = Buf("lora")
        kc = lambda ap: ap.rearrange("(kc p) n -> p kc n", p=128)
        p.dma("pool", w1[:], kc(I.c_w1), writes=[lB])
        p.dma("pool", w2[:], I.c_w2[:, :], writes=[lB])
        p.dma("pool", a1[:], kc(I.c_a1), writes=[lB])
        p.dma("pool", a2[:], I.c_a2[:, :], writes=[lB])
        p.dma("pool", g1[:], kc(I.c_g1), writes=[lB])
        p.dma("pool", g2[:], kc(I.c_g2), writes=[lB])
        X1v = fm(Sx.X1)
        starts = {tok0: (si, isp) for si, (tok0, T, isp) in enumerate(g.seqs)}
        ends = {tok0 + T - 1: si for si, (tok0, T, isp) in enumerate(g.seqs)}
        zc = sb("zc", [128, NCH])
        zB = Buf("zc")
        p.op("dve", lambda e: e.memset(zc[:, :], 0.0), writes=[zB])
        for (t0, n) in tiles_of(NT, 512):
            p.dma("sp", X[:, :, :n], X1v[:, :, t0:t0 + n], writes=[XB])
            if t0 > 0:
                p.dma("sp", XX[:, :, :n], X1v[:, :, t0 - 1:t0 + n - 1], writes=[XXB])
            else:
                p.dma("sp", XX[:, :, 1:n], X1v[:, :, 0:n - 1], writes=[XXB])
            for ts, (si, isp) in starts.items():
                if t0 <= ts < t0 + n:
                    if isp:
                        src = zc[:, :]
                        rd = [zB]
                    else:
                        i0 = PARAMS.index("shift%d" % (si - 2)) * 16
                        src = PC[:, i0:i0 + 16]
                        rd = [PCB]
                    p.op("dve", lambda e, ts=ts, src=src: e.tensor_copy(out=XX[:, :, ts - t0], in_=src), reads=rd, writes=[XXB])
            for te, si in ends.items():
                if t0 <= te < t0 + n:
                    p.dma("sp", O.shift[si:si + 1, :].rearrange("o (c q) -> q (o c)", q=128), X[:, :, te - t0], reads=[XB], slow=True)
            p.op("dve", lambda e: e.tensor_tensor(out=XX[:, :, :n], in0=XX[:, :, :n], in1=X[:, :, :n], op=OP.subtract),
                 reads=[XXB, XB], writes=[XXB])

            def mix(j):
                xm, xmB = xmr.get()
                for c in range(NCH):
                    p.op("dve", lambda e, c=c: e.scalar_tensor_tensor(out=xm[:, c, :n], in0=XX[:, c, :n], scalar=pcol("mu%d" % j, c),
                                                                      in1=X[:, c, :n], op0=OP.mult, op1=OP.add),
                         reads=[XXB, XB, PCB], writes=[xmB])
                return xm, xmB

            def big_gemm(W, xm, xmB, dst):
                Wv = kc(W)
                for c in range(NCH):
                    if c % 4 == 0:
                        w, wB = wr.get()
                        p.dma("pool", w[:], Wv[:, :, c * 128:c * 128 + 512], writes=[wB])
                    wo_ = (c % 4) * 128
                    ps, psB = p.psum()
                    for k_ in range(NCH):
                        p.op("pe", lambda e, k_=k_, wo_=wo_: e.matmul(ps[:, :n], lhsT=w[:, k_, wo_:wo_ + 128], rhs=xm[:, k_, :n], start=(k_ == 0), stop=(k_ == NCH - 1)),
                             reads=[wB, xmB], writes=[psB])
                    o, oB = ofr.get()
                    p.op("act", lambda e: e.copy(out=o[:, :n], in_=ps[:, :n]), reads=[psB], writes=[oB])
                    p.dma("sp", dst[c * 128:(c + 1) * 128, t0:t0 + n], o[:, :n], reads=[oB])

            def lora_in(wt, width, xm, xmB, func):
                lo, loB = lor.get()
                for (m0, m) in tiles_of(width, 128):
                    ps, psB = p.psum()
                    for k_ in range(NCH):
                        p.op("pe", lambda e, k_=k_: e.matmul(ps[:m, :n], lhsT=wt[:, k_, m0:m0 + m], rhs=xm[:, k_, :n], start=(k_ == 0), stop=(k_ == NCH - 1)),
                             reads=[lB, xmB], writes=[psB])
                    if func is None:
                        p.op("act", lambda e: e.copy(out=lo[:m, m0 // 128, :n], in_=ps[:m, :n]), reads=[psB], writes=[loB])
                    else:
                        p.op("act", lambda e: e.activation(out=lo[:m, m0 // 128, :n], in_=ps[:m, :n], func=func), reads=[psB], writes=[loB])
                return lo, loB

            xm, xmB = mix(0)
            big_gemm(I.c_w_r, xm, xmB, Sx.RR)
            xm, xmB = mix(1)
            lo, loB = lora_in(w1, 96, xm, xmB, AF.Tanh)
            for c in range(NCH):
                ps, psB = p.psum()
                p.op("pe", lambda e, c=c: e.matmul(ps[:, :n], lhsT=w2[:, c * 128:(c + 1) * 128], rhs=lo[:96, 0, :n], start=True, stop=True),
                     reads=[lB, loB], writes=[psB])
                o, oB = ofr.get()
                p.op("act", lambda e, c=c: e.activation(out=o[:, :n], in_=ps[:, :n], func=AF.Sigmoid, bias=pcol("w0", c)),
                     reads=[psB, PCB], writes=[oB])
                p.op("act", lambda e: e.activation(out=o[:, :n], in_=o[:, :n], func=AF.Exp, scale=-float(np.exp(-0.5))), reads=[oB], writes=[oB])
                p.dma("sp", Sx.DEC[c * 128:(c + 1) * 128, t0:t0 + n], o[:, :n], reads=[oB])
            xm, xmB = mix(2)
            big_gemm(I.c_w_k, xm, xmB, Sx.KRAW)
            xm, xmB = mix(3)
            big_gemm(I.c_w_v, xm, xmB, Sx.VV)
            xm, xmB = mix(4)
            lo, loB = lora_in(a1, 96, xm, xmB, None)
            for c in range(NCH):
                ps, psB = p.psum()
                p.op("pe", lambda e, c=c: e.matmul(ps[:, :n], lhsT=a2[:, c * 128:(c + 1) * 128], rhs=lo[:96, 0, :n], start=True, stop=True),
                     reads=[lB, loB], writes=[psB])
                o, oB = ofr.get()
                p.op("act", lambda e, c=c: e.activation(out=o[:, :n], in_=ps[:, :n], func=AF.Sigmoid, bias=pcol("a0", c)),
                     reads=[psB, PCB], writes=[oB])
                p.dma("sp", Sx.AA[c * 128:(c + 1) * 128, t0:t0 + n], o[:, :n], reads=[oB])
            xm, xmB = mix(5)
            lo, loB = lora_in(g1, 256, xm, xmB, AF.Sigmoid)
            for c in range(NCH):
                ps, psB = p.psum()
                for k_ in range(2):
                    p.op("pe", lambda e, c=c, k_=k_: e.matmul(ps[:, :n], lhsT=g2[:, k_, c * 128:(c + 1) * 128], rhs=lo[:, k_, :n],
                                                              start=(k_ == 0), stop=(k_ == 1)), reads=[lB, loB], writes=[psB])
                o, oB = obr.get()
                p.op("act", lambda e: e.copy(out=o[:, :n], in_=ps[:, :n]), reads=[psB], writes=[oB])
                p.dma("sp", Sx.GG[c * 128:(c + 1) * 128, t0:t0 + n], o[:, :n], reads=[oB])


def stage_rwkv_prep(g):
    nc, p, I, O, Sx, K = g.nc, g.p, g.I, g.O, g.Sx, g.K
    pcol, PCB, PC = g.pcol, g.PCB, g.PC
    NT = g.NT
    with ExitStack() as st:
        sb = lambda name, shape, dt=F32: st.enter_context(sbt(nc, name, shape, dt))
        omka = sb("omka", [128, NCH])
        omB = Buf("omka")
        i0 = PARAMS.index("k_a") * 16
        p.op("dve", lambda e: e.tensor_scalar(out=omka[:, :], in0=PC[:, i0:i0 + 16], scalar1=-1.0, scalar2=1.0, op0=OP.mult, op1=OP.add),
             reads=[PCB], writes=[omB])
        ring = lambda nm, k=2: Ring(nc, st, nm, k, [128, 512], F32)
        kr, ar, rr_, vr = ring("pk"), ring("pa"), ring("pr"), ring("pv")
        kkr, sqr, k2r, bbr, nkr, bor = ring("pkk"), ring("psq"), ring("pk2"), ring("pbb"), ring("pnk"), ring("pbo")
        vtk = sb("vtk", [128, 4, D], BF16)
        vtkB = Buf("vtk")
        dcr, gmr, gpr = ring("pdc"), ring("pgm"), ring("pgp")
        zer = sb("zer", [128, RB])
        zerB = Buf("zer")
        p.op("dve", lambda e: e.memset(zer[:, :], 0.0), writes=[zerB])
        for (t0, n) in tiles_of(NT, 512):
            for c in range(NCH):
                rows = slice(c * 128, (c + 1) * 128)
                k, kB = kr.get()
                a, aB = ar.get()
                r, rB = rr_.get()
                v, vB = vr.get()
                p.dma("sp", k[:, :n], Sx.KRAW[rows, t0:t0 + n], writes=[kB])
                p.dma("sp", a[:, :n], Sx.AA[rows, t0:t0 + n], writes=[aB])
                p.dma("sp", r[:, :n], Sx.RR[rows, t0:t0 + n], writes=[rB])
                p.dma("sp", v[:, :n], Sx.VV[rows, t0:t0 + n], writes=[vB])
                kk, kkB = kkr.get()
                sq, sqB = sqr.get()
                p.op("dve", lambda e: e.tensor_scalar(out=kk[:, :n], in0=k[:, :n], scalar1=pcol("k_k", c), scalar2=None, op0=OP.mult),
                     reads=[kB, PCB], writes=[kkB])
                p.op("act", lambda e: e.activation(out=sq[:, :n], in_=kk[:, :n], func=AF.Square), reads=[kkB], writes=[sqB])
                ps, psB = p.psum()
                p.op("pe", lambda e: e.matmul(ps[:, :n], lhsT=K.blk[:, :], rhs=sq[:, :n], start=True, stop=True), reads=[sqB, K.buf], writes=[psB])
                p.op("dve", lambda e: e.tensor_scalar(out=sq[:, :n], in0=ps[:, :n], scalar1=1e-24, scalar2=None, op0=OP.max),
                     reads=[psB], writes=[sqB])
                p.op("act", lambda e: e.activation(out=sq[:, :n], in_=sq[:, :n], func=AF.Sqrt), reads=[sqB], writes=[sqB])
                p.op("dve", lambda e: e.reciprocal(out=sq[:, :n], in_=sq[:, :n]), reads=[sqB], writes=[sqB])
                p.op("dve", lambda e: e.tensor_tensor(out=kk[:, :n], in0=kk[:, :n], in1=sq[:, :n], op=OP.mult), reads=[kkB, sqB], writes=[kkB])
                bb, bbB = bbr.get()
                nk, nkB = nkr.get()
                p.op("dve", lambda e: e.tensor_tensor(out=bb[:, :n], in0=kk[:, :n], in1=a[:, :n], op=OP.mult), reads=[kkB, aB], writes=[bbB])
                p.op("act", lambda e: e.activation(out=nk[:, :n], in_=kk[:, :n], func=AF.Copy, scale=-1.0), reads=[kkB], writes=[nkB])
                k2, k2B = k2r.get()
                p.op("dve", lambda e: e.tensor_scalar(out=k2[:, :n], in0=a[:, :n], scalar1=pcol("k_a", c), scalar2=omka[:, c:c + 1],
                                                      op0=OP.mult, op1=OP.add), reads=[aB, PCB, omB], writes=[k2B])
                p.op("dve", lambda e: e.tensor_tensor(out=k2[:, :n], in0=k2[:, :n], in1=k[:, :n], op=OP.mult), reads=[k2B, kB], writes=[k2B])
                bo, boB = bor.get()
                p.op("dve", lambda e: e.scalar_tensor_tensor(out=bo[:, :n], in0=r[:, :n], scalar=pcol("r_k", c), in1=k2[:, :n],
                                                             op0=OP.mult, op1=OP.mult), reads=[rB, k2B, PCB], writes=[boB])
                ps2, ps2B = p.psum()
                p.op("pe", lambda e: e.matmul(ps2[:, :n], lhsT=K.blk[:, :], rhs=bo[:, :n], start=True, stop=True), reads=[boB, K.buf], writes=[ps2B])
                p.op("dve", lambda e: e.tensor_tensor(out=bo[:, :n], in0=ps2[:, :n], in1=v[:, :n], op=OP.mult), reads=[ps2B, vB], writes=[boB])
                p.dma("sp", Sx.BONUS[rows, t0:t0 + n], bo[:, :n], reads=[boB])
                dc, dcB = dcr.get()
                gm, gmB = gmr.get()
                gp, gpB = gpr.get()
                p.dma("sp", dc[:, :n], Sx.DEC[rows, t0:t0 + n], writes=[dcB])
                for c0 in range(0, n, RB):
                    p.op("dve", lambda e, c0=c0: e.tensor_tensor_scan(out=gm[:, c0:c0 + RB], data0=dc[:, c0:c0 + RB], data1=zer[:, :RB], initial=1.0,
                                                                      op0=OP.mult, op1=OP.add), reads=[dcB, zerB], writes=[gmB])
                b3 = lambda a: a.rearrange("p (b t) -> p b t", t=RB)
                p.op("act", lambda e: e.copy(out=b3(gp[:, :n])[:, :, 1:RB], in_=b3(gm[:, :n])[:, :, 0:RB - 1]), reads=[gmB], writes=[gpB])
                p.op("dve", lambda e: e.memset(b3(gp[:, :n])[:, :, 0:1], 1.0), writes=[gpB])
                p.op("dve", lambda e: e.tensor_tensor(out=nk[:, :n], in0=nk[:, :n], in1=gp[:, :n], op=OP.mult), reads=[nkB, gpB], writes=[nkB])
                p.op("dve", lambda e: e.tensor_tensor(out=r[:, :n], in0=r[:, :n], in1=gm[:, :n], op=OP.mult), reads=[rB, gmB], writes=[rB])
                p.dma("sp", Sx.GAM[rows, t0:t0 + n], gm[:, :n], reads=[gmB])
                p.op("dve", lambda e: e.reciprocal(out=gp[:, :n], in_=gm[:, :n]), reads=[gmB], writes=[gpB])
                p.op("dve", lambda e: e.tensor_tensor(out=bb[:, :n], in0=bb[:, :n], in1=gp[:, :n], op=OP.mult), reads=[bbB, gpB], writes=[bbB])
                p.op("dve", lambda e: e.tensor_tensor(out=k2[:, :n], in0=k2[:, :n], in1=gp[:, :n], op=OP.mult), reads=[k2B, gpB], writes=[k2B])
                p.dma("sp", Sx.NKK[rows, t0:t0 + n], nk[:, :n], reads=[nkB])
                p.dma("sp", Sx.BB[rows, t0:t0 + n], bb[:, :n], reads=[bbB])
                p.dma("sp", Sx.K2[rows, t0:t0 + n], k2[:, :n], reads=[k2B])
                p.dma("sp", Sx.RR2[rows, t0:t0 + n], r[:, :n], reads=[rB])
                psT, psTB = p.psum()
                subs = tiles_of(n, 128)
                for si_, (s0, m) in enumerate(subs):
                    p.op("pe", lambda e, s0=s0, m=m, si_=si_: e.transpose(psT[:m, si_ * 128:(si_ + 1) * 128], v[:, s0:s0 + m], K.ident[:, :]),
                         reads=[vB, K.buf], writes=[psTB])
                for si_, (s0, m) in enumerate(subs):
                    p.op("act", lambda e, m=m, si_=si_, c=c: e.copy(out=vtk[:m, si_, c * 128:(c + 1) * 128], in_=psT[:m, si_ * 128:(si_ + 1) * 128]),
                         reads=[psTB], writes=[vtkB])
            for si_, (s0, m) in enumerate(tiles_of(n, 128)):
                p.dma("sp", Sx.VTOK[t0 + s0:t0 + s0 + m, :], vtk[:m, si_, :], reads=[vtkB])


def stage_rwkv_rec(g):
    nc, p, I, O, Sx, K = g.nc, g.p, g.I, g.O, g.Sx, g.K
    with ExitStack() as st:
        sb = lambda name, shape, dt=F32: st.enter_context(sbt(nc, name, shape, dt))
        ST = sb("ST", [128, 32, 64])
        STB = [Buf("ST0"), Buf("ST1")]
        blkb = sb("blkb", [128, 128], BF16)
        sel = sb("sel", [128, 2], BF16)
        idb = sb("idb", [128, 128], BF16)
        MK = sb("MK", [32, 64])
        cB = Buf("rc")
        p.op("dve", lambda e: e.tensor_copy(out=blkb[:, :], in_=K.blk[:, :]), reads=[K.buf], writes=[cB])
        p.op("dve", lambda e: e.tensor_copy(out=sel[:, 0:1], in_=K.blk[:, 0:1]), reads=[K.buf], writes=[cB])
        p.op("dve", lambda e: e.tensor_copy(out=sel[:, 1:2], in_=K.blk[:, 64:65]), reads=[K.buf], writes=[cB])
        p.op("dve", lambda e: e.tensor_copy(out=idb[:, :], in_=K.ident[:, :]), reads=[K.buf], writes=[cB])
        p.op("dve", lambda e: e.tensor_tensor(out=MK[:, 0:32], in0=K.tri[:32, :32], in1=K.ident[:32, :32], op=OP.subtract), reads=[K.buf], writes=[cB])
        p.op("dve", lambda e: e.tensor_copy(out=MK[:, 32:64], in_=K.tri[:32, :32]), reads=[K.buf], writes=[cB])
        p.op("dve", lambda e: e.memset(MK[:, 63:64], 0.0), reads=[cB], writes=[cB])
        OH = sb("OH", [32, 2, 32, 128], BF16)
        p.op("dve", lambda e: e.memset(OH[:, :, :, :], 0.0), writes=[cB])
        for h2 in range(2):
            p.op("dve", lambda e, h2=h2: e.tensor_copy(out=OH[:, h2, :, h2 * 64:(h2 + 1) * 64],
                                                        in_=K.ident[:32, :32].unsqueeze(2).to_broadcast([32, 32, 64])),
                 reads=[K.buf, cB], writes=[cB])
        k2m_r = Ring(nc, st, "k2m", 1, [128, 2, 32, RB], BF16)
        nrb_r = Ring(nc, st, "nrb", 1, [128, 2, 32, RB], BF16)
        TB = RB
        names = ("GAM", "BB", "K2", "RR2")
        blk_r = {nm: Ring(nc, st, "bt" + nm, 2, [128, 32, TB], F32) for nm in names}
        cmb_r = Ring(nc, st, "btCMB", 2, [128, 2, 32, TB], F32)
        for t_, b_ in zip(cmb_r.t, cmb_r.b):
            p.op("dve", lambda e, t_=t_: e.memset(t_[:, :, :, :], 0.0), writes=[b_])
        vblk_r = Ring(nc, st, "vblk", 2, [32, 2, D], BF16)
        gm_r = Ring(nc, st, "Gm", 2, [32, 32, 64], BF16)
        sap_r = Ring(nc, st, "saP", 2, [32, 2, D], BF16)
        yp_r = Ring(nc, st, "yPs", 1, [32, D], F32)
        ktok_r = Ring(nc, st, "ktok", 2, [32, D], BF16)
        pl_r = Ring(nc, st, "PLs", 2, [128, 2, 1024], F32)
        ayr = Ring(nc, st, "ay", 2, [128, 2, 1024], BF16)
        tBr = Ring(nc, st, "tB", 2, [128, 1024], F32)
        TS = 2
        ysr = Ring(nc, st, "ys", 2, [2, 2, TS, 1024], F32)
        sior = Ring(nc, st, "sio", 2, [64, 2, 64], F32)
        v3 = lambda t2: t2.rearrange("p (j v) -> p j v", v=64)
        RRv = fm(Sx.RR)
        RR2v = fm(Sx.RR2)
        NKv = fm(Sx.NKK)
        for grp, (sis, T) in enumerate((([0, 1], g.Tp), ([2, 3], g.Ts))):
            toks = [g.seqs[si][0] for si in sis]
            assert T % TB == 0
            if grp == 0:
                for hf in range(2):
                    p.op("dve", lambda e, hf=hf: e.memset(ST[:, hf * 16:(hf + 1) * 16, :], 0.0), writes=[STB[hf]])
            else:
                for hf in range(2):
                    for j in range(16):
                        sio, sioB = sior.get()
                        p.dma("sp", sio[:, :, :], I.S[hf, 2 * j:2 * j + 2, :, :].rearrange("h v k -> v h k"), writes=[sioB])
                        ps, psB = p.psum()
                        p.op("pe", lambda e: e.transpose(ps[:, 0:64], sio[:, :, :].rearrange("v h k -> v (h k)"), K.ident[:64, :64]),
                             reads=[sioB, K.buf], writes=[psB])
                        p.op("act", lambda e, hf=hf, j=j: e.copy(out=ST[:, hf * 16 + j, :], in_=ps[:, 0:64]), reads=[psB], writes=[STB[hf]])
            ys_slots = {}
            sth = lambda hf: ST[:, hf * 16:(hf + 1) * 16, :]

            def y_flush(ty):
                if ty % TS == TS - 1 or ty == T - 1:
                    nts = ty % TS + 1
                    tf = ty - nts + 1
                    ys, ysB = ys_slots[tf // TS]
                    for hf in range(2):
                        dst = Sx.YT[toks[hf] + tf:toks[hf] + tf + nts, :].rearrange("t (j h v) -> h t j v", h=2, v=64)
                        p.dma("sp", dst, ys[:, hf, :nts, :].rearrange("h t (j v) -> h t j v", v=64), reads=[ysB])
                    del ys_slots[tf // TS]

            def emit_y(ay, ayB, hf, ty):
                if ty // TS not in ys_slots:
                    ys_slots[ty // TS] = ysr.get()
                ys, ysB = ys_slots[ty // TS]
                ps, psBs = p.psum2([2], "recY")
                for k_ in range(2):
                    p.op("pe", lambda e, k_=k_: e.matmul(ps[:2, k_ * 512:(k_ + 1) * 512], lhsT=sel[:, :], rhs=ay[:, 1, k_ * 512:(k_ + 1) * 512],
                                                         start=True, stop=True), reads=[cB, ayB], writes=[psBs[k_]])
                p.op("act", lambda e: e.copy(out=ys[:, hf, ty % TS, :], in_=ps[:2, :]), reads=psBs, writes=[ysB])

            def prologue(tb0):
                B = {}
                for nm in names:
                    t_, b_ = blk_r[nm].get()
                    src = fm(getattr(Sx, nm))
                    for hf in range(2):
                        p.dma("sp", t_[:, hf * 16:(hf + 1) * 16, :TB], src[:, :, toks[hf] + tb0:toks[hf] + tb0 + TB], writes=[b_])
                    B[nm] = (t_, b_)
                cm, cmB = cmb_r.get()
                for hf in range(2):
                    p.dma("sp", cm[:, 0, hf * 16:(hf + 1) * 16, :TB], NKv[:, :, toks[hf] + tb0:toks[hf] + tb0 + TB], writes=[cmB])
                    p.dma("sp", cm[:, 1, hf * 16:(hf + 1) * 16, 1:TB], RR2v[:, :, toks[hf] + tb0:toks[hf] + tb0 + TB - 1], writes=[cmB])
                    if tb0 > 0:
                        p.dma("sp", cm[:, 1, hf * 16:(hf + 1) * 16, 0:1], RRv[:, :, toks[hf] + tb0 - 1:toks[hf] + tb0], writes=[cmB], slow=True)
                B["cm"] = (cm, cmB)
                vb, vbB = vblk_r.get()
                for hf in range(2):
                    p.dma("sp", vb[:, hf, :], Sx.VTOK[toks[hf] + tb0:toks[hf] + tb0 + TB, :], writes=[vbB])
                sap, sapB = sap_r.get()
                pl, plB = pl_r.get()
                B["sap"] = (sap, sapB)
                B["pl"] = (pl, plB)
                yield
                k2t, k2B = B["K2"]
                r2t, r2B = B["RR2"]
                k2m, k2mB = k2m_r.get()
                nrb, nrbB = nrb_r.get()
                for h2 in range(2):
                    p.op("act", lambda e, h2=h2: e.activation(out=k2m[:, h2, :, :], in_=k2t[:, :, :], func=AF.Copy, scale=K.blk[:, h2 * 64:h2 * 64 + 1]),
                         reads=[k2B, K.buf], writes=[k2mB])
                p.op("act", lambda e: e.copy(out=nrb[:, 0, :, :], in_=cm[:, 0, :, :]), reads=[cmB], writes=[nrbB])
                p.op("act", lambda e: e.copy(out=nrb[:, 1, :, :], in_=r2t[:, :, :]), reads=[r2B], writes=[nrbB])
                yield
                for hf in range(2):
                    gm, gmB = gm_r.get()
                    for q in range(2):
                        ps, psBs = p.psum2([3], "recP")
                        for hh in range(16):
                            h = q * 16 + hh
                            j, h2 = h // 2, h % 2
                            rows = slice(h2 * 64, (h2 + 1) * 64)
                            bj = hf * 16 + j
                            p.op("pe", lambda e, hh=hh, h2=h2, bj=bj: e.matmul(ps[:32, hh * 64:hh * 64 + 32], lhsT=k2m[:, h2, bj, :], rhs=nrb[:, 0, bj, :],
                                                                              start=True, stop=True), reads=[k2mB, nrbB], writes=[psBs[hh // 8]])
                            p.op("pe", lambda e, hh=hh, h2=h2, bj=bj: e.matmul(ps[:32, hh * 64 + 32:hh * 64 + 64], lhsT=k2m[:, h2, bj, :], rhs=nrb[:, 1, bj, :],
                                                                              start=True, stop=True), reads=[k2mB, nrbB], writes=[psBs[hh // 8]])
                            if hh % 4 == 3:
                                yield
                        p.op("dve", lambda e, q=q, ps=ps: e.tensor_tensor(out=gm[:, q * 16:(q + 1) * 16, :], in0=ps[:32, :].rearrange("s (h t) -> s h t", t=64),
                                                                         in1=MK[:, :].unsqueeze(1).to_broadcast([32, 16, 64]), op=OP.mult),
                             reads=psBs + [cB], writes=[gmB])
                        yield
                    for which in range(2):
                        if which == 1:
                            yps, ypsB = yp_r.get()
                        for q in range(2):
                            ps, psBs = p.psum2([3], "recP")
                            for hh in range(16):
                                h = q * 16 + hh
                                p.op("pe", lambda e, hh=hh, h=h: e.matmul(ps[:32, hh * 64:(hh + 1) * 64], lhsT=gm[:, h, which * 32:(which + 1) * 32],
                                                                          rhs=vb[:, hf, h * 64:(h + 1) * 64], start=True, stop=True),
                                     reads=[gmB, vbB], writes=[psBs[hh // 8]])
                                if hh % 8 == 7:
                                    yield
                            if which == 0:
                                p.op("act", lambda e, q=q, ps=ps: e.copy(out=sap[:, hf, q * 1024:(q + 1) * 1024], in_=ps[:32, :]), reads=psBs, writes=[sapB])
                            else:
                                p.op("act", lambda e, q=q, ps=ps: e.copy(out=yps[:, q * 1024:(q + 1) * 1024], in_=ps[:32, :]), reads=psBs, writes=[ypsB])
                            yield
                        if which == 1:
                            p.dma("sp", Sx.YP[toks[hf] + tb0:toks[hf] + tb0 + TB, :], yps[:, :], reads=[ypsB])
                    kt, ktB = ktok_r.get()
                    for q in range(2):
                        ps, psBs = p.psum2([3], "recP")
                        for jj in range(8):
                            j = q * 8 + jj
                            p.op("pe", lambda e, jj=jj, j=j: e.transpose(ps[:32, jj * 128:(jj + 1) * 128], k2t[:, hf * 16 + j, :], K.ident[:, :]),
                                 reads=[k2B, K.buf], writes=[psBs[jj // 4]])
                        p.op("act", lambda e, q=q, ps=ps: e.copy(out=kt[:, q * 1024:(q + 1) * 1024], in_=ps[:32, :]), reads=psBs, writes=[ktB])
                        yield
                    ps, psBs = p.psum2([3], "recP")
                    for j in range(16):
                        for h2 in range(2):
                            p.op("pe", lambda e, j=j, h2=h2: e.matmul(ps[h2 * 64:(h2 + 1) * 64, j * 64:(j + 1) * 64],
                                                                      lhsT=kt[:, j * 128 + h2 * 64:j * 128 + (h2 + 1) * 64],
                                                                      rhs=vb[:, hf, (2 * j + h2) * 64:(2 * j + h2 + 1) * 64], start=True, stop=True),
                                 reads=[ktB, vbB], writes=[psBs[j // 8]])
                        if j % 4 == 3:
                            yield
                    p.op("act", lambda e, ps=ps: e.copy(out=pl[:, hf, :], in_=ps), reads=psBs, writes=[plB])
                    yield
                self_out.append(B)

            nblk = T // TB
            self_out = []
            gen = prologue(0)
            for _ in gen:
                pass
            for bi in range(nblk):
                tb0 = bi * TB
                B = self_out.pop(0)
                gen = prologue(tb0 + TB) if bi + 1 < nblk else iter(())
                sa_next = {}
                bt = B
                cm, cmB = B["cm"]
                sap, sapB = B["sap"]
                pl, plB = B["pl"]
                for tt in range(TB):
                    t = tb0 + tt

                    def col(nm, hf, j0=0, nj=16):
                        t_ = bt[nm][0]
                        return t_[:, hf * 16 + j0:hf * 16 + j0 + nj, tt:tt + 1].to_broadcast([128, nj, 64])

                    def onehots(tt_):
                        for hf in range(2):
                            ps, psBs = p.psum2([0, 1], "recS")
                            for k_ in range(2):
                                for h2 in range(2):
                                    rhs = sap[:, hf, :].rearrange("t (j h v) -> t j h v", h=2, v=64)[:, k_ * 8:(k_ + 1) * 8, h2, :]
                                    p.op("pe", lambda e, h2=h2, rhs=rhs, k_=k_, ps=ps: e.matmul(ps[:, k_ * 512:(k_ + 1) * 512], lhsT=OH[:, h2, tt_, :],
                                                                                             rhs=rhs, start=(h2 == 0), stop=False),
                                         reads=[cB, sapB], writes=[psBs[k_]])
                            sa_next[hf] = (ps, psBs)

                    if tt == 0:
                        onehots(0)
                    pend_sa = {}
                    for hf in range(2):
                        ay, ayB = ayr.get()
                        p.op("dve", lambda e, hf=hf: e.tensor_tensor(
                            out=ay[:, :, :].rearrange("p x (j v) -> p x j v", v=64),
                            in0=sth(hf).unsqueeze(1).to_broadcast([128, 2, 16, 64]),
                            in1=cm[:, :, hf * 16:(hf + 1) * 16, tt:tt + 1].to_broadcast([128, 2, 16, 64]), op=OP.mult),
                             reads=[STB[hf], cmB], writes=[ayB])
                        ps, psBs = sa_next[hf]
                        for k_ in range(2):
                            p.op("pe", lambda e, k_=k_, ps=ps: e.matmul(ps[:, k_ * 512:(k_ + 1) * 512], lhsT=blkb[:, :], rhs=ay[:, 0, k_ * 512:(k_ + 1) * 512],
                                                                       start=False, stop=True), reads=[cB, ayB], writes=[psBs[k_]])
                        pend_sa[hf] = (ps, psBs, ay, ayB)
                    for hf in range(2):
                        ps, psBs = pend_sa[hf][0], pend_sa[hf][1]
                        tB, tBB = tBr.get()
                        p.op("dve", lambda e, hf=hf, ps=ps, tB=tB: e.tensor_tensor(out=v3(tB[:, :]), in0=v3(ps), in1=col("BB", hf), op=OP.mult),
                             reads=psBs + [bt["BB"][1]], writes=[tBB])
                        p.op("dve", lambda e, hf=hf, tB=tB: e.tensor_tensor(out=sth(hf), in0=sth(hf), in1=v3(tB[:, :]), op=OP.add),
                             reads=[STB[hf], tBB], writes=[STB[hf]])
                        if tt == TB - 1:
                            p.op("dve", lambda e, hf=hf: e.tensor_tensor(out=sth(hf), in0=sth(hf), in1=v3(pl[:, hf, :]), op=OP.add),
                                 reads=[STB[hf], plB], writes=[STB[hf]])
                            p.op("dve", lambda e, hf=hf: e.tensor_tensor(out=sth(hf), in0=sth(hf), in1=col("GAM", hf), op=OP.mult),
                                 reads=[STB[hf], bt["GAM"][1]], writes=[STB[hf]])
                    for _ in range(g.cfg.get("pro_rate", 2)):
                        next(gen, None)
                    if t > 0:
                        for hf in range(2):
                            emit_y(pend_sa[hf][2], pend_sa[hf][3], hf, t - 1)
                        y_flush(t - 1)
                    if tt + 1 < TB:
                        onehots(tt + 1)
                for _ in gen:
                    pass
            for hf in range(2):
                ay, ayB = ayr.get()
                rt, rtB = blk_r["GAM"].get()
                p.dma("sp", rt[:, hf * 16:(hf + 1) * 16, 0:1], RRv[:, :, toks[hf] + T - 1:toks[hf] + T], writes=[rtB], slow=True)
                p.op("dve", lambda e, hf=hf: e.tensor_tensor(out=v3(ay[:, 1, :]), in0=sth(hf),
                                                              in1=rt[:, hf * 16:(hf + 1) * 16, 0:1].to_broadcast([128, 16, 64]), op=OP.mult),
                     reads=[STB[hf], rtB], writes=[ayB])
                emit_y(ay, ayB, hf, T - 1)
            y_flush(T - 1)
            for hf in range(2):
                oi = sis[hf]
                for j in range(16):
                    ps, psB = p.psum()
                    p.op("pe", lambda e, hf=hf, j=j: e.transpose(ps[:64, 0:128], ST[:, hf * 16 + j, :], K.ident[:, :]),
                         reads=[STB[hf], K.buf], writes=[psB])
                    sio, sioB = sior.get()
                    p.op("act", lambda e: e.copy(out=sio[:, :, :].rearrange("v h k -> v (h k)"), in_=ps[:64, 0:128]), reads=[psB], writes=[sioB])
                    p.dma("sp", O.S[oi, 2 * j:2 * j + 2, :, :].rearrange("h v k -> v h k"), sio[:, :, :], reads=[sioB])


def stage_rwkv_post(g):
    nc, p, I, O, Sx, K = g.nc, g.p, g.I, g.O, g.Sx, g.K
    pcol, PCB = g.pcol, g.PCB
    with ExitStack() as st:
        ytr = Ring(nc, st, "pyt", 2, [128, D], F32)
        sqr = Ring(nc, st, "psq", 2, [128, D], F32)
        str_ = Ring(nc, st, "pst", 2, [128, 96], F32)
        fmr = Ring(nc, st, "pfm", 2, [128, NCH, 128], F32)
        bor = Ring(nc, st, "pbo", 2, [128, NCH, 128], F32)
        ggr = Ring(nc, st, "pgg", 2, [128, NCH, 128], BF16)
        outr = Ring(nc, st, "pout", 2, [128, NCH, 128], BF16)
        h3 = lambda a: a.rearrange("t (h v) -> t h v", v=64)
        for (r0, m) in tiles_of(g.NT, 128):
            yt, ytB = ytr.get()
            sq, sqB = sqr.get()
            sx, sxB = str_.get()
            bo, boB = bor.get()
            gg, ggB = ggr.get()
            p.dma("sp", yt[:m, :], Sx.YT[r0:r0 + m, :], writes=[ytB])
            p.dma("sp", sq[:m, :], Sx.YP[r0:r0 + m, :], writes=[sqB])
            p.op("dve", lambda e: e.tensor_tensor(out=yt[:m, :], in0=yt[:m, :], in1=sq[:m, :], op=OP.add), reads=[ytB, sqB], writes=[ytB])
            p.dma("sp", bo[:, :, :m], fm(Sx.BONUS)[:, :, r0:r0 + m], writes=[boB])
            p.dma("sp", gg[:, :, :m], fm(Sx.GG)[:, :, r0:r0 + m], writes=[ggB])
            p.op("dve", lambda e: e.tensor_reduce(out=sx[:m, 0:32], in_=h3(yt[:m, :]), axis=AX.X, op=OP.add), reads=[ytB], writes=[sxB])
            p.op("dve", lambda e: e.tensor_scalar(out=sx[:m, 0:32], in0=sx[:m, 0:32], scalar1=1.0 / 64, scalar2=None, op0=OP.mult),
                 reads=[sxB], writes=[sxB])
            p.op("dve", lambda e: e.tensor_tensor(out=h3(yt[:m, :]), in0=h3(yt[:m, :]), in1=sx[:m, 0:32].unsqueeze(2).to_broadcast([m, 32, 64]),
                                                  op=OP.subtract), reads=[ytB, sxB], writes=[ytB])
            p.op("act", lambda e: e.activation(out=sq[:m, :], in_=yt[:m, :], func=AF.Square), reads=[ytB], writes=[sqB])
            p.op("dve", lambda e: e.tensor_reduce(out=sx[:m, 32:64], in_=h3(sq[:m, :]), axis=AX.X, op=OP.add), reads=[sqB], writes=[sxB])
            p.op("act", lambda e: e.activation(out=sx[:m, 64:96], in_=sx[:m, 32:64], func=AF.Sqrt, scale=1.0 / 64, bias=K.eps_gn[:m, :]),
                 reads=[sxB, K.buf], writes=[sxB])
            p.op("dve", lambda e: e.reciprocal(out=sx[:m, 64:96], in_=sx[:m, 64:96]), reads=[sxB], writes=[sxB])
            p.op("dve", lambda e: e.tensor_tensor(out=h3(yt[:m, :]), in0=h3(yt[:m, :]), in1=sx[:m, 64:96].unsqueeze(2).to_broadcast([m, 32, 64]),
                                                  op=OP.mult), reads=[ytB, sxB], writes=[ytB])
            fmt, fmB = fmr.get()
            for c4 in range(4):
                ps, psB = p.psum()
                for j in range(4):
                    c = c4 * 4 + j
                    p.op("pe", lambda e, c=c, j=j: e.transpose(ps[:, j * 128:j * 128 + m], yt[:m, c * 128:(c + 1) * 128], K.ident[:m, :m]),
                         reads=[ytB, K.buf], writes=[psB])
                for j in range(4):
                    c = c4 * 4 + j
                    p.op("act", lambda e, c=c, j=j: e.activation(out=fmt[:, c, :m], in_=ps[:, j * 128:j * 128 + m], func=AF.Identity,
                                                                  scale=pcol("gn_g", c), bias=pcol("gn_b", c)),
                         reads=[psB, PCB], writes=[fmB])
            p.op("dve", lambda e: e.tensor_tensor(out=fmt[:, :, :m], in0=fmt[:, :, :m], in1=bo[:, :, :m], op=OP.add), reads=[fmB, boB], writes=[fmB])
            ot, otB = outr.get()
            p.op("dve", lambda e: e.tensor_tensor(out=ot[:, :, :m], in0=fmt[:, :, :m], in1=gg[:, :, :m], op=OP.mult), reads=[fmB, ggB], writes=[otB])
            p.dma("sp", fm(Sx.YG2)[:, :, r0:r0 + m], ot[:, :, :m], reads=[otB])

def host_consts():
    ident = np.eye(128, dtype=np.float32)
    tri = np.triu(np.ones((128, 128), np.float32))
    ones = np.ones((128, 128), np.float32)
    blk = np.zeros((128, 128), np.float32)
    blk[:64, :64] = 1
    blk[64:, 64:] = 1
    return {"k_ident": ident, "k_tri": tri, "k_ones": ones, "k_blk": blk}


W2D = {"a_b_ig": (1, HM), "a_b_fg": (1, HM), "a_mlstm_norm": (1, D), "a_conv_w": (4, D), "a_conv_b": (1, D),
       "a_lru_wa": (16, 128, 128), "a_lru_ba": (1, D), "a_lru_wx": (16, 128, 128), "a_lru_bx": (1, D),
       "a_lru_lambda": (1, D), "a_w_in": (D, INW), "a_w_out": (2 * D, D), "c_mu": (6, D), "c_w_r": (D, D),
       "c_w_k": (D, D), "c_w_v": (D, D), "c_w0": (1, D), "c_w1": (D, 96), "c_w2": (96, D), "c_a0": (1, D),
       "c_a1": (D, 96), "c_a2": (96, D), "c_g1": (D, 256), "c_g2": (256, D), "c_k_k": (1, D), "c_k_a": (1, D),
       "c_r_k": (1, D), "c_gn_g": (1, D), "c_gn_b": (1, D), "c_w_o": (D, D)}
WKEEP = ("ln1_g", "ln1_b", "ln2_g", "ln2_b", "mlp_w1", "mlp_w2")


def make_in_maps(inputs, n_cores, Tp, Ts):
    f = lambda a: np.ascontiguousarray(np.asarray(a, dtype=np.float32))
    shared = dict(host_consts())
    for k, shp in W2D.items():
        shared[k] = f(inputs[k]).reshape(shp)
    for k in WKEEP:
        shared[k] = f(inputs[k])
    xp, xs = f(inputs["x_prompt"]), f(inputs["x_sample"])
    maps = []
    for c in range(n_cores):
        b = slice(2 * c, 2 * c + 2)
        m = dict(shared)
        m["x_all"] = np.concatenate([xp[b].reshape(2 * Tp, D), xs[b].reshape(2 * Ts, D)], 0)
        m["st_mC"] = f(inputs["state_mlstm_C"])[0, b]
        m["st_mn"] = f(inputs["state_mlstm_n"])[0, b]
        m["st_mm"] = f(inputs["state_mlstm_m"])[0, b]
        m["st_conv"] = f(inputs["state_lru_conv"])[0, b]
        m["st_lruh"] = f(inputs["state_lru_h"])[0, b]
        m["st_shift"] = f(inputs["state_rwkv_shift"])[0, b]
        m["st_S"] = f(inputs["state_rwkv_S"])[0, b]
        m = {k: np.ascontiguousarray(v) for k, v in m.items()}
        maps.append(m)
    return maps


_CACHE = {}


def run(inputs, n_cores=8, cfg=None):
    Tp = inputs["x_prompt"].shape[1]
    Ts = inputs["x_sample"].shape[1]
    cfg = dict(cfg or {})
    cfg.update(Tp=Tp, Ts=Ts)
    nc = build(cfg)
    maps = make_in_maps(inputs, n_cores, Tp, Ts)
    res = run_bass_kernel_spmd(nc, maps, core_ids=list(range(n_cores)))
    return res.results


def kernel(**inputs):
    r = run(inputs, 8)
    Tp = inputs["x_prompt"].shape[1]
    Ts = inputs["x_sample"].shape[1]
    n = len(r)
    yp = np.stack([r[c]["o_y"][:2 * Tp].reshape(2, Tp, D) for c in range(n)]).reshape(2 * n, Tp, D)
    ys = np.stack([r[c]["o_y"][2 * Tp:].reshape(2, Ts, D) for c in range(n)]).reshape(2 * n, Ts, D)

    def gather(name, lo):
        a = np.concatenate([r[c][name][lo:lo + 2] for c in range(n)], 0)
        return np.ascontiguousarray(a[None].astype(np.float32))

    outs = [yp.astype(np.float32), ys.astype(np.float32)]
    for lo in (0, 2):
        for nm in ("o_mC", "o_mn", "o_mm", "o_conv", "o_lruh", "o_shift", "o_S"):
            outs.append(gather(nm, lo))
    return tuple(outs)
```

```python
import numpy as np
import ml_dtypes
from contextlib import ExitStack
import concourse.bass as bass
import concourse.mybir as mybir
from concourse.bass_utils import run_bass_kernel_spmd

F32 = mybir.dt.float32
BF16 = mybir.dt.bfloat16
AF = mybir.ActivationFunctionType
OP = mybir.AluOpType
AX = mybir.AxisListType

D = 2048
NCH = 16
DFF = 8192
HM = 8
DK = 128
DV = 256
INW = 10256
C_Q, C_K, C_V, C_O, C_IG, C_FG, C_XR, C_YG = 0, 1024, 2048, 4096, 6144, 6152, 6160, 8208
ALPHA = 4.0 ** 0.25
LN_EPS = 1e-5
GN_EPS = 64e-5
RH = 32
RN = 64
RB = 32


class Buf:
    __slots__ = ("name", "w", "r", "excl")

    def __init__(self, name="", excl=False):
        self.name = name
        self.excl = excl
        self.w = None
        self.r = {}


class Prog:
    NDMA = {"sp": 12, "pool": 12, "act": 4}

    def __init__(self, nc):
        self.nc = nc
        self.E = {"pe": nc.tensor, "dve": nc.vector, "act": nc.scalar, "pool": nc.gpsimd, "sp": nc.sync}
        self.sems = {}
        self.cnt = {}
        for e in ("pe", "dve", "act", "pool"):
            self.sems[e] = nc.alloc_semaphore("c_" + e)
            self.cnt[e] = 0
        self.dsem = {}
        self.dcnt = {}
        self.dnext = {}
        for q, n in self.NDMA.items():
            self.dnext[q] = 0
            for i in range(n):
                self.dsem[(q, i)] = nc.alloc_semaphore("d_%s%d" % (q, i))
                self.dcnt[(q, i)] = 0
        self.seen = {e: {} for e in self.E}
        self.psb = []
        self.psi = 0
        self.pctr = {}
        self.ninst = 0
        self.trace = {e: [] for e in self.E}

    def _sem(self, key):
        return self.sems[key] if key in self.sems else self.dsem[key]

    def _wait(self, eng, key, val):
        if val <= 0:
            return
        if self.seen[eng].get(key, 0) >= val:
            return
        self.E[eng].wait_ge(self._sem(key), val)
        self.trace[eng].append(("w", key, val))
        self.seen[eng][key] = val
        self.ninst += 1

    def _deps(self, eng, reads, writes):
        deps = []
        for b in reads:
            if b.w is not None:
                deps.append(b.w)
        for b in writes:
            if b.w is not None:
                deps.append(b.w)
            for k, (v, e) in b.r.items():
                deps.append((k, v, e))
        for k, v, e in deps:
            if e == "pe" and eng == "pe":
                continue
            self._wait(eng, k, v)

    def _commit(self, ev, reads, writes):
        for b in writes:
            b.w = ev
            b.r = {}
        for b in reads:
            b.r[ev[0]] = (ev[1], ev[2])

    def op(self, eng, fn, reads=(), writes=()):
        ex = [b for b in reads if b.excl]
        if ex:
            self._deps(eng, reads, list(writes) + ex)
            writes = list(writes) + ex
            reads = [b for b in reads if not b.excl]
        else:
            self._deps(eng, reads, writes)
        ins = fn(self.E[eng])
        self.cnt[eng] += 1
        ins.then_inc(self.sems[eng], 1)
        self.trace[eng].append(("i", eng, 1))
        self.ninst += 1
        self._commit((eng, self.cnt[eng], eng), reads, writes)

    def dma(self, q, out, in_, reads=(), writes=(), slow=False):
        self._deps(q, reads, writes)
        n = self.NDMA[q]
        s = self.dnext[q]
        self.dnext[q] = (s + 1) % n
        key = (q, s)
        self._wait(q, key, self.dcnt[key])
        if slow:
            ins = self.E[q].dma_start(out=out, in_=in_, allow_slow_non_contiguous=True)
        else:
            ins = self.E[q].dma_start(out=out, in_=in_)
        self.dcnt[key] += 16
        ins.then_inc(self.dsem[key], 16)
        self.trace[q].append(("i", key, 16))
        self.ninst += 1
        self._commit((key, self.dcnt[key], "dma_" + q), reads, writes)

    def barrier(self):
        for e in self.E:
            for k in self.sems:
                if k != e:
                    self._wait(e, k, self.cnt[k])
            for k in self.dsem:
                self._wait(e, k, self.dcnt[k])

    def finish(self):
        for k in self.dsem:
            self._wait("sp", k, self.dcnt[k])
        for k in self.sems:
            self._wait("sp", k, self.cnt[k])

    def simulate(self):
        pc = {e: 0 for e in self.E}
        val = {}
        prog = True
        while prog:
            prog = False
            for e in self.E:
                tr = self.trace[e]
                while pc[e] < len(tr):
                    k, key, v = tr[pc[e]]
                    if k == "w":
                        if val.get(key, 0) >= v:
                            pc[e] += 1
                            prog = True
                        else:
                            break
                    else:
                        val[key] = val.get(key, 0) + v
                        pc[e] += 1
                        prog = True
        stuck = {e: (pc[e], len(self.trace[e]), self.trace[e][pc[e]] if pc[e] < len(self.trace[e]) else None) for e in self.E}
        return all(pc[e] == len(self.trace[e]) for e in self.E), stuck

    def _psinit(self):
        if not self.psb:
            self.pst = [self.nc.alloc_psum_tensor("psp%d" % i, [128, 1024], F32) for i in range(4)]
            for i in range(8):
                self.psb.append((self.pst[i // 2][:, (i % 2) * 512:(i % 2 + 1) * 512], Buf("ps%d" % i, excl=True)))

    def psum(self):
        self._psinit()
        r = self.psb[self.psi]
        self.psi = (self.psi + 1) % 8
        return r

    def psum2(self, pairs=None, key=None):
        self._psinit()
        if pairs is not None:
            c = self.pctr.get(key, 0)
            self.pctr[key] = c + 1
            i = 2 * pairs[c % len(pairs)]
            return self.pst[i // 2][:, :], [self.psb[i][1], self.psb[i + 1][1]]
        if self.psi % 2:
            self.psi = (self.psi + 1) % 8
        i = self.psi
        self.psi = (self.psi + 2) % 8
        return self.pst[i // 2][:, :], [self.psb[i][1], self.psb[i + 1][1]]


_UID = [0]


def sbt(nc, name, shape, dt):
    _UID[0] += 1
    return nc.sbuf_tensor("%s_u%d" % (name, _UID[0]), shape, dt)


class Ring:
    def __init__(self, nc, stack, name, n, shape, dt):
        self.t = [stack.enter_context(sbt(nc, "%s%d" % (name, i), shape, dt)) for i in range(n)]
        self.b = [Buf("%s%d" % (name, i)) for i in range(n)]
        self.i = 0

    def get(self):
        r = (self.t[self.i], self.b[self.i])
        self.i = (self.i + 1) % len(self.t)
        return r


def tiles_of(n, step):
    return [(s, min(step, n - s)) for s in range(0, n, step)]


class Ctx:
    pass


def build(cfg):
    Tp, Ts = cfg["Tp"], cfg["Ts"]
    dbg = cfg.get("debug", ())
    stop_after = cfg.get("stop_after", None)
    NT = 2 * Tp + 2 * Ts
    seqs = [(0, Tp, True), (Tp, Tp, True), (2 * Tp, Ts, False), (2 * Tp + Ts, Ts, False)]
    nc = bass.Bass("TRN2", target_bir_lowering=False)
    g = Ctx()
    g.nc, g.cfg, g.NT, g.seqs, g.Tp, g.Ts = nc, cfg, NT, seqs, Tp, Ts
    p = Prog(nc)
    g.p = p

    def din(name, shape, dt=F32):
        return nc.dram_tensor(name, list(shape), dt, kind="ExternalInput").ap()

    def dout(name, shape, dt=F32):
        return nc.dram_tensor(name, list(shape), dt, kind="ExternalOutput").ap()

    def dscr(name, shape, dt):
        kind = "ExternalOutput" if name in dbg else "Internal"
        return nc.dram_tensor(name, list(shape), dt, kind=kind).ap()

    I = Ctx()
    g.I = I
    I.x = din("x_all", [NT, D])
    I.mC = din("st_mC", [2, HM, DK, DV])
    I.mn = din("st_mn", [2, HM, DK])
    I.mm = din("st_mm", [2, HM])
    I.conv = din("st_conv", [2, 3, D])
    I.lruh = din("st_lruh", [2, D])
    I.shift = din("st_shift", [2, D])
    I.S = din("st_S", [2, RH, RN, RN])
    for nm, shp in [("a_w_in", [D, INW]), ("a_b_ig", [1, HM]), ("a_b_fg", [1, HM]), ("a_mlstm_norm", [1, D]),
                    ("a_conv_w", [4, D]), ("a_conv_b", [1, D]), ("a_lru_wa", [16, 128, 128]), ("a_lru_ba", [1, D]),
                    ("a_lru_wx", [16, 128, 128]), ("a_lru_bx", [1, D]), ("a_lru_lambda", [1, D]),
                    ("a_w_out", [2 * D, D]), ("c_mu", [6, D]), ("c_w_r", [D, D]), ("c_w_k", [D, D]),
                    ("c_w_v", [D, D]), ("c_w0", [1, D]), ("c_w1", [D, 96]), ("c_w2", [96, D]), ("c_a0", [1, D]),
                    ("c_a1", [D, 96]), ("c_a2", [96, D]), ("c_g1", [D, 256]), ("c_g2", [256, D]),
                    ("c_k_k", [1, D]), ("c_k_a", [1, D]), ("c_r_k", [1, D]), ("c_gn_g", [1, D]),
                    ("c_gn_b", [1, D]), ("c_w_o", [D, D]), ("ln1_g", [2, D]), ("ln1_b", [2, D]),
                    ("ln2_g", [2, D]), ("ln2_b", [2, D]), ("mlp_w1", [2, D, DFF]), ("mlp_w2", [2, DFF, D])]:
        setattr(I, nm, din(nm, shp))
    I.ident = din("k_ident", [128, 128])
    I.tri = din("k_tri", [128, 128])
    I.ones = din("k_ones", [128, 128])
    I.blk = din("k_blk", [128, 128])

    O = Ctx()
    g.O = O
    O.y = dout("o_y", [NT, D])
    O.mC = dout("o_mC", [4, HM, DK, DV])
    O.mn = dout("o_mn", [4, HM, DK])
    O.mm = dout("o_mm", [4, HM])
    O.conv = dout("o_conv", [4, 3, D])
    O.lruh = dout("o_lruh", [4, D])
    O.shift = dout("o_shift", [4, D])
    O.S = dout("o_S", [4, RH, RN, RN])

    Sx = Ctx()
    g.Sx = Sx
    Sx.XT32 = dscr("s_xt32", [D, NT], F32)
    Sx.QT = dscr("s_qt", [1024, NT], BF16)
    Sx.KT = dscr("s_kt", [1024, NT], BF16)
    Sx.XR = dscr("s_xr", [D, NT], F32)
    Sx.YG = dscr("s_yg", [D, NT], BF16)
    Sx.VT = dscr("s_vtok", [NT, D], BF16)
    Sx.SO = dscr("s_sotok", [NT, D], BF16)
    Sx.KK = dscr("s_ktok", [NT, 1024], BF16)
    Sx.GT = dscr("s_gtok", [NT, 16], F32)
    Sx.CAT = dscr("s_cat", [2 * D, NT], BF16)
    Sx.X1 = dscr("s_x1", [D, NT], F32)
    for nm in ("RR", "KRAW", "VV", "DEC", "AA", "NKK", "BB", "K2", "BONUS"):
        setattr(Sx, nm, dscr("s_" + nm.lower(), [D, NT], F32))
    Sx.GG = dscr("s_gg", [D, NT], BF16)
    Sx.GAM = dscr("s_gam", [D, NT], F32)
    Sx.RR2 = dscr("s_rr2", [D, NT], F32)
    Sx.YT = dscr("s_yt", [NT, D], F32)
    Sx.YP = dscr("s_yp", [NT, D], F32)
    Sx.VTOK = dscr("s_vtok2", [NT, D], BF16)
    Sx.YG2 = dscr("s_yg2", [D, NT], BF16)

    with ExitStack() as gs:
        K = Ctx()
        g.K = K
        K.ident = gs.enter_context(nc.sbuf_tensor("ident", [128, 128], F32))
        K.tri = gs.enter_context(nc.sbuf_tensor("tri", [128, 128], F32))
        K.ones = gs.enter_context(nc.sbuf_tensor("ones", [128, 128], F32))
        K.blk = gs.enter_context(nc.sbuf_tensor("blk", [128, 128], F32))
        K.buf = Buf("consts")
        K.onesD = gs.enter_context(nc.sbuf_tensor("onesD", [128, 128], F32))
        p.op("dve", lambda e: e.memset(K.onesD[:, :], 1.0 / D), writes=[K.buf])
        K.eps_ln = gs.enter_context(nc.sbuf_tensor("eps_ln", [128, 1], F32))
        K.eps_gn = gs.enter_context(nc.sbuf_tensor("eps_gn", [128, 1], F32))
        p.op("dve", lambda e: e.memset(K.eps_ln[:, :], LN_EPS), writes=[K.buf])
        p.op("dve", lambda e: e.memset(K.eps_gn[:, :], GN_EPS), writes=[K.buf])
        for t, src in ((K.ident, I.ident), (K.tri, I.tri), (K.ones, I.ones), (K.blk, I.blk)):
            p.dma("sp", t[:], src[:, :], writes=[K.buf])
        p.barrier()
        if stop_after == "consts":
            p.finish()
            g_last[0] = p
            return nc

        load_params(g, gs)
        stage_inproj(g)
        p.barrier()
        if stop_after == "inproj":
            p.finish()
            g_last[0] = p
            return nc
        stage_mlstm(g)
        p.barrier()
        if stop_after == "mlstm":
            p.finish()
            g_last[0] = p
            return nc
        stage_lru(g)
        p.barrier()
        if stop_after == "lru":
            p.finish()
            g_last[0] = p
            return nc
        stage_tail(g, 0, Sx.CAT, 32, I.a_w_out, Sx.XT32)
        p.barrier()
        if stop_after == "tail0":
            p.finish()
            g_last[0] = p
            return nc
        stage_rwkv_proj(g)
        p.barrier()
        stage_rwkv_prep(g)
        p.barrier()
        if stop_after == "rprep":
            p.finish()
            g_last[0] = p
            return nc
        stage_rwkv_rec(g)
        p.barrier()
        stage_rwkv_post(g)
        p.barrier()
        if stop_after == "rpost":
            p.finish()
            g_last[0] = p
            return nc
        stage_tail(g, 1, Sx.YG2, 16, I.c_w_o, Sx.X1)
        p.barrier()
    p.finish()
    g_last[0] = p
    return nc


g_last = [None]


def fm(ap, c=128):
    return ap.rearrange("(c p) t -> p c t", p=c)


def stage_inproj(g):
    nc, p, I, Sx, K, NT = g.nc, g.p, g.I, g.Sx, g.K, g.NT
    with ExitStack() as st:
        xin = Ring(nc, st, "xin", 2, [128, D], F32)
        xt32 = Ring(nc, st, "xt32", 2, [128, NCH, 512], F32)
        xtb = Ring(nc, st, "xtb", 2, [128, NCH, 512], BF16)
        wf = Ring(nc, st, "wf", 2, [128, NCH, 512], BF16)
        wt = Ring(nc, st, "wt", 2, [128, NCH, 256], BF16)
        ob = Ring(nc, st, "ob", 4, [128, 512], BF16)
        of = Ring(nc, st, "of", 3, [128, 512], F32)
        og = Ring(nc, st, "og", 2, [128, 16], F32)
        win = I.a_w_in.rearrange("(kc p) n -> p kc n", p=128)
        XT32v = fm(Sx.XT32)
        for (t0, n) in tiles_of(NT, 512):
            x32, x32b = xt32.get()
            xb, xbb = xtb.get()
            for (s0, m) in tiles_of(n, 128):
                xi, xib = xin.get()
                p.dma("sp", xi[:m, :], I.x[t0 + s0:t0 + s0 + m, :], writes=[xib])
                for c4 in range(4):
                    ps, psb = p.psum()
                    for j in range(4):
                        c = c4 * 4 + j
                        p.op("pe", lambda e, c=c, j=j: e.transpose(ps[:, j * 128:j * 128 + m], xi[:m, c * 128:(c + 1) * 128],
                                                                K.ident[:m, :m]),
                             reads=[xib, K.buf], writes=[psb])
                    src = ps[:, :].rearrange("p (j t) -> p j t", j=4)[:, :, :m]
                    if "noact" not in g.cfg.get("flags", ""):
                        p.op("act", lambda e, c4=c4, src=src: e.copy(out=x32[:, c4 * 4:c4 * 4 + 4, s0:s0 + m], in_=src),
                             reads=[psb], writes=[x32b])
                    if "nodve" not in g.cfg.get("flags", ""):
                        p.op("dve", lambda e, c4=c4, src=src: e.tensor_copy(out=xb[:, c4 * 4:c4 * 4 + 4, s0:s0 + m], in_=src),
                             reads=[psb], writes=[xbb])
            if "nostore" in g.cfg.get("flags", ""):
                pass
            elif g.cfg.get("split_store", True):
                for c in range(NCH):
                    p.dma("sp", Sx.XT32[c * 128:(c + 1) * 128, t0:t0 + n], x32[:, c, :n], reads=[x32b])
            else:
                p.dma("sp", XT32v[:, :, t0:t0 + n], x32[:, :, :n], reads=[x32b])
            parts = g.cfg.get("parts", "tr,fm,tm")
            fm_jobs = ([("q", C_Q + 128 * i, i) for i in range(8)] + [("k", C_K + 128 * i, i) for i in range(8)] +
                       [("xr", C_XR + 128 * i, i) for i in range(16)] + [("yg", C_YG + 128 * i, i) for i in range(16)])
            for ji, (kind, col, ci) in enumerate(fm_jobs if "fm" in parts else []):
                if ji % 4 == 0:
                    w, wb = wf.get()
                    p.dma("pool", w[:], win[:, :, col:col + 512], writes=[wb])
                wo_ = (ji % 4) * 128
                ps, psb = p.psum()
                for kc in range(NCH):
                    p.op("pe", lambda e, kc=kc, wo_=wo_: e.matmul(ps[:, :n], lhsT=w[:, kc, wo_:wo_ + 128], rhs=xb[:, kc, :n],
                                                                  start=(kc == 0), stop=(kc == NCH - 1)),
                         reads=[wb, xbb], writes=[psb])
                if kind == "q":
                    o, obb = ob.get()
                    p.op("act", lambda e: e.activation(out=o[:, :n], in_=ps[:, :n], func=AF.Copy, scale=float(DK) ** -0.5),
                         reads=[psb], writes=[obb])
                    p.dma("sp", Sx.QT[ci * 128:(ci + 1) * 128, t0:t0 + n], o[:, :n], reads=[obb])
                elif kind == "k":
                    o, obb = ob.get()
                    p.op("dve", lambda e: e.tensor_copy(out=o[:, :n], in_=ps[:, :n]), reads=[psb], writes=[obb])
                    p.dma("sp", Sx.KT[ci * 128:(ci + 1) * 128, t0:t0 + n], o[:, :n], reads=[obb])
                elif kind == "xr":
                    o, obb = of.get()
                    p.op("dve", lambda e: e.tensor_copy(out=o[:, :n], in_=ps[:, :n]), reads=[psb], writes=[obb])
                    p.dma("sp", Sx.XR[ci * 128:(ci + 1) * 128, t0:t0 + n], o[:, :n], reads=[obb])
                else:
                    o, obb = ob.get()
                    p.op("act", lambda e: e.activation(out=o[:, :n], in_=ps[:, :n], func=AF.Gelu_apprx_tanh),
                         reads=[psb], writes=[obb])
                    p.dma("sp", Sx.YG[ci * 128:(ci + 1) * 128, t0:t0 + n], o[:, :n], reads=[obb])
            tm_jobs = ([("v", C_V + 256 * i, 256 * i, 256) for i in range(8)] +
                       [("o", C_O + 256 * i, 256 * i, 256) for i in range(8)] +
                       [("k", C_K + 256 * i, 256 * i, 256) for i in range(4)] + [("g", C_IG, 0, 16)])
            for kind, col, oc, nw in (tm_jobs if "tm" in parts else []):
                w, wb = wt.get()
                p.dma("pool", w[:, :, :nw], win[:, :, col:col + nw], writes=[wb])
                for (s0, m) in tiles_of(n, 128):
                    ps, psb = p.psum()
                    for kc in range(NCH):
                        p.op("pe", lambda e, kc=kc: e.matmul(ps[:m, :nw], lhsT=xb[:, kc, s0:s0 + m], rhs=w[:, kc, :nw],
                                                             start=(kc == 0), stop=(kc == NCH - 1)),
                             reads=[wb, xbb], writes=[psb])
                    r0 = t0 + s0
                    if kind == "g":
                        o, obb = og.get()
                        p.op("dve", lambda e: e.tensor_copy(out=o[:m, :], in_=ps[:m, :16]), reads=[psb], writes=[obb])
                        p.dma("sp", Sx.GT[r0:r0 + m, :], o[:m, :], reads=[obb])
                    elif kind == "o":
                        o, obb = ob.get()
                        p.op("act", lambda e: e.activation(out=o[:m, :nw], in_=ps[:m, :nw], func=AF.Sigmoid),
                             reads=[psb], writes=[obb])
                        p.dma("sp", Sx.SO[r0:r0 + m, oc:oc + nw], o[:m, :nw], reads=[obb])
                    else:
                        o, obb = ob.get()
                        p.op("dve", lambda e: e.tensor_copy(out=o[:m, :nw], in_=ps[:m, :nw]), reads=[psb], writes=[obb])
                        dst = Sx.VT if kind == "v" else Sx.KK
                        p.dma("sp", dst[r0:r0 + m, oc:oc + nw], o[:m, :nw], reads=[obb])


def stage_mlstm(g):
    nc, p, I, O, Sx, K = g.nc, g.p, g.I, g.O, g.Sx, g.K
    with ExitStack() as st:
        sb = lambda name, shape, dt=F32: st.enter_context(sbt(nc, name, shape, dt))
        Cn = sb("Cn", [128, HM, 257])
        Cnb = sb("Cnb", [128, HM, 257], BF16)
        CnB = [Buf("Cn%d" % h) for h in range(HM)]
        CnbB = [Buf("Cnb%d" % h) for h in range(HM)]
        mrun = sb("mrun", [8, 1])
        mrunB = Buf("mrun")
        mnbc = sb("mnbc", [128, D])
        bigbc = sb("bigbc", [128, 16])
        cB = Buf("mconst")
        p.dma("sp", mnbc[:], I.a_mlstm_norm[0:1, :].partition_broadcast(128), writes=[cB])
        p.dma("sp", bigbc[:, 0:8], I.a_b_ig[0:1, :].partition_broadcast(128), writes=[cB])
        p.dma("sp", bigbc[:, 8:16], I.a_b_fg[0:1, :].partition_broadcast(128), writes=[cB])
        em0 = sb("em0", [128, 8])
        em0B = Buf("em0")
        gtr = Ring(nc, st, "gt", 2, [128, 16], F32)
        vxr = Ring(nc, st, "vx", 2, [128, HM, 257], BF16)
        sor = Ring(nc, st, "so", 2, [128, D], BF16)
        ktr = Ring(nc, st, "ktk", 2, [128, 1024], BF16)
        qTr = Ring(nc, st, "qT", 2, [128, HM, 128], BF16)
        kTr = Ring(nc, st, "kT", 2, [128, HM, 128], BF16)
        catr = Ring(nc, st, "catT", 2, [128, 16, 128], BF16)
        for t_, b_ in zip(vxr.t, vxr.b):
            p.op("dve", lambda e, t_=t_: e.memset(t_[:, :, 256:257], 1.0), writes=[b_])
        gw = Ring(nc, st, "gw", 2, [128, 64], F32)
        g8r = Ring(nc, st, "g8", 2, [8, 8], F32)
        PTr = Ring(nc, st, "PT", 8, [128, 128], BF16)
        ksr = Ring(nc, st, "ks", 8, [128, 128], BF16)
        smr = Ring(nc, st, "sm", 8, [128, 16], F32)
        hhr = Ring(nc, st, "hh", 8, [128, 256], F32)
        hmr = Ring(nc, st, "hm", 8, [128, 256], F32)
        QTv = Sx.QT.rearrange("(h p) t -> p h t", p=128)
        KTv = Sx.KT.rearrange("(h p) t -> p h t", p=128)
        for si, (tok0, T, isp) in enumerate(g.seqs):
            L = 128 if T % 128 == 0 else T
            assert T % L == 0 and L <= 128
            if isp:
                for h in range(HM):
                    p.op("dve", lambda e, h=h: e.memset(Cn[:, h, :], 0.0), writes=[CnB[h]])
                    p.op("act", lambda e, h=h: e.copy(out=Cnb[:, h, :], in_=Cn[:, h, :]), reads=[CnB[h]], writes=[CnbB[h]])
                p.op("dve", lambda e: e.memset(mrun[:, :], 0.0), writes=[mrunB])
            else:
                b = si - 2
                p.dma("sp", em0[:, :], I.mm[b:b + 1, :].partition_broadcast(128), writes=[em0B])
                p.op("act", lambda e: e.activation(out=em0[:, :], in_=em0[:, :], func=AF.Exp), reads=[em0B], writes=[em0B])
                p.dma("sp", mrun[:, :], I.mm[b:b + 1, :].rearrange("o h -> h o"), writes=[mrunB])
                for h in range(HM):
                    p.dma("sp", Cn[:, h, 0:256], I.mC[b, h, :, :], writes=[CnB[h]])
                    p.dma("sp", Cn[:, h, 256:257], I.mn[b, h:h + 1, :].rearrange("o k -> k o"), writes=[CnB[h]])
                    p.op("dve", lambda e, h=h: e.tensor_scalar(out=Cn[:, h, :], in0=Cn[:, h, :], scalar1=em0[:, h:h + 1],
                                                                scalar2=None, op0=OP.mult), reads=[CnB[h], em0B], writes=[CnB[h]])
                    p.op("act", lambda e, h=h: e.copy(out=Cnb[:, h, :], in_=Cn[:, h, :]), reads=[CnB[h]], writes=[CnbB[h]])
            for c in range(T // L):
                r0 = tok0 + c * L
                gt, gtB = gtr.get()
                vx, vxB = vxr.get()
                so, soB = sor.get()
                kt, ktB = ktr.get()
                qT, qTB = qTr.get()
                kT, kTB = kTr.get()
                cat, catB = catr.get()
                p.dma("sp", gt[:L, :], Sx.GT[r0:r0 + L, :], writes=[gtB])
                p.dma("sp", vx[:L, :, 0:256], Sx.VT[r0:r0 + L, :].rearrange("t (h v) -> t h v", h=HM), writes=[vxB])
                p.dma("sp", so[:L, :], Sx.SO[r0:r0 + L, :], writes=[soB])
                p.dma("sp", kt[:L, :], Sx.KK[r0:r0 + L, :], writes=[ktB])
                p.dma("sp", qT[:, :, :L], QTv[:, :, r0:r0 + L], writes=[qTB])
                p.dma("sp", kT[:, :, :L], KTv[:, :, r0:r0 + L], writes=[kTB])
                w, wB = gw.get()
                p.op("dve", lambda e: e.tensor_tensor(out=w[:L, 0:16], in0=gt[:L, :], in1=bigbc[:L, :], op=OP.add),
                     reads=[gtB, cB], writes=[wB])
                p.op("act", lambda e: e.activation(out=w[:L, 8:16], in_=w[:L, 8:16], func=AF.Exp, scale=-1.0), reads=[wB], writes=[wB])
                p.op("act", lambda e: e.activation(out=w[:L, 8:16], in_=w[:L, 8:16], func=AF.Ln, bias=1.0), reads=[wB], writes=[wB])
                p.op("dve", lambda e: e.tensor_scalar(out=w[:L, 8:16], in0=w[:L, 8:16], scalar1=-1.0, scalar2=None, op0=OP.mult),
                     reads=[wB], writes=[wB])
                ps, psB = p.psum()
                p.op("pe", lambda e: e.matmul(ps[:L, 0:8], lhsT=K.tri[:L, :L], rhs=w[:L, 8:16], start=True, stop=True),
                     reads=[wB, K.buf], writes=[psB])
                p.op("pe", lambda e: e.matmul(ps[:, 8:16], lhsT=K.ones[:L, :], rhs=w[:L, 8:16], start=True, stop=True),
                     reads=[wB, K.buf], writes=[psB])
                p.op("pe", lambda e: e.matmul(ps[:8, 16:17], lhsT=w[:L, 8:16], rhs=K.ones[:L, 0:1], start=True, stop=True),
                     reads=[wB, K.buf], writes=[psB])
                p.op("dve", lambda e: e.tensor_tensor(out=w[:L, 24:32], in0=w[:L, 0:8], in1=ps[:L, 0:8], op=OP.subtract),
                     reads=[wB, psB], writes=[wB])
                p.op("act", lambda e: e.activation(out=w[:L, 32:40], in_=w[:L, 24:32], func=AF.Exp), reads=[wB], writes=[wB])
                p.op("act", lambda e: e.activation(out=w[:L, 40:48], in_=ps[:L, 0:8], func=AF.Exp, scale=-1.0), reads=[psB], writes=[wB])
                p.op("act", lambda e: e.activation(out=w[:, 48:56], in_=ps[:, 8:16], func=AF.Exp), reads=[psB], writes=[wB])
                g8, g8B = g8r.get()
                p.op("dve", lambda e: e.tensor_copy(out=g8[:, 0:1], in_=ps[:8, 16:17]), reads=[psB], writes=[g8B])
                ps2, ps2B = p.psum()
                p.op("pe", lambda e: e.transpose(ps2[:8, :L], w[:L, 24:32], K.ident[:L, :L]), reads=[wB, K.buf], writes=[ps2B])
                p.op("dve", lambda e: e.reduce_max(out=g8[:, 1:2], in_=ps2[:8, :L], axis=AX.X), reads=[ps2B], writes=[g8B])
                p.op("dve", lambda e: e.tensor_tensor(out=g8[:, 2:3], in0=g8[:, 1:2], in1=mrun[:, :], op=OP.max),
                     reads=[g8B, mrunB], writes=[g8B])
                p.op("dve", lambda e: e.tensor_tensor(out=mrun[:, :], in0=g8[:, 2:3], in1=g8[:, 0:1], op=OP.add),
                     reads=[g8B], writes=[mrunB])
                for g0 in range(0, HM, 4):
                    hs = list(range(g0, g0 + 4))
                    R_ = {}
                    for h in hs:
                        pS, pSB = p.psum()
                        p.op("pe", lambda e: e.matmul(pS[:L, :L], lhsT=kT[:, h, :L], rhs=qT[:, h, :L], start=True, stop=True),
                             reads=[kTB, qTB], writes=[pSB])
                        PT, PTB = PTr.get()
                        p.op("dve", lambda e: e.scalar_tensor_tensor(out=PT[:L, :L], in0=pS[:L, :L], scalar=w[:L, 32 + h:33 + h],
                                                                     in1=K.tri[:L, :L], op0=OP.mult, op1=OP.mult),
                             reads=[pSB, wB, K.buf], writes=[PTB])
                        ks, ksB = ksr.get()
                        p.op("dve", lambda e: e.tensor_scalar(out=ks[:L, :], in0=kt[:L, h * 128:(h + 1) * 128],
                                                              scalar1=w[:L, 32 + h:33 + h], scalar2=None, op0=OP.mult),
                             reads=[ktB, wB], writes=[ksB])
                        R_[h] = dict(PT=PT, PTB=PTB, ks=ks, ksB=ksB)
                    for h in hs:
                        r_ = R_[h]
                        pN, pNB = p.psum()
                        p.op("pe", lambda e: e.matmul(pN[:L, 0:257], lhsT=r_["PT"][:L, :L], rhs=vx[:L, h, :], start=True, stop=False),
                             reads=[r_["PTB"], vxB], writes=[pNB])
                        p.op("pe", lambda e: e.matmul(pN[:L, 0:257], lhsT=qT[:, h, :L], rhs=Cnb[:, h, :], start=False, stop=True),
                             reads=[qTB, CnbB[h]], writes=[pNB])
                        r_["pN"], r_["pNB"] = pN, pNB
                    for h in hs:
                        r_ = R_[h]
                        sm, smB = smr.get()
                        pN, pNB = r_["pN"], r_["pNB"]
                        p.op("act", lambda e: e.activation(out=sm[:L, 13:14], in_=pN[:L, 256:257], func=AF.Abs),
                             reads=[pNB], writes=[smB])
                        p.op("dve", lambda e: e.tensor_tensor(out=sm[:L, 0:1], in0=sm[:L, 13:14], in1=w[:L, 40 + h:41 + h], op=OP.max),
                             reads=[smB, wB], writes=[smB])
                        p.op("dve", lambda e: e.reciprocal(out=sm[:L, 1:2], in_=sm[:L, 0:1]), reads=[smB], writes=[smB])
                        r_["sm"], r_["smB"] = sm, smB
                    for h in hs:
                        r_ = R_[h]
                        sm, smB, pN, pNB = r_["sm"], r_["smB"], r_["pN"], r_["pNB"]
                        hh, hhB = hhr.get()
                        p.op("act", lambda e: e.activation(out=hh[:L, :], in_=pN[:L, 0:256], func=AF.Copy, scale=sm[:L, 1:2]),
                             reads=[pNB, smB], writes=[hhB])
                        r_["hh"], r_["hhB"] = hh, hhB
                    for h in hs:
                        r_ = R_[h]
                        pC, pCB = p.psum()
                        p.op("pe", lambda e: e.matmul(pC[:, 0:257], lhsT=r_["ks"][:L, :], rhs=vx[:L, h, :], start=True, stop=True),
                             reads=[r_["ksB"], vxB], writes=[pCB])
                        p.op("dve", lambda e: e.tensor_scalar(out=Cn[:, h, :], in0=Cn[:, h, :], scalar1=w[:, 48 + h:49 + h],
                                                              scalar2=None, op0=OP.mult), reads=[CnB[h], wB], writes=[CnB[h]])
                        p.op("dve", lambda e: e.scalar_tensor_tensor(out=Cn[:, h, :], in0=pC[:, 0:257], scalar=w[:, 48 + h:49 + h],
                                                                     in1=Cn[:, h, :], op0=OP.mult, op1=OP.add),
                             reads=[pCB, wB, CnB[h]], writes=[CnB[h]])
                        p.op("act", lambda e: e.copy(out=Cnb[:, h, :], in_=Cn[:, h, :]), reads=[CnB[h]], writes=[CnbB[h]])
                    for h in hs:
                        r_ = R_[h]
                        sm, smB, hh, hhB = r_["sm"], r_["smB"], r_["hh"], r_["hhB"]
                        p.op("dve", lambda e: e.bn_stats(out=sm[:L, 2:8], in_=hh[:L, :]), reads=[hhB], writes=[smB])
                        p.op("dve", lambda e: e.bn_aggr(out=sm[:L, 8:10], in_=sm[:L, 2:8]), reads=[smB], writes=[smB])
                    for h in hs:
                        r_ = R_[h]
                        sm, smB = r_["sm"], r_["smB"]
                        p.op("act", lambda e: e.activation(out=sm[:L, 10:11], in_=sm[:L, 9:10], func=AF.Sqrt, bias=g.K.eps_ln[:L, :]),
                             reads=[smB, K.buf], writes=[smB])
                    for h in hs:
                        r_ = R_[h]
                        sm, smB = r_["sm"], r_["smB"]
                        p.op("dve", lambda e: e.reciprocal(out=sm[:L, 11:12], in_=sm[:L, 10:11]), reads=[smB], writes=[smB])
                        p.op("dve", lambda e: e.scalar_tensor_tensor(out=sm[:L, 12:13], in0=sm[:L, 8:9], scalar=-1.0,
                                                                     in1=sm[:L, 11:12], op0=OP.mult, op1=OP.mult),
                             reads=[smB], writes=[smB])
                    for h in hs:
                        r_ = R_[h]
                        sm, smB, hh, hhB = r_["sm"], r_["smB"], r_["hh"], r_["hhB"]
                        hm, hmB = hmr.get()
                        p.op("act", lambda e: e.activation(out=hm[:L, :], in_=hh[:L, :], func=AF.Identity,
                                                           scale=sm[:L, 11:12], bias=sm[:L, 12:13]),
                             reads=[hhB, smB], writes=[hmB])
                        r_["hm"], r_["hmB"] = hm, hmB
                    for h in hs:
                        r_ = R_[h]
                        hm, hmB = r_["hm"], r_["hmB"]
                        p.op("dve", lambda e: e.tensor_tensor(out=hm[:L, :], in0=hm[:L, :], in1=mnbc[:L, h * 256:(h + 1) * 256], op=OP.mult),
                             reads=[hmB, cB], writes=[hmB])
                        p.op("dve", lambda e: e.tensor_tensor(out=hm[:L, :], in0=hm[:L, :], in1=so[:L, h * 256:(h + 1) * 256], op=OP.mult),
                             reads=[hmB, soB], writes=[hmB])
                    for h in hs:
                        r_ = R_[h]
                        hm, hmB = r_["hm"], r_["hmB"]
                        pT, pTB = p.psum()
                        for j in range(2):
                            p.op("pe", lambda e, j=j: e.transpose(pT[:, j * 128:j * 128 + L], hm[:L, j * 128:(j + 1) * 128], K.ident[:L, :L]),
                                 reads=[hmB, K.buf], writes=[pTB])
                        p.op("act", lambda e: e.copy(out=cat[:, 2 * h:2 * h + 2, :L],
                                                     in_=pT[:, 0:256].rearrange("p (j t) -> p j t", j=2)[:, :, :L]),
                             reads=[pTB], writes=[catB])
                p.dma("sp", fm(Sx.CAT)[:, 0:16, r0:r0 + L], cat[:, :, :L], reads=[catB])
            oi = si
            g8, g8B = g8r.get()
            p.op("dve", lambda e: e.tensor_scalar(out=g8[:, 0:8], in0=K.ident[:8, :8], scalar1=mrun[:, 0:1], scalar2=None, op0=OP.mult),
                 reads=[mrunB, K.buf], writes=[g8B])
            ps, psB = p.psum()
            p.op("pe", lambda e: e.matmul(ps[:, 0:8], lhsT=K.ones[:8, :], rhs=g8[:, 0:8], start=True, stop=True),
                 reads=[g8B, K.buf], writes=[psB])
            p.op("act", lambda e: e.activation(out=em0[:, :], in_=ps[:, 0:8], func=AF.Exp, scale=-1.0), reads=[psB], writes=[em0B])
            p.dma("sp", O.mm[oi:oi + 1, :].rearrange("o h -> h o"), mrun[:, :], reads=[mrunB])
            for h in range(HM):
                p.op("dve", lambda e, h=h: e.tensor_scalar(out=Cn[:, h, :], in0=Cn[:, h, :], scalar1=em0[:, h:h + 1], scalar2=None,
                                                            op0=OP.mult), reads=[CnB[h], em0B], writes=[CnB[h]])
                p.dma("sp", O.mC[oi, h, :, :], Cn[:, h, 0:256], reads=[CnB[h]])
                p.dma("sp", O.mn[oi, h:h + 1, :].rearrange("o k -> k o"), Cn[:, h, 256:257], reads=[CnB[h]])


PARAMS = ["conv_w0", "conv_w1", "conv_w2", "conv_w3", "conv_b", "lru_ba", "lru_bx", "lru_lam",
          "ln1_g0", "ln1_b0", "ln2_g0", "ln2_b0",
          "mu0", "mu1", "mu2", "mu3", "mu4", "mu5", "w0", "a0", "k_k", "k_a", "r_k", "gn_g", "gn_b",
          "ln1_g1", "ln1_b1", "ln2_g1", "ln2_b1", "shift0", "shift1"]


def load_params(g, stack):
    nc, p, I, K = g.nc, g.p, g.I, g.K
    src = {"conv_w0": I.a_conv_w[0:1, :], "conv_w1": I.a_conv_w[1:2, :], "conv_w2": I.a_conv_w[2:3, :],
           "conv_w3": I.a_conv_w[3:4, :], "conv_b": I.a_conv_b, "lru_ba": I.a_lru_ba, "lru_bx": I.a_lru_bx,
           "lru_lam": I.a_lru_lambda, "ln1_g0": I.ln1_g[0:1, :], "ln1_b0": I.ln1_b[0:1, :], "ln2_g0": I.ln2_g[0:1, :],
           "ln2_b0": I.ln2_b[0:1, :], "w0": I.c_w0, "a0": I.c_a0, "k_k": I.c_k_k, "k_a": I.c_k_a, "r_k": I.c_r_k,
           "gn_g": I.c_gn_g, "gn_b": I.c_gn_b, "ln1_g1": I.ln1_g[1:2, :], "ln1_b1": I.ln1_b[1:2, :],
           "ln2_g1": I.ln2_g[1:2, :], "ln2_b1": I.ln2_b[1:2, :]}
    for j in range(6):
        src["mu%d" % j] = I.c_mu[j:j + 1, :]
    src["shift0"] = I.shift[0:1, :]
    src["shift1"] = I.shift[1:2, :]
    n = len(PARAMS)
    PC = stack.enter_context(nc.sbuf_tensor("PC", [128, n * 16], F32))
    PCB = Buf("PC")
    with ExitStack() as st:
        rows = Ring(nc, st, "prow", 2, [128, 128], F32)
        for g0 in range(0, n, 8):
            names = PARAMS[g0:g0 + 8]
            rt, rb = rows.get()
            for i, nm in enumerate(names):
                p.dma("sp", rt[i * 16:(i + 1) * 16, :], src[nm].rearrange("o (c q) -> (o c) q", q=128), writes=[rb])
            R = len(names) * 16
            ps, psB = p.psum()
            p.op("pe", lambda e: e.transpose(ps[:, :R], rt[:R, :], K.ident[:R, :R]), reads=[rb, K.buf], writes=[psB])
            p.op("dve", lambda e: e.tensor_copy(out=PC[:, g0 * 16:g0 * 16 + R], in_=ps[:, :R]), reads=[psB], writes=[PCB])
        p.barrier()
    g.PC, g.PCB = PC, PCB
    g.pcol = lambda name, c: PC[:, PARAMS.index(name) * 16 + c:PARAMS.index(name) * 16 + c + 1]


def stage_lru(g):
    nc, p, I, O, Sx, K = g.nc, g.p, g.I, g.O, g.Sx, g.K
    pcol, PCB = g.pcol, g.PCB
    Tmax = max(T for _, T, _ in g.seqs)
    with ExitStack() as st:
        sb = lambda name, shape, dt=F32: st.enter_context(sbt(nc, name, shape, dt))
        wa = sb("wa", [128, 16, 128], BF16)
        wx = sb("wx", [128, 16, 128], BF16)
        wB = Buf("lruw")
        p.dma("pool", wa[:], I.a_lru_wa.rearrange("g i j -> i g j"), writes=[wB])
        p.dma("pool", wx[:], I.a_lru_wx.rearrange("g i j -> i g j"), writes=[wB])
        cl = sb("cl", [128, 32])
        clB = Buf("cl")
        lam = g.PC[:, PARAMS.index("lru_lam") * 16:PARAMS.index("lru_lam") * 16 + 16]
        p.op("act", lambda e: e.activation(out=cl[:, 0:16], in_=lam, func=AF.Exp, scale=-1.0), reads=[PCB], writes=[clB])
        p.op("act", lambda e: e.activation(out=cl[:, 0:16], in_=cl[:, 0:16], func=AF.Ln, bias=1.0), reads=[clB], writes=[clB])
        p.op("dve", lambda e: e.tensor_scalar(out=cl[:, 16:32], in0=cl[:, 0:16], scalar1=-16.0, scalar2=None, op0=OP.mult),
             reads=[clB], writes=[clB])
        p.op("dve", lambda e: e.tensor_scalar(out=cl[:, 0:16], in0=cl[:, 0:16], scalar1=-8.0, scalar2=None, op0=OP.mult),
             reads=[clB], writes=[clB])
        xpr = Ring(nc, st, "xp", 2, [128, Tmax + 3], F32)
        ygr = Ring(nc, st, "ygl", 2, [128, Tmax], BF16)
        xcr = Ring(nc, st, "xc", 2, [128, Tmax], F32)
        xcbr = Ring(nc, st, "xcb", 2, [128, Tmax], BF16)
        rr = Ring(nc, st, "rr", 2, [128, Tmax], F32)
        gir = Ring(nc, st, "gi", 2, [128, Tmax], F32)
        mr = Ring(nc, st, "ml", 2, [128, Tmax], F32)
        hsr = Ring(nc, st, "hs", 2, [128, Tmax], F32)
        ybr = Ring(nc, st, "yb", 2, [128, Tmax], BF16)
        h0r = Ring(nc, st, "h0", 2, [128, 1], F32)
        for si, (tok0, T, isp) in enumerate(g.seqs):
            for cc in range(NCH):
                rows = slice(cc * 128, (cc + 1) * 128)
                xp, xpB = xpr.get()
                yg, ygB = ygr.get()
                if isp:
                    p.op("dve", lambda e: e.memset(xp[:, 0:3], 0.0), writes=[xpB])
                else:
                    p.dma("sp", xp[:, 0:3], I.conv[si - 2, :, rows].rearrange("j c -> c j"), writes=[xpB], slow=True)
                p.dma("sp", xp[:, 3:3 + T], Sx.XR[rows, tok0:tok0 + T], writes=[xpB])
                p.dma("sp", yg[:, :T], Sx.YG[rows, tok0:tok0 + T], writes=[ygB])
                xc, xcB = xcr.get()
                p.op("dve", lambda e: e.tensor_scalar(out=xc[:, :T], in0=xp[:, 3:3 + T], scalar1=pcol("conv_w3", cc),
                                                      scalar2=pcol("conv_b", cc), op0=OP.mult, op1=OP.add),
                     reads=[xpB, PCB], writes=[xcB])
                for j in range(3):
                    p.op("dve", lambda e, j=j: e.scalar_tensor_tensor(out=xc[:, :T], in0=xp[:, j:j + T], scalar=pcol("conv_w%d" % j, cc),
                                                                      in1=xc[:, :T], op0=OP.mult, op1=OP.add),
                         reads=[xpB, PCB, xcB], writes=[xcB])
                xcb, xcbB = xcbr.get()
                p.op("act", lambda e: e.copy(out=xcb[:, :T], in_=xc[:, :T]), reads=[xcB], writes=[xcbB])
                r, rB = rr.get()
                gi, giB = gir.get()
                for (c0, n) in tiles_of(T, 512):
                    ps, psB = p.psum()
                    p.op("pe", lambda e: e.matmul(ps[:, :n], lhsT=wa[:, cc, :], rhs=xcb[:, c0:c0 + n], start=True, stop=True),
                         reads=[wB, xcbB], writes=[psB])
                    p.op("act", lambda e: e.activation(out=r[:, c0:c0 + n], in_=ps[:, :n], func=AF.Sigmoid, bias=pcol("lru_ba", cc)),
                         reads=[psB, PCB], writes=[rB])
                    ps2, ps2B = p.psum()
                    p.op("pe", lambda e: e.matmul(ps2[:, :n], lhsT=wx[:, cc, :], rhs=xcb[:, c0:c0 + n], start=True, stop=True),
                         reads=[wB, xcbB], writes=[ps2B])
                    p.op("act", lambda e: e.activation(out=gi[:, c0:c0 + n], in_=ps2[:, :n], func=AF.Sigmoid, bias=pcol("lru_bx", cc)),
                         reads=[ps2B, PCB], writes=[giB])
                ml, mlB = mr.get()
                p.op("act", lambda e: e.activation(out=ml[:, :T], in_=r[:, :T], func=AF.Exp, scale=cl[:, 16 + cc:17 + cc]),
                     reads=[rB, clB], writes=[mlB])
                p.op("act", lambda e: e.activation(out=r[:, :T], in_=r[:, :T], func=AF.Exp, scale=cl[:, cc:cc + 1]),
                     reads=[rB, clB], writes=[rB])
                p.op("dve", lambda e: e.tensor_scalar(out=ml[:, :T], in0=ml[:, :T], scalar1=-1.0, scalar2=1.0, op0=OP.mult, op1=OP.add),
                     reads=[mlB], writes=[mlB])
                p.op("act", lambda e: e.activation(out=ml[:, :T], in_=ml[:, :T], func=AF.Sqrt), reads=[mlB], writes=[mlB])
                if isp:
                    p.op("dve", lambda e: e.memset(ml[:, 0:1], 1.0), reads=[mlB], writes=[mlB])
                p.op("dve", lambda e: e.tensor_tensor(out=gi[:, :T], in0=gi[:, :T], in1=xc[:, :T], op=OP.mult),
                     reads=[giB, xcB], writes=[giB])
                p.op("dve", lambda e: e.tensor_tensor(out=gi[:, :T], in0=gi[:, :T], in1=ml[:, :T], op=OP.mult),
                     reads=[giB, mlB], writes=[giB])
                hs, hsB = hsr.get()
                if isp:
                    p.op("dve", lambda e: e.tensor_tensor_scan(out=hs[:, :T], data0=r[:, :T], data1=gi[:, :T], initial=0.0,
                                                               op0=OP.mult, op1=OP.add), reads=[rB, giB], writes=[hsB])
                else:
                    h0, h0B = h0r.get()
                    p.dma("sp", h0[:, :], I.lruh[si - 2:si - 1, rows].rearrange("o c -> c o"), writes=[h0B], slow=True)
                    p.op("dve", lambda e: e.tensor_tensor_scan(out=hs[:, :T], data0=r[:, :T], data1=gi[:, :T], initial=h0[:, 0:1],
                                                               op0=OP.mult, op1=OP.add), reads=[rB, giB, h0B], writes=[hsB])
                yb, ybB = ybr.get()
                p.op("dve", lambda e: e.tensor_tensor(out=yb[:, :T], in0=hs[:, :T], in1=yg[:, :T], op=OP.mult),
                     reads=[hsB, ygB], writes=[ybB])
                p.dma("sp", Sx.CAT[D + cc * 128:D + (cc + 1) * 128, tok0:tok0 + T], yb[:, :T], reads=[ybB])
                p.dma("sp", O.lruh[si:si + 1, rows].rearrange("o c -> c o"), hs[:, T - 1:T], reads=[hsB], slow=True)
                p.dma("sp", O.conv[si, :, rows].rearrange("j c -> c j"), xp[:, T:T + 3], reads=[xpB], slow=True)


def emit_ln(g, R, z, zB, n, gname, bname, xb=None, xbB=None):
    nc, p, K = g.nc, g.p, g.K
    pcol, PCB = g.pcol, g.PCB
    psM, psMB = p.psum()
    psQ, psQB = p.psum()
    for c in range(NCH):
        sq, sqB = R["sq"].get()
        p.op("act", lambda e, c=c: e.activation(out=sq[:, :n], in_=z[:, c, :n], func=AF.Square), reads=[zB], writes=[sqB])
        p.op("pe", lambda e, c=c: e.matmul(psM[:, :n], lhsT=K.onesD[:, :], rhs=z[:, c, :n], start=(c == 0), stop=(c == NCH - 1)),
             reads=[zB, K.buf], writes=[psMB])
        p.op("pe", lambda e, c=c: e.matmul(psQ[:, :n], lhsT=K.onesD[:, :], rhs=sq[:, :n], start=(c == 0), stop=(c == NCH - 1)),
             reads=[sqB, K.buf], writes=[psQB])
    mean, meanB = R["st"].get()
    rstd, rstdB = R["st"].get()
    p.op("act", lambda e: e.copy(out=mean[:, :n], in_=psM[:, :n]), reads=[psMB], writes=[meanB])
    p.op("dve", lambda e: e.tensor_tensor(out=rstd[:, :n], in0=mean[:, :n], in1=mean[:, :n], op=OP.mult), reads=[meanB], writes=[rstdB])
    p.op("dve", lambda e: e.tensor_tensor(out=rstd[:, :n], in0=psQ[:, :n], in1=rstd[:, :n], op=OP.subtract), reads=[psQB, rstdB], writes=[rstdB])
    p.op("act", lambda e: e.activation(out=rstd[:, :n], in_=rstd[:, :n], func=AF.Sqrt, bias=K.eps_ln[:, :]), reads=[rstdB, K.buf], writes=[rstdB])
    p.op("dve", lambda e: e.reciprocal(out=rstd[:, :n], in_=rstd[:, :n]), reads=[rstdB], writes=[rstdB])
    for c in range(NCH):
        p.op("dve", lambda e, c=c: e.tensor_tensor(out=z[:, c, :n], in0=z[:, c, :n], in1=mean[:, :n], op=OP.subtract),
             reads=[zB, meanB], writes=[zB])
        p.op("dve", lambda e, c=c: e.tensor_tensor(out=z[:, c, :n], in0=z[:, c, :n], in1=rstd[:, :n], op=OP.mult),
             reads=[zB, rstdB], writes=[zB])
        p.op("act", lambda e, c=c: e.activation(out=z[:, c, :n], in_=z[:, c, :n], func=AF.Identity, scale=pcol(gname, c), bias=pcol(bname, c)),
             reads=[zB, PCB], writes=[zB])
        if xb is not None:
            p.op("dve", lambda e, c=c: e.tensor_copy(out=xb[:, c, :n], in_=z[:, c, :n]), reads=[zB], writes=[xbB])


def stage_tail(g, layer, XinT, Kc, W, res_in):
    nc, p, I, O, Sx, K = g.nc, g.p, g.I, g.O, g.Sx, g.K
    NT = g.NT
    with ExitStack() as st:
        sb = lambda name, shape, dt=F32: st.enter_context(sbt(nc, name, shape, dt))
        big = sb("big", [128, 32, 512], BF16)
        bigB = Buf("big")
        z = sb("z", [128, NCH, 512])
        zB = Buf("z")
        xb = sb("x1b", [128, NCH, 512], BF16)
        xbB = Buf("x1b")
        wo = Ring(nc, st, "wo", 2, [128, Kc, 128], BF16)
        w1r = Ring(nc, st, "w1", 2, [128, NCH, 512], BF16)
        w2r = Ring(nc, st, "w2", 2, [128, 32, 256], BF16)
        R = {"sq": Ring(nc, st, "sq", 2, [128, 512], F32), "st": Ring(nc, st, "lnst", 4, [128, 512], F32)}
        rsr = Ring(nc, st, "rs", 2, [128, 512], F32)
        rlr = Ring(nc, st, "rl", 2, [128, 512], F32)
        ytr = Ring(nc, st, "ytk", 1, [128, D], F32)
        Wv = W.rearrange("(kc p) n -> p kc n", p=128)
        W1v = I.mlp_w1[layer].rearrange("(kc p) n -> p kc n", p=128)
        W2v = I.mlp_w2[layer].rearrange("(kc p) n -> p kc n", p=128)
        Xv = fm(XinT)
        sfx = str(layer)
        for (t0, n) in tiles_of(NT, 512):
            p.dma("sp", big[:, :Kc, :n], Xv[:, :, t0:t0 + n], writes=[bigB])
            for c in range(NCH):
                w, wB = wo.get()
                p.dma("pool", w[:], Wv[:, :, c * 128:(c + 1) * 128], writes=[wB])
                rs, rsB = rsr.get()
                p.dma("sp", rs[:, :n], res_in[c * 128:(c + 1) * 128, t0:t0 + n], writes=[rsB])
                ps, psB = p.psum()
                for kc in range(Kc):
                    p.op("pe", lambda e, kc=kc: e.matmul(ps[:, :n], lhsT=w[:, kc, :], rhs=big[:, kc, :n], start=(kc == 0), stop=(kc == Kc - 1)),
                         reads=[wB, bigB], writes=[psB])
                p.op("dve", lambda e, c=c: e.scalar_tensor_tensor(out=z[:, c, :n], in0=rs[:, :n], scalar=ALPHA, in1=ps[:, :n],
                                                                  op0=OP.mult, op1=OP.add), reads=[rsB, psB], writes=[zB])
            emit_ln(g, R, z, zB, n, "ln1_g" + sfx, "ln1_b" + sfx, xb, xbB)
            for half in range(2):
                for hc in range(32):
                    col = (half * 32 + hc) * 128
                    if hc % 4 == 0:
                        w, wB = w1r.get()
                        p.dma("pool", w[:], W1v[:, :, col:col + 512], writes=[wB])
                    wo_ = (hc % 4) * 128
                    ps, psB = p.psum()
                    for kc in range(NCH):
                        p.op("pe", lambda e, kc=kc, wo_=wo_: e.matmul(ps[:, :n], lhsT=w[:, kc, wo_:wo_ + 128], rhs=xb[:, kc, :n], start=(kc == 0), stop=(kc == NCH - 1)),
                             reads=[wB, xbB], writes=[psB])
                    rl, rlB = rlr.get()
                    p.op("act", lambda e: e.activation(out=rl[:, :n], in_=ps[:, :n], func=AF.Relu), reads=[psB], writes=[rlB])
                    p.op("dve", lambda e, hc=hc: e.tensor_tensor(out=big[:, hc, :n], in0=rl[:, :n], in1=rl[:, :n], op=OP.mult),
                         reads=[rlB], writes=[bigB])
                for blk in range(8):
                    w, wB = w2r.get()
                    p.dma("pool", w[:], W2v[:, half * 32:(half + 1) * 32, blk * 256:(blk + 1) * 256], writes=[wB])
                    for j in range(2):
                        c = blk * 2 + j
                        ps, psB = p.psum()
                        for kc in range(32):
                            p.op("pe", lambda e, kc=kc, j=j: e.matmul(ps[:, :n], lhsT=w[:, kc, j * 128:(j + 1) * 128], rhs=big[:, kc, :n],
                                                                      start=(kc == 0), stop=(kc == 31)), reads=[wB, bigB], writes=[psB])
                        if half == 0:
                            p.op("dve", lambda e, c=c: e.scalar_tensor_tensor(out=z[:, c, :n], in0=z[:, c, :n], scalar=ALPHA, in1=ps[:, :n],
                                                                              op0=OP.mult, op1=OP.add), reads=[psB, zB], writes=[zB])
                        else:
                            p.op("dve", lambda e, c=c: e.tensor_tensor(out=z[:, c, :n], in0=z[:, c, :n], in1=ps[:, :n], op=OP.add),
                                 reads=[psB, zB], writes=[zB])
            emit_ln(g, R, z, zB, n, "ln2_g" + sfx, "ln2_b" + sfx)
            if layer == 0:
                for c in range(NCH):
                    p.dma("sp", Sx.X1[c * 128:(c + 1) * 128, t0:t0 + n], z[:, c, :n], reads=[zB])
            else:
                for (s0, m) in tiles_of(n, 128):
                    yt, ytB = ytr.get()
                    for c4 in range(4):
                        ps, psB = p.psum()
                        for j in range(4):
                            c = c4 * 4 + j
                            p.op("pe", lambda e, c=c, j=j: e.transpose(ps[:m, j * 128:(j + 1) * 128], z[:, c, s0:s0 + m], K.ident[:, :]),
                                 reads=[zB, K.buf], writes=[psB])
                        p.op("act", lambda e, c4=c4: e.copy(out=yt[:m, c4 * 512:(c4 + 1) * 512], in_=ps[:m, :]), reads=[psB], writes=[ytB])
                    p.dma("sp", O.y[t0 + s0:t0 + s0 + m, :], yt[:m, :], reads=[ytB])


def stage_rwkv_proj(g):
    nc, p, I, O, Sx, K = g.nc, g.p, g.I, g.O, g.Sx, g.K
    pcol, PCB, PC = g.pcol, g.PCB, g.PC
    NT = g.NT
    with ExitStack() as st:
        sb = lambda name, shape, dt=F32: st.enter_context(sbt(nc, name, shape, dt))
        X = sb("X", [128, NCH, 512])
        XB = Buf("X")
        XX = sb("XX", [128, NCH, 512])
        XXB = Buf("XX")
        xmr = Ring(nc, st, "xm", 2, [128, NCH, 512], BF16)
        wr = Ring(nc, st, "wq", 2, [128, NCH, 512], BF16)
        ofr = Ring(nc, st, "of", 3, [128, 512], F32)
        obr = Ring(nc, st, "ob", 2, [128, 512], BF16)
        lor = Ring(nc, st, "lo", 2, [128, 2, 512], BF16)
        w1 = sb("lw1", [128, NCH, 96], BF16)
        w2 = sb("lw2", [96, D], BF16)
        a1 = sb("la1", [128, NCH, 96], BF16)
        a2 = sb("la2", [96, D], BF16)
        g1 = sb("lg1", [128, NCH, 256], BF16)
        g2 = sb("lg2", [128, 2, D], BF16)
        lB = Buf("lora")
        kc = lambda ap: ap.rearrange("(kc p) n -> p kc n", p=128)
        p.dma("pool", w1[:], kc(I.c_w1), writes=[lB])
        p.dma("pool", w2[:], I.c_w2[:, :], writes=[lB])
        p.dma("pool", a1[:], kc(I.c_a1), writes=[lB])
        p.dma("pool", a2[:], I.c_a2[:, :], writes=[lB])
        p.dma("pool", g1[:], kc(I.c_g1), writes=[lB])
        p.dma("pool", g2[:], kc(I.c_g2), writes=[lB])
        X1v = fm(Sx.X1)
        starts = {tok0: (si, isp) for si, (tok0, T, isp) in enumerate(g.seqs)}
        ends = {tok0 + T - 1: si for si, (tok0, T, isp) in enumerate(g.seqs)}
        zc = sb("zc", [128, NCH])
        zB = Buf("zc")
        p.op("dve", lambda e: e.memset(zc[:, :], 0.0), writes=[zB])
        for (t0, n) in tiles_of(NT, 512):
            p.dma("sp", X[:, :, :n], X1v[:, :, t0:t0 + n], writes=[XB])
            if t0 > 0:
                p.dma("sp", XX[:, :, :n], X1v[:, :, t0 - 1:t0 + n - 1], writes=[XXB])
            else:
                p.dma("sp", XX[:, :, 1:n], X1v[:, :, 0:n - 1], writes=[XXB])
            for ts, (si, isp) in starts.items():
                if t0 <= ts < t0 + n:
                    if isp:
                        src = zc[:, :]
                        rd = [zB]
                    else:
                        i0 = PARAMS.index("shift%d" % (si - 2)) * 16
                        src = PC[:, i0:i0 + 16]
                        rd = [PCB]
                    p.op("dve", lambda e, ts=ts, src=src: e.tensor_copy(out=XX[:, :, ts - t0], in_=src), reads=rd, writes=[XXB])
            for te, si in ends.items():
                if t0 <= te < t0 + n:
                    p.dma("sp", O.shift[si:si + 1, :].rearrange("o (c q) -> q (o c)", q=128), X[:, :, te - t0], reads=[XB], slow=True)
            p.op("dve", lambda e: e.tensor_tensor(out=XX[:, :, :n], in0=XX[:, :, :n], in1=X[:, :, :n], op=OP.subtract),
                 reads=[XXB, XB], writes=[XXB])

            def mix(j):
                xm, xmB = xmr.get()
                for c in range(NCH):
                    p.op("dve", lambda e, c=c: e.scalar_tensor_tensor(out=xm[:, c, :n], in0=XX[:, c, :n], scalar=pcol("mu%d" % j, c),
                                                                      in1=X[:, c, :n], op0=OP.mult, op1=OP.add),
                         reads=[XXB, XB, PCB], writes=[xmB])
                return xm, xmB

            def big_gemm(W, xm, xmB, dst):
                Wv = kc(W)
                for c in range(NCH):
                    if c % 4 == 0:
                        w, wB = wr.get()
                        p.dma("pool", w[:], Wv[:, :, c * 128:c * 128 + 512], writes=[wB])
                    wo_ = (c % 4) * 128
                    ps, psB = p.psum()
                    for k_ in range(NCH):
                        p.op("pe", lambda e, k_=k_, wo_=wo_: e.matmul(ps[:, :n], lhsT=w[:, k_, wo_:wo_ + 128], rhs=xm[:, k_, :n], start=(k_ == 0), stop=(k_ == NCH - 1)),
                             reads=[wB, xmB], writes=[psB])
                    o, oB = ofr.get()
                    p.op("act", lambda e: e.copy(out=o[:, :n], in_=ps[:, :n]), reads=[psB], writes=[oB])
                    p.dma("sp", dst[c * 128:(c + 1) * 128, t0:t0 + n], o[:, :n], reads=[oB])

            def lora_in(wt, width, xm, xmB, func):
                lo, loB = lor.get()
                for (m0, m) in tiles_of(width, 128):
                    ps, psB = p.psum()
                    for k_ in range(NCH):
                        p.op("pe", lambda e, k_=k_: e.matmul(ps[:m, :n], lhsT=wt[:, k_, m0:m0 + m], rhs=xm[:, k_, :n], start=(k_ == 0), stop=(k_ == NCH - 1)),
                             reads=[lB, xmB], writes=[psB])
                    if func is None:
                        p.op("act", lambda e: e.copy(out=lo[:m, m0 // 128, :n], in_=ps[:m, :n]), reads=[psB], writes=[loB])
                    else:
                        p.op("act", lambda e: e.activation(out=lo[:m, m0 // 128, :n], in_=ps[:m, :n], func=func), reads=[psB], writes=[loB])
                return lo, loB

            xm, xmB = mix(0)
            big_gemm(I.c_w_r, xm, xmB, Sx.RR)
            xm, xmB = mix(1)
            lo, loB = lora_in(w1, 96, xm, xmB, AF.Tanh)
            for c in range(NCH):
                ps, psB = p.psum()
                p.op("pe", lambda e, c=c: e.matmul(ps[:, :n], lhsT=w2[:, c * 128:(c + 1) * 128], rhs=lo[:96, 0, :n], start=True, stop=True),
                     reads=[lB, loB], writes=[psB])
                o, oB = ofr.get()
                p.op("act", lambda e, c=c: e.activation(out=o[:, :n], in_=ps[:, :n], func=AF.Sigmoid, bias=pcol("w0", c)),
                     reads=[psB, PCB], writes=[oB])
                p.op("act", lambda e: e.activation(out=o[:, :n], in_=o[:, :n], func=AF.Exp, scale=-float(np.exp(-0.5))), reads=[oB], writes=[oB])
                p.dma("sp", Sx.DEC[c * 128:(c + 1) * 128, t0:t0 + n], o[:, :n], reads=[oB])
            xm, xmB = mix(2)
            big_gemm(I.c_w_k, xm, xmB, Sx.KRAW)
            xm, xmB = mix(3)
            big_gemm(I.c_w_v, xm, xmB, Sx.VV)
            xm, xmB = mix(4)
            lo, loB = lora_in(a1, 96, xm, xmB, None)
            for c in range(NCH):
                ps, psB = p.psum()
                p.op("pe", lambda e, c=c: e.matmul(ps[:, :n], lhsT=a2[:, c * 128:(c + 1) * 128], rhs=lo[:96, 0, :n], start=True, stop=True),
                     reads=[lB, loB], writes=[psB])
                o, oB = ofr.get()
                p.op("act", lambda e, c=c: e.activation(out=o[:, :n], in_=ps[:, :n], func=AF.Sigmoid, bias=pcol("a0", c)),
                     reads=[psB, PCB], writes=[oB])
                p.dma("sp", Sx.AA[c * 128:(c + 1) * 128, t0:t0 + n], o[:, :n], reads=[oB])
            xm, xmB = mix(5)
            lo, loB = lora_in(g1, 256, xm, xmB, AF.Sigmoid)
            for c in range(NCH):
                ps, psB = p.psum()
                for k_ in range(2):
                    p.op("pe", lambda e, c=c, k_=k_: e.matmul(ps[:, :n], lhsT=g2[:, k_, c * 128:(c + 1) * 128], rhs=lo[:, k_, :n],
                                                              start=(k_ == 0), stop=(k_ == 1)), reads=[lB, loB], writes=[psB])
                o, oB = obr.get()
                p.op("act", lambda e: e.copy(out=o[:, :n], in_=ps[:, :n]), reads=[psB], writes=[oB])
                p.dma("sp", Sx.GG[c * 128:(c + 1) * 128, t0:t0 + n], o[:, :n], reads=[oB])


def stage_rwkv_prep(g):
    nc, p, I, O, Sx, K = g.nc, g.p, g.I, g.O, g.Sx, g.K
    pcol, PCB, PC = g.pcol, g.PCB, g.PC
    NT = g.NT
    with ExitStack() as st:
        sb = lambda name, shape, dt=F32: st.enter_context(sbt(nc, name, shape, dt))
        omka = sb("omka", [128, NCH])
        omB = Buf("omka")
        i0 = PARAMS.index("k_a") * 16
        p.op("dve", lambda e: e.tensor_scalar(out=omka[:, :], in0=PC[:, i0:i0 + 16], scalar1=-1.0, scalar2=1.0, op0=OP.mult, op1=OP.add),
             reads=[PCB], writes=[omB])
        ring = lambda nm, k=2: Ring(nc, st, nm, k, [128, 512], F32)
        kr, ar, rr_, vr = ring("pk"), ring("pa"), ring("pr"), ring("pv")
        kkr, sqr, k2r, bbr, nkr, bor = ring("pkk"), ring("psq"), ring("pk2"), ring("pbb"), ring("pnk"), ring("pbo")
        vtk = sb("vtk", [128, 4, D], BF16)
        vtkB = Buf("vtk")
        dcr, gmr, gpr = ring("pdc"), ring("pgm"), ring("pgp")
        zer = sb("zer", [128, RB])
        zerB = Buf("zer")
        p.op("dve", lambda e: e.memset(zer[:, :], 0.0), writes=[zerB])
        for (t0, n) in tiles_of(NT, 512):
            for c in range(NCH):
                rows = slice(c * 128, (c + 1) * 128)
                k, kB = kr.get()
                a, aB = ar.get()
                r, rB = rr_.get()
                v, vB = vr.get()
                p.dma("sp", k[:, :n], Sx.KRAW[rows, t0:t0 + n], writes=[kB])
                p.dma("sp", a[:, :n], Sx.AA[rows, t0:t0 + n], writes=[aB])
                p.dma("sp", r[:, :n], Sx.RR[rows, t0:t0 + n], writes=[rB])
                p.dma("sp", v[:, :n], Sx.VV[rows, t0:t0 + n], writes=[vB])
                kk, kkB = kkr.get()
                sq, sqB = sqr.get()
                p.op("dve", lambda e: e.tensor_scalar(out=kk[:, :n], in0=k[:, :n], scalar1=pcol("k_k", c), scalar2=None, op0=OP.mult),
                     reads=[kB, PCB], writes=[kkB])
                p.op("act", lambda e: e.activation(out=sq[:, :n], in_=kk[:, :n], func=AF.Square), reads=[kkB], writes=[sqB])
                ps, psB = p.psum()
                p.op("pe", lambda e: e.matmul(ps[:, :n], lhsT=K.blk[:, :], rhs=sq[:, :n], start=True, stop=True), reads=[sqB, K.buf], writes=[psB])
                p.op("dve", lambda e: e.tensor_scalar(out=sq[:, :n], in0=ps[:, :n], scalar1=1e-24, scalar2=None, op0=OP.max),
                     reads=[psB], writes=[sqB])
                p.op("act", lambda e: e.activation(out=sq[:, :n], in_=sq[:, :n], func=AF.Sqrt), reads=[sqB], writes=[sqB])
                p.op("dve", lambda e: e.reciprocal(out=sq[:, :n], in_=sq[:, :n]), reads=[sqB], writes=[sqB])
                p.op("dve", lambda e: e.tensor_tensor(out=kk[:, :n], in0=kk[:, :n], in1=sq[:, :n], op=OP.mult), reads=[kkB, sqB], writes=[kkB])
                bb, bbB = bbr.get()
                nk, nkB = nkr.get()
                p.op("dve", lambda e: e.tensor_tensor(out=bb[:, :n], in0=kk[:, :n], in1=a[:, :n], op=OP.mult), reads=[kkB, aB], writes=[bbB])
                p.op("act", lambda e: e.activation(out=nk[:, :n], in_=kk[:, :n], func=AF.Copy, scale=-1.0), reads=[kkB], writes=[nkB])
                k2, k2B = k2r.get()
                p.op("dve", lambda e: e.tensor_scalar(out=k2[:, :n], in0=a[:, :n], scalar1=pcol("k_a", c), scalar2=omka[:, c:c + 1],
                                                      op0=OP.mult, op1=OP.add), reads=[aB, PCB, omB], writes=[k2B])
                p.op("dve", lambda e: e.tensor_tensor(out=k2[:, :n], in0=k2[:, :n], in1=k[:, :n], op=OP.mult), reads=[k2B, kB], writes=[k2B])
                bo, boB = bor.get()
                p.op("dve", lambda e: e.scalar_tensor_tensor(out=bo[:, :n], in0=r[:, :n], scalar=pcol("r_k", c), in1=k2[:, :n],
                                                             op0=OP.mult, op1=OP.mult), reads=[rB, k2B, PCB], writes=[boB])
                ps2, ps2B = p.psum()
                p.op("pe", lambda e: e.matmul(ps2[:, :n], lhsT=K.blk[:, :], rhs=bo[:, :n], start=True, stop=True), reads=[boB, K.buf], writes=[ps2B])
                p.op("dve", lambda e: e.tensor_tensor(out=bo[:, :n], in0=ps2[:, :n], in1=v[:, :n], op=OP.mult), reads=[ps2B, vB], writes=[boB])
                p.dma("sp", Sx.BONUS[rows, t0:t0 + n], bo[:, :n], reads=[boB])
                dc, dcB = dcr.get()
                gm, gmB = gmr.get()
                gp, gpB = gpr.get()
                p.dma("sp", dc[:, :n], Sx.DEC[rows, t0:t0 + n], writes=[dcB])
                for c0 in range(0, n, RB):
                    p.op("dve", lambda e, c0=c0: e.tensor_tensor_scan(out=gm[:, c0:c0 + RB], data0=dc[:, c0:c0 + RB], data1=zer[:, :RB], initial=1.0,
                                                                      op0=OP.mult, op1=OP.add), reads=[dcB, zerB], writes=[gmB])
                b3 = lambda a: a.rearrange("p (b t) -> p b t", t=RB)
                p.op("act", lambda e: e.copy(out=b3(gp[:, :n])[:, :, 1:RB], in_=b3(gm[:, :n])[:, :, 0:RB - 1]), reads=[gmB], writes=[gpB])
                p.op("dve", lambda e: e.memset(b3(gp[:, :n])[:, :, 0:1], 1.0), writes=[gpB])
                p.op("dve", lambda e: e.tensor_tensor(out=nk[:, :n], in0=nk[:, :n], in1=gp[:, :n], op=OP.mult), reads=[nkB, gpB], writes=[nkB])
                p.op("dve", lambda e: e.tensor_tensor(out=r[:, :n], in0=r[:, :n], in1=gm[:, :n], op=OP.mult), reads=[rB, gmB], writes=[rB])
                p.dma("sp", Sx.GAM[rows, t0:t0 + n], gm[:, :n], reads=[gmB])
                p.op("dve", lambda e: e.reciprocal(out=gp[:, :n], in_=gm[:, :n]), reads=[gmB], writes=[gpB])
                p.op("dve", lambda e: e.tensor_tensor(out=bb[:, :n], in0=bb[:, :n], in1=gp[:, :n], op=OP.mult), reads=[bbB, gpB], writes=[bbB])
                p.op("dve", lambda e: e.tensor_tensor(out=k2[:, :n], in0=k2[:, :n], in1=gp[:, :n], op=OP.mult), reads=[k2B, gpB], writes=[k2B])
                p.dma("sp", Sx.NKK[rows, t0:t0 + n], nk[:, :n], reads=[nkB])
                p.dma("sp", Sx.BB[rows, t0:t0 + n], bb[:, :n], reads=[bbB])
                p.dma("sp", Sx.K2[rows, t0:t0 + n], k2[:, :n], reads=[k2B])
                p.dma("sp", Sx.RR2[rows, t0:t0 + n], r[:, :n], reads=[rB])
                psT, psTB = p.psum()
                subs = tiles_of(n, 128)
                for si_, (s0, m) in enumerate(subs):
                    p.op("pe", lambda e, s0=s0, m=m, si_=si_: e.transpose(psT[:m, si_ * 128:(si_ + 1) * 128], v[:, s0:s0 + m], K.ident[:, :]),
                         reads=[vB, K.buf], writes=[psTB])
                for si_, (s0, m) in enumerate(subs):
                    p.op("act", lambda e, m=m, si_=si_, c=c: e.copy(out=vtk[:m, si_, c * 128:(c + 1) * 128], in_=psT[:m, si_ * 128:(si_ + 1) * 128]),
                         reads=[psTB], writes=[vtkB])
            for si_, (s0, m) in enumerate(tiles_of(n, 128)):
                p.dma("sp", Sx.VTOK[t0 + s0:t0 + s0 + m, :], vtk[:m, si_, :], reads=[vtkB])


def stage_rwkv_rec(g):
    nc, p, I, O, Sx, K = g.nc, g.p, g.I, g.O, g.Sx, g.K
    with ExitStack() as st:
        sb = lambda name, shape, dt=F32: st.enter_context(sbt(nc, name, shape, dt))
        ST = sb("ST", [128, 32, 64])
        STB = [Buf("ST0"), Buf("ST1")]
        blkb = sb("blkb", [128, 128], BF16)
        sel = sb("sel", [128, 2], BF16)
        idb = sb("idb", [128, 128], BF16)
        MK = sb("MK", [32, 64])
        cB = Buf("rc")
        p.op("dve", lambda e: e.tensor_copy(out=blkb[:, :], in_=K.blk[:, :]), reads=[K.buf], writes=[cB])
        p.op("dve", lambda e: e.tensor_copy(out=sel[:, 0:1], in_=K.blk[:, 0:1]), reads=[K.buf], writes=[cB])
        p.op("dve", lambda e: e.tensor_copy(out=sel[:, 1:2], in_=K.blk[:, 64:65]), reads=[K.buf], writes=[cB])
        p.op("dve", lambda e: e.tensor_copy(out=idb[:, :], in_=K.ident[:, :]), reads=[K.buf], writes=[cB])
        p.op("dve", lambda e: e.tensor_tensor(out=MK[:, 0:32], in0=K.tri[:32, :32], in1=K.ident[:32, :32], op=OP.subtract), reads=[K.buf], writes=[cB])
        p.op("dve", lambda e: e.tensor_copy(out=MK[:, 32:64], in_=K.tri[:32, :32]), reads=[K.buf], writes=[cB])
        p.op("dve", lambda e: e.memset(MK[:, 63:64], 0.0), reads=[cB], writes=[cB])
        OH = sb("OH", [32, 2, 32, 128], BF16)
        p.op("dve", lambda e: e.memset(OH[:, :, :, :], 0.0), writes=[cB])
        for h2 in range(2):
            p.op("dve", lambda e, h2=h2: e.tensor_copy(out=OH[:, h2, :, h2 * 64:(h2 + 1) * 64],
                                                        in_=K.ident[:32, :32].unsqueeze(2).to_broadcast([32, 32, 64])),
                 reads=[K.buf, cB], writes=[cB])
        k2m_r = Ring(nc, st, "k2m", 1, [128, 2, 32, RB], BF16)
        nrb_r = Ring(nc, st, "nrb", 1, [128, 2, 32, RB], BF16)
        TB = RB
        names = ("GAM", "BB", "K2", "RR2")
        blk_r = {nm: Ring(nc, st, "bt" + nm, 2, [128, 32, TB], F32) for nm in names}
        cmb_r = Ring(nc, st, "btCMB", 2, [128, 2, 32, TB], F32)
        for t_, b_ in zip(cmb_r.t, cmb_r.b):
            p.op("dve", lambda e, t_=t_: e.memset(t_[:, :, :, :], 0.0), writes=[b_])
        vblk_r = Ring(nc, st, "vblk", 2, [32, 2, D], BF16)
        gm_r = Ring(nc, st, "Gm", 2, [32, 32, 64], BF16)
        sap_r = Ring(nc, st, "saP", 2, [32, 2, D], BF16)
        yp_r = Ring(nc, st, "yPs", 1, [32, D], F32)
        ktok_r = Ring(nc, st, "ktok", 2, [32, D], BF16)
        pl_r = Ring(nc, st, "PLs", 2, [128, 2, 1024], F32)
        ayr = Ring(nc, st, "ay", 2, [128, 2, 1024], BF16)
        tBr = Ring(nc, st, "tB", 2, [128, 1024], F32)
        TS = 2
        ysr = Ring(nc, st, "ys", 2, [2, 2, TS, 1024], F32)
        sior = Ring(nc, st, "sio", 2, [64, 2, 64], F32)
        v3 = lambda t2: t2.rearrange("p (j v) -> p j v", v=64)
        RRv = fm(Sx.RR)
        RR2v = fm(Sx.RR2)
        NKv = fm(Sx.NKK)
        for grp, (sis, T) in enumerate((([0, 1], g.Tp), ([2, 3], g.Ts))):
            toks = [g.seqs[si][0] for si in sis]
            assert T % TB == 0
            if grp == 0:
                for hf in range(2):
                    p.op("dve", lambda e, hf=hf: e.memset(ST[:, hf * 16:(hf + 1) * 16, :], 0.0), writes=[STB[hf]])
            else:
                for hf in range(2):
                    for j in range(16):
                        sio, sioB = sior.get()
                        p.dma("sp", sio[:, :, :], I.S[hf, 2 * j:2 * j + 2, :, :].rearrange("h v k -> v h k"), writes=[sioB])
                        ps, psB = p.psum()
                        p.op("pe", lambda e: e.transpose(ps[:, 0:64], sio[:, :, :].rearrange("v h k -> v (h k)"), K.ident[:64, :64]),
                             reads=[sioB, K.buf], writes=[psB])
                        p.op("act", lambda e, hf=hf, j=j: e.copy(out=ST[:, hf * 16 + j, :], in_=ps[:, 0:64]), reads=[psB], writes=[STB[hf]])
            ys_slots = {}
            sth = lambda hf: ST[:, hf * 16:(hf + 1) * 16, :]

            def y_flush(ty):
                if ty % TS == TS - 1 or ty == T - 1:
                    nts = ty % TS + 1
                    tf = ty - nts + 1
                    ys, ysB = ys_slots[tf // TS]
                    for hf in range(2):
                        dst = Sx.YT[toks[hf] + tf:toks[hf] + tf + nts, :].rearrange("t (j h v) -> h t j v", h=2, v=64)
                        p.dma("sp", dst, ys[:, hf, :nts, :].rearrange("h t (j v) -> h t j v", v=64), reads=[ysB])
                    del ys_slots[tf // TS]

            def emit_y(ay, ayB, hf, ty):
                if ty // TS not in ys_slots:
                    ys_slots[ty // TS] = ysr.get()
                ys, ysB = ys_slots[ty // TS]
                ps, psBs = p.psum2([2], "recY")
                for k_ in range(2):
                    p.op("pe", lambda e, k_=k_: e.matmul(ps[:2, k_ * 512:(k_ + 1) * 512], lhsT=sel[:, :], rhs=ay[:, 1, k_ * 512:(k_ + 1) * 512],
                                                         start=True, stop=True), reads=[cB, ayB], writes=[psBs[k_]])
                p.op("act", lambda e: e.copy(out=ys[:, hf, ty % TS, :], in_=ps[:2, :]), reads=psBs, writes=[ysB])

            def prologue(tb0):
                B = {}
                for nm in names:
                    t_, b_ = blk_r[nm].get()
                    src = fm(getattr(Sx, nm))
                    for hf in range(2):
                        p.dma("sp", t_[:, hf * 16:(hf + 1) * 16, :TB], src[:, :, toks[hf] + tb0:toks[hf] + tb0 + TB], writes=[b_])
                    B[nm] = (t_, b_)
                cm, cmB = cmb_r.get()
                for hf in range(2):
                    p.dma("sp", cm[:, 0, hf * 16:(hf + 1) * 16, :TB], NKv[:, :, toks[hf] + tb0:toks[hf] + tb0 + TB], writes=[cmB])
                    p.dma("sp", cm[:, 1, hf * 16:(hf + 1) * 16, 1:TB], RR2v[:, :, toks[hf] + tb0:toks[hf] + tb0 + TB - 1], writes=[cmB])
                    if tb0 > 0:
                        p.dma("sp", cm[:, 1, hf * 16:(hf + 1) * 16, 0:1], RRv[:, :, toks[hf] + tb0 - 1:toks[hf] + tb0], writes=[cmB], slow=True)
                B["cm"] = (cm, cmB)
                vb, vbB = vblk_r.get()
                for hf in range(2):
                    p.dma("sp", vb[:, hf, :], Sx.VTOK[toks[hf] + tb0:toks[hf] + tb0 + TB, :], writes=[vbB])
                sap, sapB = sap_r.get()
                pl, plB = pl_r.get()
                B["sap"] = (sap, sapB)
                B["pl"] = (pl, plB)
                yield
                k2t, k2B = B["K2"]
                r2t, r2B = B["RR2"]
                k2m, k2mB = k2m_r.get()
                nrb, nrbB = nrb_r.get()
                for h2 in range(2):
                    p.op("act", lambda e, h2=h2: e.activation(out=k2m[:, h2, :, :], in_=k2t[:, :, :], func=AF.Copy, scale=K.blk[:, h2 * 64:h2 * 64 + 1]),
                         reads=[k2B, K.buf], writes=[k2mB])
                p.op("act", lambda e: e.copy(out=nrb[:, 0, :, :], in_=cm[:, 0, :, :]), reads=[cmB], writes=[nrbB])
                p.op("act", lambda e: e.copy(out=nrb[:, 1, :, :], in_=r2t[:, :, :]), reads=[r2B], writes=[nrbB])
                yield
                for hf in range(2):
                    gm, gmB = gm_r.get()
                    for q in range(2):
                        ps, psBs = p.psum2([3], "recP")
                        for hh in range(16):
                            h = q * 16 + hh
                            j, h2 = h // 2, h % 2
                            rows = slice(h2 * 64, (h2 + 1) * 64)
                            bj = hf * 16 + j
                            p.op("pe", lambda e, hh=hh, h2=h2, bj=bj: e.matmul(ps[:32, hh * 64:hh * 64 + 32], lhsT=k2m[:, h2, bj, :], rhs=nrb[:, 0, bj, :],
                                                                              start=True, stop=True), reads=[k2mB, nrbB], writes=[psBs[hh // 8]])
                            p.op("pe", lambda e, hh=hh, h2=h2, bj=bj: e.matmul(ps[:32, hh * 64 + 32:hh * 64 + 64], lhsT=k2m[:, h2, bj, :], rhs=nrb[:, 1, bj, :],
                                                                              start=True, stop=True), reads=[k2mB, nrbB], writes=[psBs[hh // 8]])
                            if hh % 4 == 3:
                                yield
                        p.op("dve", lambda e, q=q, ps=ps: e.tensor_tensor(out=gm[:, q * 16:(q + 1) * 16, :], in0=ps[:32, :].rearrange("s (h t) -> s h t", t=64),
                                                                         in1=MK[:, :].unsqueeze(1).to_broadcast([32, 16, 64]), op=OP.mult),
                             reads=psBs + [cB], writes=[gmB])
                        yield
                    for which in range(2):
                        if which == 1:
                            yps, ypsB = yp_r.get()
                        for q in range(2):
                            ps, psBs = p.psum2([3], "recP")
                            for hh in range(16):
                                h = q * 16 + hh
                                p.op("pe", lambda e, hh=hh, h=h: e.matmul(ps[:32, hh * 64:(hh + 1) * 64], lhsT=gm[:, h, which * 32:(which + 1) * 32],
                                                                          rhs=vb[:, hf, h * 64:(h + 1) * 64], start=True, stop=True),
                                     reads=[gmB, vbB], writes=[psBs[hh // 8]])
                                if hh % 8 == 7:
                                    yield
                            if which == 0:
                                p.op("act", lambda e, q=q, ps=ps: e.copy(out=sap[:, hf, q * 1024:(q + 1) * 1024], in_=ps[:32, :]), reads=psBs, writes=[sapB])
                            else:
                                p.op("act", lambda e, q=q, ps=ps: e.copy(out=yps[:, q * 1024:(q + 1) * 1024], in_=ps[:32, :]), reads=psBs, writes=[ypsB])
                            yield
                        if which == 1:
                            p.dma("sp", Sx.YP[toks[hf] + tb0:toks[hf] + tb0 + TB, :], yps[:, :], reads=[ypsB])
                    kt, ktB = ktok_r.get()
                    for q in range(2):
                        ps, psBs = p.psum2([3], "recP")
                        for jj in range(8):
                            j = q * 8 + jj
                            p.op("pe", lambda e, jj=jj, j=j: e.transpose(ps[:32, jj * 128:(jj + 1) * 128], k2t[:, hf * 16 + j, :], K.ident[:, :]),
                                 reads=[k2B, K.buf], writes=[psBs[jj // 4]])
                        p.op("act", lambda e, q=q, ps=ps: e.copy(out=kt[:, q * 1024:(q + 1) * 1024], in_=ps[:32, :]), reads=psBs, writes=[ktB])
                        yield
                    ps, psBs = p.psum2([3], "recP")
                    for j in range(16):
                        for h2 in range(2):
                            p.op("pe", lambda e, j=j, h2=h2: e.matmul(ps[h2 * 64:(h2 + 1) * 64, j * 64:(j + 1) * 64],
                                                                      lhsT=kt[:, j * 128 + h2 * 64:j * 128 + (h2 + 1) * 64],
                                                                      rhs=vb[:, hf, (2 * j + h2) * 64:(2 * j + h2 + 1) * 64], start=True, stop=True),
                                 reads=[ktB, vbB], writes=[psBs[j // 8]])
                        if j % 4 == 3:
                            yield
                    p.op("act", lambda e, ps=ps: e.copy(out=pl[:, hf, :], in_=ps), reads=psBs, writes=[plB])
                    yield
                self_out.append(B)

            nblk = T // TB
            self_out = []
            gen = prologue(0)
            for _ in gen:
                pass
            for bi in range(nblk):
                tb0 = bi * TB
                B = self_out.pop(0)
                gen = prologue(tb0 + TB) if bi + 1 < nblk else iter(())
                sa_next = {}
                bt = B
                cm, cmB = B["cm"]
                sap, sapB = B["sap"]
                pl, plB = B["pl"]
                for tt in range(TB):
                    t = tb0 + tt

                    def col(nm, hf, j0=0, nj=16):
                        t_ = bt[nm][0]
                        return t_[:, hf * 16 + j0:hf * 16 + j0 + nj, tt:tt + 1].to_broadcast([128, nj, 64])

                    def onehots(tt_):
                        for hf in range(2):
                            ps, psBs = p.psum2([0, 1], "recS")
                            for k_ in range(2):
                                for h2 in range(2):
                                    rhs = sap[:, hf, :].rearrange("t (j h v) -> t j h v", h=2, v=64)[:, k_ * 8:(k_ + 1) * 8, h2, :]
                                    p.op("pe", lambda e, h2=h2, rhs=rhs, k_=k_, ps=ps: e.matmul(ps[:, k_ * 512:(k_ + 1) * 512], lhsT=OH[:, h2, tt_, :],
                                                                                             rhs=rhs, start=(h2 == 0), stop=False),
                                         reads=[cB, sapB], writes=[psBs[k_]])
                            sa_next[hf] = (ps, psBs)

                    if tt == 0:
                        onehots(0)
                    pend_sa = {}
                    for hf in range(2):
                        ay, ayB = ayr.get()
                        p.op("dve", lambda e, hf=hf: e.tensor_tensor(
                            out=ay[:, :, :].rearrange("p x (j v) -> p x j v", v=64),
                            in0=sth(hf).unsqueeze(1).to_broadcast([128, 2, 16, 64]),
                            in1=cm[:, :, hf * 16:(hf + 1) * 16, tt:tt + 1].to_broadcast([128, 2, 16, 64]), op=OP.mult),
                             reads=[STB[hf], cmB], writes=[ayB])
                        ps, psBs = sa_next[hf]
                        for k_ in range(2):
                            p.op("pe", lambda e, k_=k_, ps=ps: e.matmul(ps[:, k_ * 512:(k_ + 1) * 512], lhsT=blkb[:, :], rhs=ay[:, 0, k_ * 512:(k_ + 1) * 512],
                                                                       start=False, stop=True), reads=[cB, ayB], writes=[psBs[k_]])
                        pend_sa[hf] = (ps, psBs, ay, ayB)
                    for hf in range(2):
                        ps, psBs = pend_sa[hf][0], pend_sa[hf][1]
                        tB, tBB = tBr.get()
                        p.op("dve", lambda e, hf=hf, ps=ps, tB=tB: e.tensor_tensor(out=v3(tB[:, :]), in0=v3(ps), in1=col("BB", hf), op=OP.mult),
                             reads=psBs + [bt["BB"][1]], writes=[tBB])
                        p.op("dve", lambda e, hf=hf, tB=tB: e.tensor_tensor(out=sth(hf), in0=sth(hf), in1=v3(tB[:, :]), op=OP.add),
                             reads=[STB[hf], tBB], writes=[STB[hf]])
                        if tt == TB - 1:
                            p.op("dve", lambda e, hf=hf: e.tensor_tensor(out=sth(hf), in0=sth(hf), in1=v3(pl[:, hf, :]), op=OP.add),
                                 reads=[STB[hf], plB], writes=[STB[hf]])
                            p.op("dve", lambda e, hf=hf: e.tensor_tensor(out=sth(hf), in0=sth(hf), in1=col("GAM", hf), op=OP.mult),
                                 reads=[STB[hf], bt["GAM"][1]], writes=[STB[hf]])
                    for _ in range(g.cfg.get("pro_rate", 2)):
                        next(gen, None)
                    if t > 0:
                        for hf in range(2):
                            emit_y(pend_sa[hf][2], pend_sa[hf][3], hf, t - 1)
                        y_flush(t - 1)
                    if tt + 1 < TB:
                        onehots(tt + 1)
                for _ in gen:
                    pass
            for hf in range(2):
                ay, ayB = ayr.get()
                rt, rtB = blk_r["GAM"].get()
                p.dma("sp", rt[:, hf * 16:(hf + 1) * 16, 0:1], RRv[:, :, toks[hf] + T - 1:toks[hf] + T], writes=[rtB], slow=True)
                p.op("dve", lambda e, hf=hf: e.tensor_tensor(out=v3(ay[:, 1, :]), in0=sth(hf),
                                                              in1=rt[:, hf * 16:(hf + 1) * 16, 0:1].to_broadcast([128, 16, 64]), op=OP.mult),
                     reads=[STB[hf], rtB], writes=[ayB])
                emit_y(ay, ayB, hf, T - 1)
            y_flush(T - 1)
            for hf in range(2):
                oi = sis[hf]
                for j in range(16):
                    ps, psB = p.psum()
                    p.op("pe", lambda e, hf=hf, j=j: e.transpose(ps[:64, 0:128], ST[:, hf * 16 + j, :], K.ident[:, :]),
                         reads=[STB[hf], K.buf], writes=[psB])
                    sio, sioB = sior.get()
                    p.op("act", lambda e: e.copy(out=sio[:, :, :].rearrange("v h k -> v (h k)"), in_=ps[:64, 0:128]), reads=[psB], writes=[sioB])
                    p.dma("sp", O.S[oi, 2 * j:2 * j + 2, :, :].rearrange("h v k -> v h k"), sio[:, :, :], reads=[sioB])


def stage_rwkv_post(g):
    nc, p, I, O, Sx, K = g.nc, g.p, g.I, g.O, g.Sx, g.K
    pcol, PCB = g.pcol, g.PCB
    with ExitStack() as st:
        ytr = Ring(nc, st, "pyt", 2, [128, D], F32)
        sqr = Ring(nc, st, "psq", 2, [128, D], F32)
        str_ = Ring(nc, st, "pst", 2, [128, 96], F32)
        fmr = Ring(nc, st, "pfm", 2, [128, NCH, 128], F32)
        bor = Ring(nc, st, "pbo", 2, [128, NCH, 128], F32)
        ggr = Ring(nc, st, "pgg", 2, [128, NCH, 128], BF16)
        outr = Ring(nc, st, "pout", 2, [128, NCH, 128], BF16)
        h3 = lambda a: a.rearrange("t (h v) -> t h v", v=64)
        for (r0, m) in tiles_of(g.NT, 128):
            yt, ytB = ytr.get()
            sq, sqB = sqr.get()
            sx, sxB = str_.get()
            bo, boB = bor.get()
            gg, ggB = ggr.get()
            p.dma("sp", yt[:m, :], Sx.YT[r0:r0 + m, :], writes=[ytB])
            p.dma("sp", sq[:m, :], Sx.YP[r0:r0 + m, :], writes=[sqB])
            p.op("dve", lambda e: e.tensor_tensor(out=yt[:m, :], in0=yt[:m, :], in1=sq[:m, :], op=OP.add), reads=[ytB, sqB], writes=[ytB])
            p.dma("sp", bo[:, :, :m], fm(Sx.BONUS)[:, :, r0:r0 + m], writes=[boB])
            p.dma("sp", gg[:, :, :m], fm(Sx.GG)[:, :, r0:r0 + m], writes=[ggB])
            p.op("dve", lambda e: e.tensor_reduce(out=sx[:m, 0:32], in_=h3(yt[:m, :]), axis=AX.X, op=OP.add), reads=[ytB], writes=[sxB])
            p.op("dve", lambda e: e.tensor_scalar(out=sx[:m, 0:32], in0=sx[:m, 0:32], scalar1=1.0 / 64, scalar2=None, op0=OP.mult),
                 reads=[sxB], writes=[sxB])
            p.op("dve", lambda e: e.tensor_tensor(out=h3(yt[:m, :]), in0=h3(yt[:m, :]), in1=sx[:m, 0:32].unsqueeze(2).to_broadcast([m, 32, 64]),
                                                  op=OP.subtract), reads=[ytB, sxB], writes=[ytB])
            p.op("act", lambda e: e.activation(out=sq[:m, :], in_=yt[:m, :], func=AF.Square), reads=[ytB], writes=[sqB])
            p.op("dve", lambda e: e.tensor_reduce(out=sx[:m, 32:64], in_=h3(sq[:m, :]), axis=AX.X, op=OP.add), reads=[sqB], writes=[sxB])
            p.op("act", lambda e: e.activation(out=sx[:m, 64:96], in_=sx[:m, 32:64], func=AF.Sqrt, scale=1.0 / 64, bias=K.eps_gn[:m, :]),
                 reads=[sxB, K.buf], writes=[sxB])
            p.op("dve", lambda e: e.reciprocal(out=sx[:m, 64:96], in_=sx[:m, 64:96]), reads=[sxB], writes=[sxB])
            p.op("dve", lambda e: e.tensor_tensor(out=h3(yt[:m, :]), in0=h3(yt[:m, :]), in1=sx[:m, 64:96].unsqueeze(2).to_broadcast([m, 32, 64]),
                                                  op=OP.mult), reads=[ytB, sxB], writes=[ytB])
            fmt, fmB = fmr.get()
            for c4 in range(4):
                ps, psB = p.psum()
                for j in range(4):
                    c = c4 * 4 + j
                    p.op("pe", lambda e, c=c, j=j: e.transpose(ps[:, j * 128:j * 128 + m], yt[:m, c * 128:(c + 1) * 128], K.ident[:m, :m]),
                         reads=[ytB, K.buf], writes=[psB])
                for j in range(4):
                    c = c4 * 4 + j
                    p.op("act", lambda e, c=c, j=j: e.activation(out=fmt[:, c, :m], in_=ps[:, j * 128:j * 128 + m], func=AF.Identity,
                                                                  scale=pcol("gn_g", c), bias=pcol("gn_b", c)),
                         reads=[psB, PCB], writes=[fmB])
            p.op("dve", lambda e: e.tensor_tensor(out=fmt[:, :, :m], in0=fmt[:, :, :m], in1=bo[:, :, :m], op=OP.add), reads=[fmB, boB], writes=[fmB])
            ot, otB = outr.get()
            p.op("dve", lambda e: e.tensor_tensor(out=ot[:, :, :m], in0=fmt[:, :, :m], in1=gg[:, :, :m], op=OP.mult), reads=[fmB, ggB], writes=[otB])
            p.dma("sp", fm(Sx.YG2)[:, :, r0:r0 + m], ot[:, :, :m], reads=[otB])

def host_consts():
    ident = np.eye(128, dtype=np.float32)
    tri = np.triu(np.ones((128, 128), np.float32))
    ones = np.ones((128, 128), np.float32)
    blk = np.zeros((128, 128), np.float32)
    blk[:64, :64] = 1
    blk[64:, 64:] = 1
    return {"k_ident": ident, "k_tri": tri, "k_ones": ones, "k_blk": blk}


W2D = {"a_b_ig": (1, HM), "a_b_fg": (1, HM), "a_mlstm_norm": (1, D), "a_conv_w": (4, D), "a_conv_b": (1, D),
       "a_lru_wa": (16, 128, 128), "a_lru_ba": (1, D), "a_lru_wx": (16, 128, 128), "a_lru_bx": (1, D),
       "a_lru_lambda": (1, D), "a_w_in": (D, INW), "a_w_out": (2 * D, D), "c_mu": (6, D), "c_w_r": (D, D),
       "c_w_k": (D, D), "c_w_v": (D, D), "c_w0": (1, D), "c_w1": (D, 96), "c_w2": (96, D), "c_a0": (1, D),
       "c_a1": (D, 96), "c_a2": (96, D), "c_g1": (D, 256), "c_g2": (256, D), "c_k_k": (1, D), "c_k_a": (1, D),
       "c_r_k": (1, D), "c_gn_g": (1, D), "c_gn_b": (1, D), "c_w_o": (D, D)}
WKEEP = ("ln1_g", "ln1_b", "ln2_g", "ln2_b", "mlp_w1", "mlp_w2")


def make_in_maps(inputs, n_cores, Tp, Ts):
    f = lambda a: np.ascontiguousarray(np.asarray(a, dtype=np.float32))
    shared = dict(host_consts())
    for k, shp in W2D.items():
        shared[k] = f(inputs[k]).reshape(shp)
    for k in WKEEP:
        shared[k] = f(inputs[k])
    xp, xs = f(inputs["x_prompt"]), f(inputs["x_sample"])
    maps = []
    for c in range(n_cores):
        b = slice(2 * c, 2 * c + 2)
        m = dict(shared)
        m["x_all"] = np.concatenate([xp[b].reshape(2 * Tp, D), xs[b].reshape(2 * Ts, D)], 0)
        m["st_mC"] = f(inputs["state_mlstm_C"])[0, b]
        m["st_mn"] = f(inputs["state_mlstm_n"])[0, b]
        m["st_mm"] = f(inputs["state_mlstm_m"])[0, b]
        m["st_conv"] = f(inputs["state_lru_conv"])[0, b]
        m["st_lruh"] = f(inputs["state_lru_h"])[0, b]
        m["st_shift"] = f(inputs["state_rwkv_shift"])[0, b]
        m["st_S"] = f(inputs["state_rwkv_S"])[0, b]
        m = {k: np.ascontiguousarray(v) for k, v in m.items()}
        maps.append(m)
    return maps


_CACHE = {}


def run(inputs, n_cores=8, cfg=None):
    Tp = inputs["x_prompt"].shape[1]
    Ts = inputs["x_sample"].shape[1]
    cfg = dict(cfg or {})
    cfg.update(Tp=Tp, Ts=Ts)
    nc = build(cfg)
    maps = make_in_maps(inputs, n_cores, Tp, Ts)
    res = run_bass_kernel_spmd(nc, maps, core_ids=list(range(n_cores)))
    return res.results


def kernel(**inputs):
    r = run(inputs, 8)
    Tp = inputs["x_prompt"].shape[1]
    Ts = inputs["x_sample"].shape[1]
    n = len(r)
    yp = np.stack([r[c]["o_y"][:2 * Tp].reshape(2, Tp, D) for c in range(n)]).reshape(2 * n, Tp, D)
    ys = np.stack([r[c]["o_y"][2 * Tp:].reshape(2, Ts, D) for c in range(n)]).reshape(2 * n, Ts, D)

    def gather(name, lo):
        a = np.concatenate([r[c][name][lo:lo + 2] for c in range(n)], 0)
        return np.ascontiguousarray(a[None].astype(np.float32))

    outs = [yp.astype(np.float32), ys.astype(np.float32)]
    for lo in (0, 2):
        for nm in ("o_mC", "o_mn", "o_mm", "o_conv", "o_lruh", "o_shift", "o_S"):
            outs.append(gather(nm, lo))
    return tuple(outs)
```
